# Optimizing a Trainium2 kernel written in Bass

```python
import math
import jax, jax.numpy as jnp
from jax import lax
import numpy as np

D_MODEL = 1024
BATCH = 8
SEQ = 2048
DEPTH = 1
DEC_BATCH = 128
DEC_SEQ = 8
PAST_LEN = 16384
PAGE_SIZE = 128

MIX_WIDTH = 2 * D_MODEL
SC_WIDTH = MIX_WIDTH // 2
SC_GROUPS = 16
SC_CONV_K = 3
SSM_WIDTH = MIX_WIDTH - SC_WIDTH
SSM_HEAD_DIM = 64
SSM_HEADS = SSM_WIDTH // SSM_HEAD_DIM
SSM_GROUPS = 4
SSM_STATE = 128
SSM_CONV_K = 4
SSM_CHUNK = 128
XBC_WIDTH = SSM_WIDTH + 2 * SSM_GROUPS * SSM_STATE
IN_PROJ_WIDTH = 3 * SC_WIDTH + SSM_WIDTH + XBC_WIDTH + SSM_HEADS
FFN_HIDDEN = -(-8 * D_MODEL // (3 * 256)) * 256
EPS = 1e-6

kernel_name = "hymba_shortconv_ssd_sandwich_adaln_step"


def rmsnorm(x, g):
    xf = x.astype(jnp.float32)
    y = xf * lax.rsqrt(jnp.mean(xf * xf, axis=-1, keepdims=True) + EPS)
    return y.astype(x.dtype) * g


def causal_dwconv(u, buf, w, b=None):
    K = w.shape[0]
    L = u.shape[1]
    ext = jnp.concatenate([buf.astype(u.dtype), u], axis=1)
    y = w[0] * ext[:, 0:L]
    for k in range(1, K):
        y = y + w[k] * ext[:, k:k + L]
    if b is not None:
        y = y + b
    return y, ext[:, L:]


def ssd_scan(xs, dt, A, Bm, Cm, d_skip, s0):
    b, L, H, P = xs.shape
    G, N = Bm.shape[2], Bm.shape[3]
    R = H // G
    Q = SSM_CHUNK if L % SSM_CHUNK == 0 else L
    nc = L // Q
    x = xs.astype(jnp.float32).reshape(b, nc, Q, G, R, P)
    dtc = dt.reshape(b, nc, Q, G, R)
    Bc = Bm.astype(jnp.float32).reshape(b, nc, Q, G, N)
    Cc = Cm.astype(jnp.float32).reshape(b, nc, Q, G, N)
    acum = jnp.cumsum(dtc * A.reshape(G, R), axis=2)
    xdt = x * dtc[..., None]
    seg = acum[:, :, :, None] - acum[:, :, None, :]
    causal = jnp.tril(jnp.ones((Q, Q), dtype=bool))[:, :, None, None]
    Lmat = jnp.exp(jnp.where(causal, seg, -jnp.inf))
    cb = jnp.einsum('bclgn,bcsgn->bclsg', Cc, Bc)
    y_diag = jnp.einsum('bclsg,bclsgr,bcsgrp->bclgrp', cb, Lmat, xdt)
    decay_to_end = jnp.exp(acum[:, :, -1:] - acum)
    chunk_states = jnp.einsum('bclgn,bclgr,bclgrp->bcgrpn', Bc, decay_to_end, xdt)
    chunk_decay = jnp.exp(acum[:, :, -1])

    def step(s, inp):
        dec, st = inp
        s_new = s * dec[..., None, None] + st
        return s_new, s

    s_init = s0.astype(jnp.float32).reshape(b, G, R, P, N)
    s_final, prev = lax.scan(step, s_init,
                             (jnp.moveaxis(chunk_decay, 1, 0), jnp.moveaxis(chunk_states, 1, 0)))
    prev = jnp.moveaxis(prev, 0, 1)
    y_off = jnp.einsum('bclgn,bcgrpn,bclgr->bclgrp', Cc, prev, jnp.exp(acum))
    y = y_diag + y_off + x * d_skip.astype(jnp.float32).reshape(G, R)[:, :, None]
    return y.reshape(b, L, H, P), s_final.reshape(b, H, P, N)


def hybrid_layer(x, c, sc_buf, ssm_buf, ssm_state,
                 w_ada, b_ada, g_mix_pre, g_mix_post, g_ffn_pre, g_ffn_post,
                 w_in, sc_conv_w, ssm_conv_w, ssm_conv_b, dt_bias, a_log, d_skip,
                 ssm_norm_g, w_out, w_gate, w_up, w_down):
    bsz, L, _ = x.shape
    mod = jax.nn.silu(c) @ w_ada + b_ada
    shift_m, scale_m, gate_m, shift_f, scale_f, gate_f = [m[:, None, :] for m in jnp.split(mod, 6, axis=-1)]

    h = rmsnorm(x, g_mix_pre) * (1 + scale_m) + shift_m
    proj = h @ w_in
    idx = [SC_WIDTH, 2 * SC_WIDTH, 3 * SC_WIDTH, 3 * SC_WIDTH + SSM_WIDTH,
           3 * SC_WIDTH + SSM_WIDTH + XBC_WIDTH]
    sc_b, sc_c, sc_x, z, xbc, dt_raw = jnp.split(proj, idx, axis=-1)

    u = sc_c * sc_x
    uc, new_sc_buf = causal_dwconv(u, sc_buf, sc_conv_w)
    y_sc = sc_b * uc

    xbc_c, new_ssm_buf = causal_dwconv(xbc, ssm_buf, ssm_conv_w, ssm_conv_b)
    xbc_c = jax.nn.silu(xbc_c)
    xs, Bm, Cm = jnp.split(xbc_c, [SSM_WIDTH, SSM_WIDTH + SSM_GROUPS * SSM_STATE], axis=-1)
    xs = xs.reshape(bsz, L, SSM_HEADS, SSM_HEAD_DIM)
    Bm = Bm.reshape(bsz, L, SSM_GROUPS, SSM_STATE)
    Cm = Cm.reshape(bsz, L, SSM_GROUPS, SSM_STATE)
    dt = jax.nn.softplus(dt_raw.astype(jnp.float32) + dt_bias.astype(jnp.float32))
    A = -jnp.exp(a_log.astype(jnp.float32))
    y_ssd, new_state = ssd_scan(xs, dt, A, Bm, Cm, d_skip, ssm_state)
    y_ssd = y_ssd.reshape(bsz, L, SSM_WIDTH).astype(x.dtype)
    yg = (y_ssd * jax.nn.silu(z)).reshape(bsz, L, SSM_GROUPS, SSM_WIDTH // SSM_GROUPS)
    y_ssd = rmsnorm(yg, 1.0).reshape(bsz, L, SSM_WIDTH) * ssm_norm_g

    mix = jnp.concatenate([y_sc, y_ssd], axis=-1) @ w_out
    x = x + gate_m * rmsnorm(mix, g_mix_post)

    h2 = rmsnorm(x, g_ffn_pre) * (1 + scale_f) + shift_f
    f = (jax.nn.silu(h2 @ w_gate) * (h2 @ w_up)) @ w_down
    x = x + gate_f * rmsnorm(f, g_ffn_post)
    return x, new_sc_buf, new_ssm_buf, new_state.astype(ssm_state.dtype)


def setup_inputs(seed: int = 0) -> dict:
    key = jax.random.key(seed)
    ks = jax.random.split(key, 24)

    def nrm(k, shape, scale):
        return jax.random.normal(k, shape, jnp.float32) * scale

    dt0 = jnp.exp(jax.random.uniform(ks[12], (DEPTH, SSM_HEADS), jnp.float32,
                                     minval=math.log(1e-3), maxval=math.log(1e-1)))
    dt_bias = dt0 + jnp.log(-jnp.expm1(-dt0))
    a_log = jnp.log(jax.random.uniform(ks[13], (DEPTH, SSM_HEADS), jnp.float32, minval=1.0, maxval=16.0))
    return {
        "x_prompt": nrm(ks[0], (BATCH, SEQ, D_MODEL), 1.0),
        "x_sample": nrm(ks[1], (DEC_BATCH, DEC_SEQ, D_MODEL), 1.0),
        "c_prompt": nrm(ks[2], (BATCH, D_MODEL), 1.0),
        "c_sample": nrm(ks[3], (DEC_BATCH, D_MODEL), 1.0),
        "state_sc_conv": nrm(ks[4], (DEPTH, DEC_BATCH, SC_CONV_K - 1, SC_WIDTH), 1.0),
        "state_ssm_conv": nrm(ks[5], (DEPTH, DEC_BATCH, SSM_CONV_K - 1, XBC_WIDTH), 1.0),
        "state_ssm": nrm(ks[6], (DEPTH, DEC_BATCH, SSM_HEADS, SSM_HEAD_DIM, SSM_STATE), 0.1),
        "w_ada": nrm(ks[7], (DEPTH, D_MODEL, 6 * D_MODEL), 0.5 * D_MODEL ** -0.5),
        "b_ada": nrm(ks[8], (DEPTH, 6 * D_MODEL), 0.01),
        "g_mix_pre": 1.0 + nrm(ks[9], (DEPTH, D_MODEL), 0.05),
        "g_mix_post": 1.0 + nrm(ks[10], (DEPTH, D_MODEL), 0.05),
        "g_ffn_pre": 1.0 + nrm(ks[11], (DEPTH, D_MODEL), 0.05),
        "g_ffn_post": 1.0 + nrm(ks[14], (DEPTH, D_MODEL), 0.05),
        "w_in": nrm(ks[15], (DEPTH, D_MODEL, IN_PROJ_WIDTH), D_MODEL ** -0.5),
        "sc_conv_w": nrm(ks[16], (DEPTH, SC_CONV_K, SC_WIDTH), SC_CONV_K ** -0.5),
        "ssm_conv_w": nrm(ks[17], (DEPTH, SSM_CONV_K, XBC_WIDTH), SSM_CONV_K ** -0.5),
        "ssm_conv_b": nrm(ks[18], (DEPTH, XBC_WIDTH), 0.01),
        "dt_bias": dt_bias,
        "a_log": a_log,
        "d_skip": 1.0 + nrm(ks[19], (DEPTH, SSM_HEADS), 0.1),
        "ssm_norm_g": 1.0 + nrm(ks[20], (DEPTH, SSM_WIDTH), 0.05),
        "w_out": nrm(ks[21], (DEPTH, MIX_WIDTH, D_MODEL), MIX_WIDTH ** -0.5),
        "w_gate": nrm(ks[22], (DEPTH, D_MODEL, FFN_HIDDEN), D_MODEL ** -0.5),
        "w_up": nrm(ks[23], (DEPTH, D_MODEL, FFN_HIDDEN), D_MODEL ** -0.5),
        "w_down": nrm(jax.random.fold_in(key, 99), (DEPTH, FFN_HIDDEN, D_MODEL), FFN_HIDDEN ** -0.5),
    }


def reference(x_prompt, x_sample, c_prompt, c_sample, state_sc_conv, state_ssm_conv, state_ssm,
              w_ada, b_ada, g_mix_pre, g_mix_post, g_ffn_pre, g_ffn_post, w_in, sc_conv_w,
              ssm_conv_w, ssm_conv_b, dt_bias, a_log, d_skip, ssm_norm_g, w_out, w_gate, w_up, w_down):
    bp = x_prompt.shape[0]
    dt_ = x_prompt.dtype
    xp, xs = x_prompt, x_sample
    p_sc, p_sconv, p_ssm = [], [], []
    s_sc, s_sconv, s_ssm = [], [], []
    for l in range(DEPTH):
        params = (w_ada[l], b_ada[l], g_mix_pre[l], g_mix_post[l], g_ffn_pre[l], g_ffn_post[l],
                  w_in[l], sc_conv_w[l], ssm_conv_w[l], ssm_conv_b[l], dt_bias[l], a_log[l],
                  d_skip[l], ssm_norm_g[l], w_out[l], w_gate[l], w_up[l], w_down[l])
        xp, nb1, nb2, nst = hybrid_layer(
            xp, c_prompt,
            jnp.zeros((bp, SC_CONV_K - 1, SC_WIDTH), dt_),
            jnp.zeros((bp, SSM_CONV_K - 1, XBC_WIDTH), dt_),
            jnp.zeros((bp, SSM_HEADS, SSM_HEAD_DIM, SSM_STATE), state_ssm.dtype),
            *params)
        p_sc.append(nb1); p_sconv.append(nb2); p_ssm.append(nst)
        xs, mb1, mb2, mst = hybrid_layer(
            xs, c_sample, state_sc_conv[l], state_ssm_conv[l], state_ssm[l], *params)
        s_sc.append(mb1); s_sconv.append(mb2); s_ssm.append(mst)
    return (xp, xs,
            jnp.stack(p_sc), jnp.stack(p_sconv), jnp.stack(p_ssm),
            jnp.stack(s_sc), jnp.stack(s_sconv), jnp.stack(s_ssm))
```

```python
import contextlib
import numpy as np
import concourse.bass as bass
import concourse.mybir as mybir
from concourse.bass_utils import run_bass_kernel_spmd

F32 = mybir.dt.float32
BF16 = mybir.dt.bfloat16
AF = mybir.ActivationFunctionType
ALU = mybir.AluOpType

NCORES = 8
D = 1024
LP = 2048
NSEQ = 16
LS = 8
NTOK = LP + NSEQ * LS
NCH = LP // 128
HID = 2816
NHT = HID // 128
EPS = 1e-6
NEG = -30000.0
XLAT = 0.7

_CP = {}
_off = 0
for _n, _w in [("ident", 128), ("tri", 128), ("negm", 128), ("triBD", 128), ("negmBD", 128),
               ("same", 128), ("seqsel", 16), ("gpre", 8), ("gfpre", 8), ("scw", 24),
               ("xcw", 64), ("xcb", 16), ("dtb", 16), ("alog", 16), ("dcol", 8), ("ng", 8),
               ("bada", 48), ("cT", 136), ("eps", 1)]:
    _CP[_n] = _off
    _off += _w
CPW = _off


class Buf:
    __slots__ = ("name", "writers", "readers", "excl", "gen_deps")

    def __init__(self, name, excl=False):
        self.name = name
        self.writers = []
        self.readers = []
        self.gen_deps = set()
        self.excl = excl


class Chan:
    def __init__(self, name):
        self.name = name
        self.sem = None
        self.n = 0
        self.last = None


class Op:
    __slots__ = ("eng", "fn", "deps", "idx", "marked", "chan", "seq", "cnt", "waits", "sdeps", "cost", "dlat", "site", "adeps")


class _FakeEng:
    def __getattr__(self, name):
        def f(*a, **k):
            out = k.get("out", a[0] if a else None)
            return (name, out, k)
        return f


def _est_cost(eng, fn, is_dma):
    try:
        name, out, k = fn(_FakeEng())
        n = 1
        for d in out.shape[1:]:
            n *= int(d)
        parts = int(out.shape[0])
    except Exception:
        return 0.3, 0.0
    if is_dma:
        return 0.15, 2.0 + n * parts * 4 / 250e3
    if eng == "pe":
        return max(0.058, (n + 12) / 2400.0), 0.0
    if eng == "act":
        nap = sum(1 for a_ in (k.get("scale"), k.get("bias")) if a_ is not None and not isinstance(a_, (int, float)))
        return 0.17 + n / 1250.0 + (0.1 if k.get("accum_out") is not None else 0.0) + 0.09 * nap, 0.0
    if eng == "dve":
        return 0.19 + n / 960.0, 0.0
    if eng == "pool":
        return 0.2 + n / 560.0, 0.0
    return 0.2, 0.0


class Prog:
    ENGS = ("pe", "act", "dve", "pool", "sp")

    def __init__(self):
        self.ops = []
        self.chans = []

    def chan(self, name):
        c = Chan(name)
        self.chans.append(c)
        return c

    def _record(self, eng, fn, reads, writes, chan=None, extra=(), nowaw=False):
        op = Op()
        op.eng = eng
        op.fn = fn
        op.idx = len(self.ops)
        op.marked = False
        op.chan = chan
        op.seq = None
        op.cnt = None
        deps = set(extra)
        sdeps = set()
        for b in reads:
            deps.update(b.writers)
            if b.excl:
                for r in b.readers:
                    if self.ops[r].eng != eng:
                        deps.add(r)
        for b in writes:
            deps.update(b.readers)
            if nowaw and not b.readers:
                deps.update(b.gen_deps)
            for w in b.writers:
                wo = self.ops[w]
                if wo.eng != eng or wo.chan is not None or chan is not None:
                    if not nowaw:
                        deps.add(w)
                else:
                    sdeps.add(w)
        if chan is not None and getattr(chan, "last", None) is not None:
            sdeps.add(chan.last)
        if eng == "pe" and fn is not None:
            pp = {d for d in deps if (self.ops[d].eng == "pe" and self.ops[d].chan is None)}
            sdeps |= pp
            deps = deps - pp
        import sys as _sys
        op.site = _sys._getframe(2).f_lineno
        op.deps = deps
        op.adeps = set(deps)
        op.sdeps = sdeps | deps
        op.cost, op.dlat = (0.0, 0.0) if fn is None else _est_cost(eng, fn, chan is not None)
        if chan is not None:
            chan.n += 1
            op.seq = chan.n
            chan.last = op.idx
        self.ops.append(op)
        for b in reads:
            b.readers.append(op.idx)
        for b in writes:
            if b.readers:
                b.gen_deps = set(b.readers) | set(b.writers)
                b.writers = [op.idx]
                b.readers = []
            else:
                b.writers.append(op.idx)
        return op

    def op(self, eng, fn, reads=(), writes=(), nowaw=False):
        return self._record(eng, fn, reads, writes, nowaw=nowaw)

    def dma(self, eng, out, in_, chan, reads=(), writes=()):
        def fn(e, out=out, in_=in_):
            return e.dma_start(out=out, in_=in_)
        return self._record(eng, fn, reads, writes, chan=chan)

    def barrier(self):
        last = {}
        for op in self.ops:
            if op.fn is None:
                continue
            if op.chan is None:
                last[("e", op.eng)] = op.idx
            else:
                last[("c", id(op.chan))] = op.idx
        deps = set(last.values())
        for eng in self.ENGS:
            self._record(eng, None, (), (), extra=deps)

    def schedule(self):
        import heapq
        ops = self.ops
        order = []
        seg = []

        def flush():
            if not seg:
                return
            ids = set(o.idx for o in seg)
            preds = {o.idx: [d for d in o.sdeps if d in ids] for o in seg}
            succs = {o.idx: [] for o in seg}
            for o in seg:
                for d in preds[o.idx]:
                    succs[d].append(o.idx)

            def lat(d, o):
                do = ops[d]
                if do.chan is not None:
                    return do.dlat + 0.2
                return XLAT if do.eng != o.eng else 0.02

            prio = {}
            for o in reversed(seg):
                p = 0.0
                for sidx in succs[o.idx]:
                    q = lat(o.idx, ops[sidx]) + prio[sidx]
                    if q > p:
                        p = q
                prio[o.idx] = p + o.cost
            nun = {o.idx: len(preds[o.idx]) for o in seg}
            avail = {o.idx: 0.0 for o in seg}
            pending = {e: [] for e in self.ENGS}
            ready = {e: [] for e in self.ENGS}
            free = {e: 0.0 for e in self.ENGS}
            for o in seg:
                if nun[o.idx] == 0:
                    heapq.heappush(pending[o.eng], (0.0, -prio[o.idx], o.idx))
            start = {}
            left = len(seg)
            while left:
                best = None
                for e in self.ENGS:
                    while pending[e] and pending[e][0][0] <= free[e]:
                        a, np_, i = heapq.heappop(pending[e])
                        heapq.heappush(ready[e], (np_, i))
                    if ready[e]:
                        t = free[e]
                    elif pending[e]:
                        t = pending[e][0][0]
                    else:
                        continue
                    if best is None or t < best[0]:
                        best = (t, e)
                t, e = best
                while pending[e] and pending[e][0][0] <= t:
                    a, np_, i = heapq.heappop(pending[e])
                    heapq.heappush(ready[e], (np_, i))
                np_, i = heapq.heappop(ready[e])
                o = ops[i]
                start[i] = t
                fin = t + o.cost
                free[e] = fin
                left -= 1
                for sidx in succs[i]:
                    a = fin + lat(i, ops[sidx])
                    if a > avail[sidx]:
                        avail[sidx] = a
                    nun[sidx] -= 1
                    if nun[sidx] == 0:
                        heapq.heappush(pending[ops[sidx].eng], (avail[sidx], -prio[sidx], sidx))
            order.extend(sorted(seg, key=lambda o: (start[o.idx], o.idx)))
            self.sim_time = getattr(self, "sim_time", 0.0) + max(free.values())
            del seg[:]

        for op in ops:
            if op.fn is None:
                flush()
                order.append(op)
            else:
                seg.append(op)
        flush()
        self.order = order

    def emit(self, nc, final_chans=()):
        import os
        if os.environ.get("KSCHED", "1") == "1":
            self.schedule()
            ops_order = self.order
        else:
            ops_order = list(self.ops)
        allops = self.ops
        rank = {}
        for r, op in enumerate(ops_order):
            rank[op.idx] = r
        last = {}
        for op in ops_order:
            if op.fn is None:
                op.deps = set(last.values())
            elif op.chan is not None:
                last[("c", id(op.chan))] = op.idx
            else:
                last[("e", op.eng)] = op.idx
        ops = allops
        for op in ops_order:
            red = {}
            for d in op.deps:
                dop = ops[d]
                if dop.fn is None:
                    continue
                key = ("c", id(dop.chan)) if dop.chan is not None else ("e", dop.eng)
                if key not in red or rank[red[key]] < rank[d]:
                    red[key] = d
            op.deps = set(red.values())
            for d in op.deps:
                ops[d].marked = True
        with contextlib.ExitStack() as st:
            esem = {e: st.enter_context(nc.semaphore("s_" + e)) for e in self.ENGS}
            for c in self.chans:
                c.sem = st.enter_context(nc.semaphore("c_" + c.name))
            cnt = {e: 0 for e in self.ENGS}
            for op in ops_order:
                if op.chan is None and op.marked and op.fn is not None:
                    cnt[op.eng] += 1
                    op.cnt = cnt[op.eng]
            waited = {e: {} for e in self.ENGS}
            per_eng = {e: [] for e in self.ENGS}
            for op in ops_order:
                need = {}
                for d in op.deps:
                    dop = ops[d]
                    if dop.fn is None:
                        continue
                    if dop.chan is not None:
                        key, val = ("c", dop.chan), 16 * dop.seq
                    else:
                        key, val = ("e", dop.eng), dop.cnt
                    if need.get(key, 0) < val:
                        need[key] = val
                w = []
                for key, val in need.items():
                    if waited[op.eng].get(key, 0) >= val:
                        continue
                    waited[op.eng][key] = val
                    sem = key[1].sem if key[0] == "c" else esem[key[1]]
                    w.append((sem, val))
                op.waits = w
                per_eng[op.eng].append(op)
            self.stats = {e: len(per_eng[e]) for e in self.ENGS}
            self.stats["cnt"] = dict(cnt)
            with nc.Block() as block:
                def run(e, lst, is_sp=False):
                    for op in lst:
                        for sem, val in op.waits:
                            e.wait_ge(sem, val)
                        if op.fn is None:
                            continue
                        ins = op.fn(e)
                        if op.chan is not None:
                            ins.then_inc(op.chan.sem, 16)
                        elif op.marked:
                            ins.then_inc(esem[op.eng], 1)
                    if is_sp:
                        for c in final_chans:
                            if c.n:
                                e.wait_ge(c.sem, 16 * c.n)

                @block.tensor
                def _(e):
                    run(e, per_eng["pe"])

                @block.scalar
                def _(e):
                    run(e, per_eng["act"])

                @block.vector
                def _(e):
                    run(e, per_eng["dve"])

                @block.gpsimd
                def _(e):
                    run(e, per_eng["pool"])

                @block.sync
                def _(e):
                    run(e, per_eng["sp"], is_sp=True)


class _Stop(Exception):
    pass


class Tl:
    def __init__(self, ap, buf, off, nbytes):
        self.ap = ap
        self.buf = buf
        self.off = off
        self.nbytes = nbytes

    def __getitem__(self, k):
        return self.ap[k]


class SBA:
    def __init__(self, big, total_bytes):
        self.big = big
        self.total = total_bytes
        self.off = 0
        self.peak = 0

    def _view(self, off, shape, dt):
        nfree = 1
        for s in shape[1:]:
            nfree *= s
        esz = 4 if dt == F32 else 2
        nbytes = nfree * esz
        assert off % 4 == 0 and nbytes % 4 == 0, (off, nbytes)
        ap = self.big[0:shape[0], off // 4:(off + nbytes) // 4]
        if dt != F32:
            ap = ap.bitcast(dt)
        if len(shape) == 3:
            ap = ap.rearrange("p (a b) -> p a b", a=shape[1])
        elif len(shape) == 4:
            ap = ap.rearrange("p (a b c) -> p a b c", a=shape[1], b=shape[2])
        return ap, nbytes

    def alloc(self, name, shape, dt=F32):
        off = (self.off + 31) // 32 * 32
        ap, nbytes = self._view(off, shape, dt)
        self.off = off + nbytes
        self.peak = max(self.peak, self.off)
        assert self.off <= self.total, ("SBUF overflow", name, self.off, self.total)
        return Tl(ap, Buf(name), off, nbytes)

    def alias(self, t, shape, dt=F32, span=None):
        ap, nbytes = self._view(t.off, shape, dt)
        assert nbytes <= (span if span is not None else t.nbytes)
        return Tl(ap, t.buf, t.off, t.nbytes)


def bc(ap, shape):
    return ap.broadcast_to(list(shape))


def build_program(stop=99):
    try:
        return _build_inner(stop)
    except _Stop as e:
        return e.args


def _build_inner(stop=99):
    nc = bass.Bass("TRN2", target_bir_lowering=False)

    def din(name, shape):
        return nc.dram_tensor(name, list(shape), F32, kind="ExternalInput").ap()

    def dout(name, shape):
        return nc.dram_tensor(name, list(shape), F32, kind="ExternalOutput").ap()

    xall_d = din("xall", [NTOK, D])
    cpk_d = din("cpk", [128, CPW])
    scst_d = din("scst", [128, 8 * 16 * 2])
    xcst_d = din("xcst", [128, 16 * 16 * 3])
    sst_d = din("sst", [NSEQ, 1024, 128])
    wada_d = din("wada", [6, 128, 8 * 1024])
    badar_d = din("badar", [1, 6144])
    gpostr_d = din("gpostr", [2, 1024])
    wsc_d = din("wsc", [8, 128, 8 * 384])
    wssm_d = din("wssm", [6, 128, 8 * 512])
    wdt_d = din("wdt", [128, 8 * 16])
    wout_d = din("wout", [2048, 1024])
    wg_d = din("wg", [4, 128, 8 * 704])
    wu_d = din("wu", [4, 128, 8 * 704])
    wd_d = din("wd", [HID, 1024])

    y_d = dout("y", [NTOK, D])
    nscp_d = dout("nscp", [128, 16])
    nxcp_d = dout("nxcp", [128, 48])
    nstp_d = dout("nstp", [1024, 128])
    nscs_d = dout("nscs", [128, 256])
    nxcs_d = dout("nxcs", [128, 768])
    nsts_d = dout("nsts", [NSEQ, 1024, 128])

    x1s_d = nc.dram_tensor("x1s", [NTOK, D], F32).ap()
    gsave_d = nc.dram_tensor("gsave", [2, 128, 1024], F32).ap()

    P = Prog()
    SB_BYTES = 207 * 1024
    with contextlib.ExitStack() as es:
        big = es.enter_context(nc.sbuf_tensor("sbig", [128, SB_BYTES // 4], F32))
        ps = es.enter_context(nc.psum_tensor("ps", [128, 4096], F32))
        A = SBA(big, SB_BYTES)

        bank = [ps[:, i * 512:(i + 1) * 512] for i in range(8)]
        bankbf = [b.bitcast(BF16) for b in bank]
        pb = [Buf("pb%d" % i, excl=True) for i in range(8)]

        def pair(i):
            return ps[:, i * 512:(i + 2) * 512]

        out_chans = []

        def cp(n):
            if stop == n:
                P.barrier()
                P.emit(nc, final_chans=out_chans)
                raise _Stop(nc, P)

        cpk = A.alloc("cpk", [128, CPW])
        cb = A.alloc("cb", [128, 768], BF16)
        ones_b = A.alloc("ones_b", [128, 128], BF16)
        zero_b = A.alloc("zero_b", [128, 128], BF16)
        ntri = A.alloc("ntri", [128, 2, 128], BF16)
        diagD = A.alloc("diagD", [128, 8, 128], BF16)
        Abc = A.alloc("Abc", [128, 16])
        A_m = A.alloc("A_m", [128, 8, 17])
        Bv_m = A.alloc("Bv_m", [128, 8, 17])
        A_f = A.alloc("A_f", [128, 8, 17])
        Bv_f = A.alloc("Bv_f", [128, 8, 17])
        gmP = A.alloc("gmP", [128, 1024])
        gmS = A.alloc("gmS", [128, 1024])
        stt = A.alloc("stt", [128, 8])
        mhalf = A.alloc("mhalf", [128, 4])
        cvw = A.alloc("cvw", [128, 80])
        mark_p2 = A.off
        y_scT = A.alloc("y_scT", [128, 8, NTOK], BF16)
        Wssm = A.alloc("Wssm", [128, 8, 3088], BF16)

        ident_b = cb[:, 0:128]
        tri_b = cb[:, 128:256]
        negm_b = cb[:, 256:384]
        triBD_b = cb[:, 384:512]
        negmBD_b = cb[:, 512:640]
        same_b = cb[:, 640:768]
        ident_f = cpk[:, _CP["ident"]:_CP["ident"] + 128]
        tri_f = cpk[:, _CP["tri"]:_CP["tri"] + 128]
        triBD_f = cpk[:, _CP["triBD"]:_CP["triBD"] + 128]
        epscol = cpk[:, _CP["eps"]:_CP["eps"] + 1]

        def cpc(name, i, n=1):
            return cpk[:, _CP[name] + i:_CP[name] + i + n]

        ch_c = P.chan("cpk")
        P.dma("sp", cpk[:], cpk_d[:, :], ch_c, writes=[cpk.buf])
        P.op("dve", lambda e: e.tensor_copy(out=cb[:], in_=cpk[:, 0:768]), reads=[cpk.buf], writes=[cb.buf])
        P.op("pool", lambda e: e.memset(ones_b[:], 1.0), writes=[ones_b.buf])
        P.op("pool", lambda e: e.memset(zero_b[:], 0.0), writes=[zero_b.buf])
        for i_, nm_ in enumerate(("tri", "triBD")):
            P.op("dve", lambda e, i_=i_, nm_=nm_: e.tensor_scalar(out=ntri[:, i_, :], in0=cpk[:, _CP[nm_]:_CP[nm_] + 128], scalar1=-1.0,
                                                               scalar2=None, op0=ALU.mult),
                 reads=[cpk.buf], writes=[ntri.buf])
        P.op("pool", lambda e: e.memset(mhalf[:], -0.5), writes=[mhalf.buf])
        P.op("dve", lambda e: e.tensor_scalar(out=cvw[:], in0=cpk[:, _CP["xcw"]:_CP["xcw"] + 80], scalar1=0.5, scalar2=None, op0=ALU.mult),
             reads=[cpk.buf], writes=[cvw.buf])
        P.op("act", lambda e: e.activation(out=Abc[:], in_=cpk[:, _CP["alog"]:_CP["alog"] + 16], func=AF.Exp),
             reads=[cpk.buf], writes=[Abc.buf])
        P.op("dve", lambda e: e.tensor_scalar(out=Abc[:], in0=Abc[:], scalar1=-1.0, scalar2=None, op0=ALU.mult),
             reads=[Abc.buf], writes=[Abc.buf])
        for j in range(8):
            P.op("dve", lambda e, j=j: e.tensor_scalar(out=diagD[:, j, :], in0=ident_b, scalar1=cpc("dcol", j),
                                                       scalar2=None, op0=ALU.mult),
                 reads=[cb.buf, cpk.buf], writes=[diagD.buf])

        ch_wssm = [P.chan("wssm%d" % q) for q in range(7)]

        mark0 = A.off
        wsl = [A.alloc("wsl%d" % i, [128, 8, 1024], BF16) for i in range(2)]
        ch_wsl = [P.chan("wsl%d" % i) for i in range(2)]
        siluT = A.alloc("siluT", [128, 8, 17], BF16)
        siluPx = A.alloc("siluPx", [128, 8, 128], BF16)
        siluSx = A.alloc("siluSx", [128, 8, 128], BF16)
        badaT = A.alloc("badaT", [128, 1024])
        gpostT = A.alloc("gpostT", [128, 1024])
        tmpT = A.alloc("tmpT", [128, 1024])
        gtmp = [A.alloc("gtmp%d" % i, [128, 1024]) for i in range(2)]
        mtmp = A.alloc("mtmp", [128, 8, 17])
        ch_bada = P.chan("bada")
        ch_gpost = P.chan("gpost")
        ch_gs = [P.chan("gsave%d" % i) for i in range(2)]

        P.op("act", lambda e: e.activation(out=siluT[:], in_=cpk[:, _CP["cT"]:_CP["cT"] + 136].rearrange("p (k s) -> p k s", k=8),
                                           func=AF.Silu), reads=[cpk.buf], writes=[siluT.buf])
        P.op("pool", lambda e: e.tensor_copy(out=siluPx[:], in_=bc(siluT[:, :, 0:1], [128, 8, 128])),
             reads=[siluT.buf], writes=[siluPx.buf])
        P.op("pool", lambda e: e.tensor_copy(out=siluSx[:].rearrange("p k (b l) -> p k b l", l=8),
                                             in_=bc(siluT[:, :, 1:17].unsqueeze(3), [128, 8, 16, 8])),
             reads=[siluT.buf], writes=[siluSx.buf])

        order_v = [1, 0, 2, 4, 3, 5]
        for vi, v in enumerate(order_v):
            s = vi % 2
            P.dma("pool", wsl[s][:], wada_d[v].rearrange("p (k c) -> p k c", k=8), ch_wsl[s], writes=[wsl[s].buf])
            if v in (0, 1, 3, 4):
                bi = vi % 2
                for ct in range(8):
                    for k in range(8):
                        P.op("pe", lambda e, s=s, ct=ct, k=k, bi=bi: e.matmul(
                            bank[bi][:, ct * 17:(ct + 1) * 17], lhsT=wsl[s][:, k, ct * 128:(ct + 1) * 128],
                            rhs=siluT[:, k, :], start=(k == 0), stop=(k == 7)),
                            reads=[wsl[s].buf, siluT.buf], writes=[pb[bi]])
                pv = bank[bi][:, 0:136].rearrange("p (c s) -> p c s", c=8)
                bb = bc(cpk[:, _CP["bada"] + v * 8:_CP["bada"] + v * 8 + 8].unsqueeze(2), [128, 8, 17])
                if v in (0, 3):
                    dst = Bv_m if v == 0 else Bv_f
                    P.op("dve", lambda e, pv=pv, bb=bb, dst=dst: e.tensor_tensor(out=dst[:], in0=pv, in1=bb, op=ALU.add),
                         reads=[pb[bi], cpk.buf], writes=[dst.buf])
                else:
                    dst = A_m if v == 1 else A_f
                    gn = "gpre" if v == 1 else "gfpre"
                    gb = bc(cpk[:, _CP[gn]:_CP[gn] + 8].unsqueeze(2), [128, 8, 17])
                    P.op("dve", lambda e, pv=pv, bb=bb: e.tensor_tensor(out=mtmp[:], in0=pv, in1=bb, op=ALU.add),
                         reads=[pb[bi], cpk.buf], writes=[mtmp.buf])
                    P.op("dve", lambda e, dst=dst, gb=gb: e.scalar_tensor_tensor(out=dst[:], in0=mtmp[:], scalar=1.0, in1=gb,
                                                                               op0=ALU.add, op1=ALU.mult),
                         reads=[mtmp.buf, cpk.buf], writes=[dst.buf])
            else:
                gi = 0 if v == 2 else 1
                P.dma("sp", badaT[:], bc(badar_d[0:1, v * 1024:(v + 1) * 1024], [128, 1024]), ch_bada, writes=[badaT.buf])
                P.dma("sp", gpostT[:], bc(gpostr_d[gi:gi + 1, :], [128, 1024]), ch_gpost, writes=[gpostT.buf])
                for wi, sx in enumerate((siluPx, siluSx)):
                    if v == 2:
                        dst = gmP if wi == 0 else gmS
                    else:
                        dst = gtmp[wi]
                    for half in range(2):
                        bi = 2 + (wi * 2 + half) % 4
                        for k in range(8):
                            P.op("pe", lambda e, s=s, k=k, bi=bi, half=half, sx=sx: e.matmul(
                                bank[bi], lhsT=sx[:, k, :], rhs=wsl[s][:, k, half * 512:(half + 1) * 512],
                                start=(k == 0), stop=(k == 7)),
                                reads=[wsl[s].buf, sx.buf], writes=[pb[bi]])
                        hs = slice(half * 512, (half + 1) * 512)
                        P.op("dve", lambda e, bi=bi, hs=hs: e.tensor_tensor(out=tmpT[:, hs], in0=bank[bi], in1=badaT[:, hs], op=ALU.add),
                             reads=[pb[bi], badaT.buf], writes=[tmpT.buf])
                        P.op("pool", lambda e, dst=dst, hs=hs: e.tensor_tensor(out=dst[:, hs], in0=tmpT[:, hs], in1=gpostT[:, hs], op=ALU.mult),
                             reads=[tmpT.buf, gpostT.buf], writes=[dst.buf])
                    if v == 5:
                        P.dma("sp", gsave_d[wi], dst[:], ch_gs[wi], reads=[dst.buf])

        P.barrier()
        A.off = mark0
        if stop == 0:
            P.emit(nc, final_chans=out_chans)
            return nc, P

        def rstd_pool(t, c0, n, N, eps=EPS):
            P.op("pool", lambda e: e.tensor_scalar(out=t[:, c0 + n:c0 + 2 * n], in0=t[:, c0:c0 + n], scalar1=1.0 / N, scalar2=eps,
                                                   op0=ALU.mult, op1=ALU.add), reads=[t.buf], writes=[t.buf])
            P.op("pool", lambda e: e.tensor_tensor(out=t[:, c0 + 2 * n:c0 + 3 * n], in0=t[:, c0 + n:c0 + 2 * n], in1=mhalf[:, 0:n], op=ALU.pow),
                 reads=[t.buf, mhalf.buf], writes=[t.buf])

        def stage_A_pre(xt, xn, st_):
            P.op("act", lambda e: e.activation(out=xn[:], in_=xt[:], func=AF.Square, accum_out=st_[:, 0:1]),
                 reads=[xt.buf], writes=[xn.buf, st_.buf])
            rstd_pool(st_, 0, 1, D)
            P.op("act", lambda e: e.activation(out=xn[:], in_=xt[:], func=AF.Identity, scale=st_[:, 2:3]),
                 reads=[xt.buf, st_.buf], writes=[xn.buf])

        def stage_A(xt, xn, Amod, Bmod, sample, hT_dst, hT_buf, tmp4k):
            stage_A_pre(xt, xn, stt)
            stage_A_post(xn, Amod, Bmod, sample, hT_dst, hT_buf, tmp4k)

        def stage_A_post(xn, Amod, Bmod, sample, hT_dst, hT_buf, tmp4k):
            cp(20)
            for k in range(8):
                P.op("pe", lambda e, k=k: e.transpose(bankbf[0][:, k * 128:(k + 1) * 128], xn[:, k * 128:(k + 1) * 128], ident_b),
                     reads=[xn.buf, cb.buf], writes=[pb[0]])
            cp(21)
            if not sample:
                import os
                for k in range(8):
                    kev = os.environ.get("KEVAC", "act")
                    if kev == "dve":
                        P.op("dve", lambda e, k=k: e.tensor_scalar(out=hT_dst[:, k, :], in0=bankbf[0][:, k * 128:(k + 1) * 128],
                                                                   scalar1=Amod[:, k, 0:1], scalar2=Bmod[:, k, 0:1],
                                                                   op0=ALU.mult, op1=ALU.add),
                             reads=[pb[0], Amod.buf, Bmod.buf], writes=[hT_buf], nowaw=True)
                    else:
                        P.op("act", lambda e, k=k: e.activation(out=hT_dst[:, k, :], in_=bankbf[0][:, k * 128:(k + 1) * 128],
                                                                func=AF.Identity, scale=Amod[:, k, 0:1], bias=Bmod[:, k, 0:1]),
                             reads=[pb[0], Amod.buf, Bmod.buf], writes=[hT_buf], nowaw=True)
            else:
                pv = bankbf[0].rearrange("p (k b l) -> p k b l", k=8, b=16)
                tv = tmp4k[:].rearrange("p (k b l) -> p k b l", k=8, b=16)
                P.op("dve", lambda e: e.tensor_tensor(out=tv, in0=pv, in1=bc(Amod[:, :, 1:17].unsqueeze(3), [128, 8, 16, 8]), op=ALU.mult),
                     reads=[pb[0], Amod.buf], writes=[tmp4k.buf])
                P.op("dve", lambda e: e.tensor_tensor(out=hT_dst.rearrange("p k (b l) -> p k b l", b=16), in0=tv,
                                                      in1=bc(Bmod[:, :, 1:17].unsqueeze(3), [128, 8, 16, 8]), op=ALU.add),
                     reads=[tmp4k.buf, Bmod.buf], writes=[hT_buf])

        mark1 = A.off
        Wsc = A.alloc("Wsc", [128, 8, 3072], BF16)
        ch_wsc = [P.chan("wsc%d" % j) for j in range(8)]
        wsc_bufs = [Buf("wsc%d" % j) for j in range(8)]
        for j in range(8):
            P.dma("pool", Wsc[:, :, j * 384:(j + 1) * 384], wsc_d[j].rearrange("p (k c) -> p k c", k=8), ch_wsc[j],
                  writes=[wsc_bufs[j]])
        wssm_bufs = [Buf("wssmq%d" % q) for q in range(7)]
        for q in range(6):
            P.dma("pool", Wssm[:, :, q * 512:(q + 1) * 512], wssm_d[q].rearrange("p (k c) -> p k c", k=8), ch_wssm[q],
                  writes=[wssm_bufs[q]])
        P.dma("pool", Wssm[:, :, 3072:3088], wdt_d[:, :].rearrange("p (k c) -> p k c", k=8), ch_wssm[6], writes=[wssm_bufs[6]])
        xin = [A.alloc("xin%d" % i, [128, 1024]) for i in range(2)]
        ch_xin = [P.chan("xin%d" % i) for i in range(2)]
        xn = A.alloc("xn", [128, 1024], BF16)
        hT = A.alloc("hT", [128, 8, 512], BF16)
        pxs = A.alloc("pxs", [128, 512])
        uext = [A.alloc("uext%d" % i, [128, 516]) for i in range(2)]
        ucv = [A.alloc("ucv%d" % i, [128, 512]) for i in range(2)]
        uhist = A.alloc("uhist", [128, 8, 2])
        scst = A.alloc("scst", [128, 8, 16, 2])
        nscs = A.alloc("nscs", [128, 8, 16, 2])
        tmp4k = A.alloc("tmp4k", [128, 1024])
        ch_scst = P.chan("scst")
        ch_nscp = P.chan("nscp")
        ch_nscs = P.chan("nscs")
        out_chans += [ch_nscp, ch_nscs]
        P.dma("sp", scst[:].rearrange("p j b r -> p (j b r)"), scst_d[:, :], ch_scst, writes=[scst.buf])
        P.op("pool", lambda e: e.memset(uhist[:], 0.0), writes=[uhist.buf])

        groups = [(g * 512, 512, False) for g in range(4)] + [(LP, 128, True)]
        tile_toks = [t0 + tt * 128 for (t0, ts, _) in groups for tt in range(ts // 128)]

        def load_x1a(n):
            if n < len(tile_toks):
                P.dma("sp", xin[n % 2][:], xall_d[tile_toks[n]:tile_toks[n] + 128, :], ch_xin[n % 2], writes=[xin[n % 2].buf])

        cp(10)
        hTs = [hT, A.alloc("hT1", [128, 8, 512], BF16)]
        xns = [xn] + [A.alloc("xn_%d" % i, [128, 1024], BF16) for i in range(1, 4)]
        stq = [A.alloc("stq%d" % i, [128, 4]) for i in range(4)]
        xc = [0]

        def stage_A1_pre(gi):
            t0_, ts_, smp_ = groups[gi]
            for tt in range(ts_ // 128):
                xs_ = xin[xc[0] % 2]
                load_x1a(xc[0] + 1)
                xc[0] += 1
                stage_A_pre(xs_, xns[tt], stq[tt])

        def stage_A1_post(gi):
            t0_, ts_, smp_ = groups[gi]
            for tt in range(ts_ // 128):
                stage_A_post(xns[tt], A_m, Bv_m, smp_, hTs[gi % 2][:, :, tt * 128:(tt + 1) * 128], hTs[gi % 2].buf, tmp4k)

        def stage_A1(gi):
            stage_A1_pre(gi)
            stage_A1_post(gi)

        load_x1a(0)
        stage_A1(0)
        for gi, (tok0, TS, sample) in enumerate(groups):
            hTc = hTs[gi % 2]
            for j in range(8):
                bs = (1, 2, 3) if j % 2 == 0 else (4, 5, 6)
                for part in range(3):
                    bi = bs[part]
                    for k in range(8):
                        P.op("pe", lambda e, j=j, part=part, bi=bi, k=k, TS=TS, hTc=hTc: e.matmul(
                            bank[bi][:, 0:TS], lhsT=Wsc[:, k, j * 384 + part * 128:j * 384 + (part + 1) * 128],
                            rhs=hTc[:, k, 0:TS], start=(k == 0), stop=(k == 7)),
                            reads=[wsc_bufs[j], hTc.buf], writes=[pb[bi]])
                if j == 2 and gi + 1 < len(groups):
                    stage_A1_pre(gi + 1)
                if j == 5 and gi + 1 < len(groups):
                    stage_A1_post(gi + 1)
                bcn, bxn, bbn = bs
                u = uext[j % 2]
                uc = ucv[j % 2]
                P.op("act", lambda e, bxn=bxn, TS=TS: e.activation(out=pxs[:, 0:TS], in_=bank[bxn][:, 0:TS], func=AF.Copy),
                     reads=[pb[bxn]], writes=[pxs.buf])
                if not sample:
                    P.op("pool", lambda e, u=u, j=j: e.tensor_copy(out=u[:, 0:2], in_=uhist[:, j, :]),
                         reads=[uhist.buf], writes=[u.buf])
                    P.op("dve", lambda e, u=u, bcn=bcn, TS=TS: e.tensor_tensor(out=u[:, 2:2 + TS], in0=bank[bcn][:, 0:TS], in1=pxs[:, 0:TS], op=ALU.mult),
                         reads=[pb[bcn], pxs.buf], writes=[u.buf], nowaw=True)
                    for r in range(3):
                        if r == 0:
                            P.op("dve", lambda e, u=u, uc=uc, j=j, TS=TS: e.tensor_scalar(
                                out=uc[:, 0:TS], in0=u[:, 0:TS], scalar1=cpc("scw", j * 3), scalar2=None, op0=ALU.mult),
                                reads=[u.buf, cpk.buf], writes=[uc.buf])
                        else:
                            P.op("dve", lambda e, u=u, uc=uc, j=j, r=r, TS=TS: e.scalar_tensor_tensor(
                                out=uc[:, 0:TS], in0=u[:, r:r + TS], scalar=cpc("scw", j * 3 + r), in1=uc[:, 0:TS],
                                op0=ALU.mult, op1=ALU.add),
                                reads=[u.buf, uc.buf, cpk.buf], writes=[uc.buf])
                    P.op("dve", lambda e, uc=uc, bbn=bbn, j=j, tok0=tok0, TS=TS: e.tensor_tensor(
                        out=y_scT[:, j, tok0:tok0 + TS], in0=bank[bbn][:, 0:TS], in1=uc[:, 0:TS], op=ALU.mult),
                        reads=[pb[bbn], uc.buf], writes=[y_scT.buf])
                    P.op("pool", lambda e, u=u, j=j, TS=TS: e.tensor_copy(out=uhist[:, j, :], in_=u[:, TS:TS + 2]),
                         reads=[u.buf], writes=[uhist.buf])
                else:
                    u3 = u[:, 0:160].rearrange("p (b c) -> p b c", b=16)
                    uc3 = uc[:, 0:128].rearrange("p (b l) -> p b l", b=16)
                    P.op("pool", lambda e, u3=u3, j=j: e.tensor_copy(out=u3[:, :, 0:2], in_=scst[:, j, :, :]),
                         reads=[scst.buf], writes=[u.buf])
                    P.op("dve", lambda e, u3=u3, bcn=bcn: e.tensor_tensor(
                        out=u3[:, :, 2:10], in0=bank[bcn][:, 0:128].rearrange("p (b l) -> p b l", b=16),
                        in1=pxs[:, 0:128].rearrange("p (b l) -> p b l", b=16), op=ALU.mult),
                        reads=[pb[bcn], pxs.buf], writes=[u.buf], nowaw=True)
                    for r in range(3):
                        if r == 0:
                            P.op("dve", lambda e, u3=u3, uc3=uc3, j=j: e.tensor_scalar(
                                out=uc3, in0=u3[:, :, 0:8], scalar1=cpc("scw", j * 3), scalar2=None, op0=ALU.mult),
                                reads=[u.buf, cpk.buf], writes=[uc.buf])
                        else:
                            P.op("dve", lambda e, u3=u3, uc3=uc3, j=j, r=r: e.scalar_tensor_tensor(
                                out=uc3, in0=u3[:, :, r:r + 8], scalar=cpc("scw", j * 3 + r), in1=uc3,
                                op0=ALU.mult, op1=ALU.add),
                                reads=[u.buf, uc.buf, cpk.buf], writes=[uc.buf])
                    P.op("dve", lambda e, uc=uc, bbn=bbn, j=j: e.tensor_tensor(
                        out=y_scT[:, j, LP:LP + 128], in0=bank[bbn][:, 0:128], in1=uc[:, 0:128], op=ALU.mult),
                        reads=[pb[bbn], uc.buf], writes=[y_scT.buf])
                    P.op("pool", lambda e, u3=u3, j=j: e.tensor_copy(out=nscs[:, j, :, :], in_=u3[:, :, 8:10]),
                         reads=[u.buf], writes=[nscs.buf])
            if tok0 == 3 * 512:
                P.dma("sp", nscp_d[:, :], uhist[:].rearrange("p j r -> p (j r)"), ch_nscp, reads=[uhist.buf])
        P.dma("sp", nscs_d[:, :], nscs[:].rearrange("p j b r -> p (j b r)"), ch_nscs, reads=[nscs.buf])

        P.barrier()
        A.off = mark1
        if stop == 1:
            P.emit(nc, final_chans=out_chans)
            return nc, P

        Wout = A.alloc("Wout", [128, 16, 1024], BF16)
        ch_wout = [P.chan("wout%d" % i) for i in range(4)]
        wout_bufs = [Buf("wout%d" % i) for i in range(4)]
        for i in range(4):
            P.dma("pool", Wout[:, i * 4:(i + 1) * 4, :], wout_d[i * 512:(i + 1) * 512, :].rearrange("(k p) n -> p k n", p=128),
                  ch_wout[i], writes=[wout_bufs[i]])
        NXS = 3
        xinb = [A.alloc("xinb%d" % i, [128, 1024]) for i in range(NXS)]
        xnb = A.alloc("xnb", [128, 1024], BF16)
        hTb = A.alloc("hTb", [128, 8, 128], BF16)
        ext = [A.alloc("ext%d" % i, [128, 4, 176]) for i in range(2)]
        cv = [A.alloc("cv%d" % i, [128, 4, 128]) for i in range(2)]
        xsT0 = A.alloc("xsT", [128, 8, 128], BF16)
        BT0 = A.alloc("BT", [128, 4, 128], BF16)
        CT0 = A.alloc("CT", [128, 4, 128], BF16)
        szT0 = A.alloc("szT", [128, 8, 128], BF16)
        sm16_0 = A.alloc("sm16", [128, 12, 16])
        dhl0 = A.alloc("dhl", [128, 2, 16], BF16)
        xdt = A.alloc("xdt", [128, 1024], BF16)
        xdtd = A.alloc("xdtd", [128, 1024], BF16)
        Btok = A.alloc("Btok", [128, 512], BF16)
        smk = A.alloc("smk", [128, 4, 128])
        Lsb = [A.alloc("Lsb%d" % i, [128, 4, 128]) for i in range(2)]
        Mt = A.alloc("Mt", [128, 16, 128], BF16)
        ST = A.alloc("ST", [128, 1024])
        STb = A.alloc("STb", [128, 1024], BF16)
        ycomb = A.alloc("ycomb", [128, 1024])
        ygn = A.alloc("ygn", [128, 1024], BF16)
        yssdT = A.alloc("yssdT", [128, 8, 128], BF16)
        xhist = A.alloc("xhist", [128, 16, 3])
        xcst = A.alloc("xcst", [128, 16, 16, 3])
        nxcs = A.alloc("nxcs", [128, 16, 16, 3])
        ss4 = A.alloc("ss4", [128, 12])
        sttb = A.alloc("sttb", [128, 8])
        ptmp = A.alloc("ptmp", [128, 4, 128])
        CTm = [A.alloc("CTm%d" % i, [128, 4, 128], BF16) for i in range(2)]
        Bm = A.alloc("Bm", [128, 512], BF16)
        decT = A.alloc("decT", [128, 8, 16])
        S0n = [A.alias(ext[0], [128, 8, 128], span=ext[0].nbytes + ext[1].nbytes),
               A.alias(cv[0], [128, 8, 128], span=cv[0].nbytes + cv[1].nbytes),
               A.alias(ST, [128, 8, 128])]
        assert ext[1].off == ext[0].off + ext[0].nbytes and cv[1].off == cv[0].off + cv[0].nbytes
        S0n_bufs = [[ext[0].buf, ext[1].buf], [cv[0].buf, cv[1].buf], [ST.buf]]
        dtAx = A.alias(Mt, [128, 2, 1024], BF16)
        S0b = A.alias(ygn, [128, 1024], BF16)
        S0T = A.alias(STb, [128, 1024], BF16)
        bsets = [dict(xsT=xsT0, BT=BT0, CT=CT0, szT=szT0, sm16=sm16_0, dhl=dhl0),
                 dict(xsT=A.alias(xcst, [128, 8, 128], BF16), szT=A.alias(nxcs, [128, 8, 128], BF16),
                      BT=A.alias(CTm[0], [128, 4, 128], BF16), CT=A.alias(CTm[1], [128, 4, 128], BF16),
                      sm16=A.alias(Bm, [128, 12, 16]), dhl=A.alias(decT, [128, 2, 16], BF16))]
        ch_xinb = [P.chan("xinb%d" % i) for i in range(NXS)]
        ch_xcst = P.chan("xcst")
        ch_s0 = [P.chan("s0n%d" % i) for i in range(3)]
        ch_nxcp = P.chan("nxcp")
        ch_nxcs = P.chan("nxcs")
        ch_nstp = P.chan("nstp")
        out_chans += [ch_nxcp, ch_nxcs, ch_nstp] + ch_s0
        P.dma("sp", xcst[:].rearrange("p j b r -> p (j b r)"), xcst_d[:, :], ch_xcst, writes=[xcst.buf])
        P.op("pool", lambda e: e.memset(xhist[:], 0.0), writes=[xhist.buf])
        for i in range(2):
            P.op("pool", lambda e, i=i: e.memset(CTm[i][:], 0.0), writes=[CTm[i].buf])

        pb1f = pb[1]
        pb1b = pb[1]
        b1tok = bankbf[5][:, 0:512]

        DTR, DT, DTA, TMP, NAC, EAC, DLT, DTE, DEC, EXPX = range(10)

        chunk_list = [(LP, True, -1)] + [(c * 128, False, c) for c in range(NCH)]

        def load_x1b(n):
            if n < len(chunk_list):
                t0 = chunk_list[n][0]
                P.dma("sp", xinb[n % NXS][:], xall_d[t0:t0 + 128, :], ch_xinb[n % NXS], writes=[xinb[n % NXS].buf])

        def conv_wide(eng, wt, c_, taps, shp, j0, ebuf):
            nd = len(shp)

            def wb(r):
                w = cvw[:, j0 * 4 + r:j0 * 4 + 16:4]
                w = w.unsqueeze(2) if nd == 3 else w.unsqueeze(2).unsqueeze(3)
                return bc(w, shp)

            bb = cvw[:, 64 + j0:64 + j0 + 4]
            bb = bc(bb.unsqueeze(2) if nd == 3 else bb.unsqueeze(2).unsqueeze(3), shp)
            cview = c_[:] if nd == 3 else c_[:].rearrange("p a (b l) -> p a b l", b=16)
            wview = wt[:].bitcast(F32).rearrange("p (a t) -> p a t", a=4) if wt.ap.dtype != F32 else wt[:]
            if nd == 4:
                wview = wview.rearrange("p a (b l) -> p a b l", b=16)
            for jj in range(4):
                P.op("act", lambda e, jj=jj: e.activation(out=cview[:, jj], in_=taps[0][:, jj], func=AF.Identity,
                                                          scale=cvw[:, (j0 + jj) * 4:(j0 + jj) * 4 + 1],
                                                          bias=cvw[:, 64 + j0 + jj:64 + j0 + jj + 1]),
                     reads=[ebuf, cvw.buf], writes=[c_.buf])
            for jj in range(4):
                for r in range(1, 4):
                    P.op(eng, lambda e, r=r, jj=jj: e.scalar_tensor_tensor(
                        out=cview[:, jj], in0=taps[r][:, jj], scalar=cvw[:, (j0 + jj) * 4 + r:(j0 + jj) * 4 + r + 1], in1=cview[:, jj],
                        op0=ALU.mult, op1=ALU.add),
                        reads=[ebuf, cvw.buf, c_.buf], writes=[c_.buf])

        def front(ci):
            tok0, sample, cidx = chunk_list[ci]
            S = bsets[0] if sample else bsets[cidx % 2]
            xsT, BT, CT, szT, sm16, dhl = S["xsT"], S["BT"], S["CT"], S["szT"], S["sm16"], S["dhl"]

            def s16(i):
                return sm16[:, i, :]

            xs_ = xinb[ci % NXS]
            stage_A(xs_, xnb, A_m, Bv_m, sample, hTb[:], hTb.buf, ycomb if sample else None)
            yield
            for k in range(8):
                P.op("pe", lambda e, k=k: e.matmul(bank[1][:, 0:16], lhsT=hTb[:, k, :], rhs=Wssm[:, k, 3072:3088],
                                                   start=(k == 0), stop=(k == 7)),
                     reads=[wssm_bufs[6], hTb.buf], writes=[pb1f])
            P.op("dve", lambda e: e.tensor_tensor(out=s16(DTR), in0=bank[1][:, 0:16], in1=cpk[:, _CP["dtb"]:_CP["dtb"] + 16], op=ALU.add),
                 reads=[pb1f, cpk.buf], writes=[sm16.buf])
            P.op("act", lambda e: e.activation(out=s16(EXPX), in_=s16(DTR), func=AF.Exp), reads=[sm16.buf], writes=[sm16.buf])
            P.op("act", lambda e: e.activation(out=s16(DT), in_=s16(EXPX), func=AF.Ln, bias=1.0), reads=[sm16.buf], writes=[sm16.buf])
            P.op("dve", lambda e: e.tensor_tensor(out=s16(DTA), in0=s16(DT), in1=Abc[:], op=ALU.mult),
                 reads=[sm16.buf, Abc.buf], writes=[sm16.buf])
            P.op("dve", lambda e: e.tensor_copy(out=dhl[:, 0, :], in_=s16(DTA)), reads=[sm16.buf], writes=[dhl.buf])
            P.op("dve", lambda e: e.tensor_tensor(out=s16(TMP), in0=s16(DTA), in1=dhl[:, 0, :], op=ALU.subtract),
                 reads=[sm16.buf, dhl.buf], writes=[sm16.buf])
            P.op("dve", lambda e: e.tensor_copy(out=dhl[:, 1, :], in_=s16(TMP)), reads=[sm16.buf], writes=[dhl.buf])
            trm = triBD_b if sample else tri_b
            onm = same_b if sample else ones_b[:]
            for hl in range(2):
                P.op("pe", lambda e, hl=hl, trm=trm: e.matmul(bank[1][:, 16:32], lhsT=trm, rhs=dhl[:, hl, :], start=(hl == 0), stop=(hl == 1)),
                     reads=[dhl.buf, cb.buf], writes=[pb1f])
            for hl in range(2):
                P.op("pe", lambda e, hl=hl, onm=onm: e.matmul(bank[1][:, 32:48], lhsT=onm, rhs=dhl[:, hl, :], start=(hl == 0), stop=(hl == 1)),
                     reads=[dhl.buf, cb.buf, ones_b.buf], writes=[pb1f])
            P.op("dve", lambda e: e.tensor_scalar(out=s16(NAC), in0=bank[1][:, 16:32], scalar1=-1.0, scalar2=None, op0=ALU.mult),
                 reads=[pb1f], writes=[sm16.buf])
            P.op("act", lambda e: e.activation(out=s16(EAC), in_=bank[1][:, 16:32], func=AF.Exp), reads=[pb1f], writes=[sm16.buf])
            P.op("dve", lambda e: e.tensor_tensor(out=s16(DLT), in0=bank[1][:, 32:48], in1=s16(NAC), op=ALU.add),
                 reads=[pb1f, sm16.buf], writes=[sm16.buf])
            P.op("act", lambda e: e.activation(out=s16(DTE), in_=s16(DLT), func=AF.Exp), reads=[sm16.buf], writes=[sm16.buf])
            if not sample:
                P.op("act", lambda e: e.activation(out=s16(DEC), in_=bank[1][:, 32:48], func=AF.Exp), reads=[pb1f], writes=[sm16.buf])
            yield

            for qi, q in enumerate((0, 1, 2, 3, 4, 5)):
                bi = 2 + qi % 2
                for jj in range(4):
                    for k in range(8):
                        P.op("pe", lambda e, q=q, jj=jj, k=k, bi=bi: e.matmul(
                            bank[bi][:, jj * 128:(jj + 1) * 128], lhsT=Wssm[:, k, q * 512 + jj * 128:q * 512 + (jj + 1) * 128],
                            rhs=hTb[:, k, :], start=(k == 0), stop=(k == 7)),
                            reads=[wssm_bufs[q], hTb.buf], writes=[pb[bi]])
                bv = bank[bi].rearrange("p (a t) -> p a t", a=4)
                if q < 2:
                    thv = ext[q % 2][:, :, 0:128]
                    P.op("act", lambda e, bv=bv, thv=thv: e.activation(out=thv, in_=bv, func=AF.Tanh, scale=0.5),
                         reads=[pb[bi]], writes=[ext[q % 2].buf])
                    P.op("dve", lambda e, q=q, bv=bv, thv=thv, szT=szT: e.scalar_tensor_tensor(
                        out=szT[:, q * 4:(q + 1) * 4, :], in0=thv, scalar=1.0, in1=bv, op0=ALU.add, op1=ALU.mult),
                        reads=[ext[q % 2].buf, pb[bi]], writes=[szT.buf])
                    yield
                    continue
                jq = q - 2
                j0 = jq * 4
                e_ = ext[jq % 2]
                c_ = cv[jq % 2]
                on_pool = False
                ceng = "pool" if on_pool else "dve"
                wt_ = ptmp if on_pool else xnb
                if not sample:
                    e3 = e_[:, :, 0:131]
                    P.op("pool", lambda e, e3=e3, j0=j0: e.tensor_copy(out=e3[:, :, 0:3], in_=xhist[:, j0:j0 + 4, :]),
                         reads=[xhist.buf], writes=[e_.buf])
                    P.op("act", lambda e, e3=e3, bv=bv: e.activation(out=e3[:, :, 3:131], in_=bv, func=AF.Copy),
                         reads=[pb[bi]], writes=[e_.buf], nowaw=True)
                    conv_wide(ceng, wt_, c_, [e3[:, :, r:r + 128] for r in range(4)], [128, 4, 128], j0, e_.buf)
                    P.op("pool", lambda e, e3=e3, j0=j0: e.tensor_copy(out=xhist[:, j0:j0 + 4, :], in_=e3[:, :, 128:131]),
                         reads=[e_.buf], writes=[xhist.buf])
                else:
                    e4 = e_[:].rearrange("p a (b c) -> p a b c", b=16)
                    P.op("pool", lambda e, e4=e4, j0=j0: e.tensor_copy(out=e4[:, :, :, 0:3], in_=xcst[:, j0:j0 + 4, :, :]),
                         reads=[xcst.buf], writes=[e_.buf])
                    P.op("act", lambda e, e4=e4, bi=bi: e.activation(
                        out=e4[:, :, :, 3:11], in_=bank[bi].rearrange("p (a b l) -> p a b l", a=4, b=16), func=AF.Copy),
                        reads=[pb[bi]], writes=[e_.buf], nowaw=True)
                    conv_wide(ceng, wt_, c_, [e4[:, :, :, r:r + 8] for r in range(4)], [128, 4, 16, 8], j0, e_.buf)
                    P.op("pool", lambda e, e4=e4, j0=j0: e.tensor_copy(out=nxcs[:, j0:j0 + 4, :, :], in_=e4[:, :, :, 8:11]),
                         reads=[e_.buf], writes=[nxcs.buf])
                if jq < 2:
                    dst, dbuf = xsT[:, j0:j0 + 4, :], xsT.buf
                elif jq == 2:
                    dst, dbuf = BT[:], BT.buf
                else:
                    dst, dbuf = CT[:], CT.buf
                thv = e_[:, :, 0:128]
                P.op("act", lambda e, c_=c_, thv=thv: e.activation(out=thv, in_=c_[:], func=AF.Tanh),
                     reads=[c_.buf], writes=[e_.buf])
                P.op("dve", lambda e, c_=c_, dst=dst, thv=thv: e.scalar_tensor_tensor(
                    out=dst, in0=thv, scalar=1.0, in1=c_[:], op0=ALU.add, op1=ALU.mult),
                    reads=[e_.buf, c_.buf], writes=[dbuf])
                yield

        def back(ci):
            tok0, sample, cidx = chunk_list[ci]
            first = (not sample) and cidx == 0
            S = bsets[0] if sample else bsets[cidx % 2]
            xsT, BT, CT, szT, sm16, dhl = S["xsT"], S["BT"], S["CT"], S["szT"], S["sm16"], S["dhl"]

            def s16(i):
                return sm16[:, i, :]

            xs_ = xinb[ci % NXS]
            trm = triBD_b if sample else tri_b
            for j in range(8):
                P.op("pe", lambda e, j=j: e.transpose(bankbf[4][:, j * 128:(j + 1) * 128], xsT[:, j, :], ident_b),
                     reads=[xsT.buf, cb.buf], writes=[pb[4]])
            if ci == 1:
                cp(70)
            for g in range(4):
                P.op("pe", lambda e, g=g: e.transpose(b1tok[:, g * 128:(g + 1) * 128], BT[:, g, :], ident_b),
                     reads=[BT.buf, cb.buf], writes=[pb[5]])
            if ci == 1:
                cp(71)
            P.op("dve", lambda e: e.tensor_tensor(out=xdt[:].rearrange("p (h q) -> p h q", h=16),
                                                  in0=bankbf[4].rearrange("p (h q) -> p h q", h=16),
                                                  in1=bc(s16(DT).unsqueeze(2), [128, 16, 64]), op=ALU.mult),
                 reads=[pb[4], sm16.buf], writes=[xdt.buf])
            P.op("pool", lambda e: e.tensor_tensor(out=xdtd[:].rearrange("p (h q) -> p h q", h=16),
                                                   in0=xdt[:].rearrange("p (h q) -> p h q", h=16),
                                                   in1=bc(s16(DTE).unsqueeze(2), [128, 16, 64]), op=ALU.mult),
                 reads=[xdt.buf, sm16.buf], writes=[xdtd.buf])
            P.op("act", lambda e: e.activation(out=Btok[:], in_=b1tok, func=AF.Copy), reads=[pb[5]], writes=[Btok.buf])
            yield
            for g in range(4):
                P.op("pe", lambda e, g=g: e.matmul(bank[5][:, g * 128:(g + 1) * 128], lhsT=BT[:, g, :], rhs=CT[:, g, :], start=True, stop=True),
                     reads=[BT.buf, CT.buf], writes=[pb[5]])
            P.op("act", lambda e: e.activation(out=smk[:], in_=bank[5].rearrange("p (g l) -> p g l", g=4), func=AF.Copy),
                 reads=[pb[5]], writes=[smk.buf])
            if (not sample) and (not first):
                for g in range(4):
                    bi = 6 + g // 2
                    P.op("pe", lambda e, g=g, bi=bi: e.matmul(bank[bi][:, (g % 2) * 256:(g % 2) * 256 + 256], lhsT=CT[:, g, :],
                                                              rhs=STb[:, g * 256:(g + 1) * 256], start=True, stop=True),
                         reads=[CT.buf, STb.buf], writes=[pb[bi]])
            yield
            ngm = negmBD_b if sample else negm_b
            ntm = ntri[:, 1, :] if sample else ntri[:, 0, :]
            for g in range(4):
                bi = 4 + g % 2
                for r in range(4):
                    h = 4 * g + r
                    osl = bank[bi][:, r * 128:(r + 1) * 128]
                    for hl in range(2):
                        P.op("pe", lambda e, osl=osl, hl=hl, h=h, trm=trm: e.matmul(
                            osl, lhsT=bc(dhl[:, hl, h:h + 1], [128, 128]), rhs=trm, start=(hl == 0), stop=False),
                            reads=[dhl.buf, cb.buf], writes=[pb[bi]])
                    for hl in range(2):
                        P.op("pe", lambda e, osl=osl, hl=hl, h=h, ntm=ntm: e.matmul(
                            osl, lhsT=ntm, rhs=bc(dhl[:, hl, h:h + 1], [128, 128]), start=False, stop=False),
                            reads=[dhl.buf, ntri.buf], writes=[pb[bi]])
                    P.op("pe", lambda e, osl=osl, ngm=ngm: e.matmul(osl, lhsT=ident_b, rhs=ngm, start=False, stop=True),
                         reads=[cb.buf], writes=[pb[bi]])
                L_ = Lsb[g % 2]
                P.op("act", lambda e, L_=L_, bi=bi: e.activation(out=L_[:], in_=bank[bi].rearrange("p (r l) -> p r l", r=4), func=AF.Exp),
                     reads=[pb[bi]], writes=[L_.buf])
                P.op("dve", lambda e, L_=L_, g=g: e.tensor_tensor(out=Mt[:, 4 * g:4 * g + 4, :], in0=L_[:],
                                                                   in1=bc(smk[:, g, :].unsqueeze(1), [128, 4, 128]), op=ALU.mult),
                     reads=[L_.buf, smk.buf], writes=[Mt.buf])
                yield
            for j in range(8):
                bi = 4 + j // 4
                col = (j % 4) * 128
                P.op("pe", lambda e, j=j, bi=bi, col=col: e.matmul(bank[bi][:, col:col + 128], lhsT=xsT[:, j, :], rhs=diagD[:, j, :],
                                                                   start=True, stop=False),
                     reads=[xsT.buf, diagD.buf], writes=[pb[bi]])
                for hh in range(2):
                    h = 2 * j + hh
                    P.op("pe", lambda e, h=h, bi=bi, col=col, hh=hh: e.matmul(
                        bank[bi][:, col + hh * 64:col + (hh + 1) * 64], lhsT=Mt[:, h, :], rhs=xdt[:, h * 64:(h + 1) * 64],
                        start=False, stop=(hh == 1)),
                        reads=[Mt.buf, xdt.buf], writes=[pb[bi]])
            yd = pair(4)
            yo = pair(6)
            if first:
                P.op("dve", lambda e: e.tensor_copy(out=ycomb[:], in_=yd), reads=[pb[4], pb[5]], writes=[ycomb.buf])
            elif not sample:
                P.op("dve", lambda e: e.tensor_tensor(out=ycomb[:].rearrange("p (h q) -> p h q", h=16),
                                                      in0=yo.rearrange("p (h q) -> p h q", h=16),
                                                      in1=bc(s16(EAC).unsqueeze(2), [128, 16, 64]), op=ALU.mult),
                     reads=[pb[6], pb[7], sm16.buf], writes=[ycomb.buf])
                P.op("dve", lambda e: e.tensor_tensor(out=ycomb[:], in0=ycomb[:], in1=yd, op=ALU.add),
                     reads=[ycomb.buf, pb[4], pb[5]], writes=[ycomb.buf])
            else:
                for hl in range(2):
                    P.op("dve", lambda e, hl=hl: e.tensor_copy(out=dtAx[:, hl, :].rearrange("p (h q) -> p h q", h=16),
                                                               in_=bc(dhl[:, hl, :].unsqueeze(2), [128, 16, 64])),
                         reads=[dhl.buf, Mt.buf], writes=[dtAx.buf])
                selb = A.alias(Bm, [128, 16], BF16)
                P.op("dve", lambda e: e.tensor_copy(out=selb[:], in_=cpk[:, _CP["seqsel"]:_CP["seqsel"] + 16]),
                     reads=[cpk.buf], writes=[selb.buf])
                for jp in range(8):
                    for hl in range(2):
                        P.op("pe", lambda e, jp=jp, hl=hl: e.matmul(bank[1][:, 64 + jp * 16:64 + (jp + 1) * 16],
                                                                    lhsT=dtAx[:, hl, jp * 128:(jp + 1) * 128], rhs=selb[:],
                                                                    start=(hl == 0), stop=(hl == 1)),
                             reads=[dtAx.buf, selb.buf], writes=[pb1b])
                P.op("act", lambda e: e.activation(out=decT[:], in_=bank[1][:, 64:192].rearrange("p (j b) -> p j b", j=8), func=AF.Exp),
                     reads=[pb1b], writes=[decT.buf])
                for bi in (6, 7):
                    P.op("pe", lambda e, bi=bi: e.matmul(bank[bi], lhsT=zero_b[:], rhs=cb[:, 0:512], start=True, stop=False),
                         reads=[zero_b.buf, cb.buf], writes=[pb[bi]])

                def load_s0(b):
                    if b < NSEQ:
                        P.dma("sp", S0n[b % 3][:], sst_d[b].rearrange("(j p) n -> p j n", p=128), ch_s0[b % 3], writes=S0n_bufs[b % 3])

                load_s0(0)
                load_s0(1)
                for b in range(NSEQ):
                    sn_ = S0n[b % 3]
                    snb = S0n_bufs[b % 3]
                    cm = CTm[b % 2]
                    if b >= 1:
                        P.dma("sp", nsts_d[b - 1].rearrange("(j p) n -> p j n", p=128), S0n[(b - 1) % 3][:], ch_s0[(b - 1) % 3],
                              reads=S0n_bufs[(b - 1) % 3])
                    load_s0(b + 2)
                    if b >= 2:
                        P.op("pool", lambda e, cm=cm, b=b: e.memset(cm[:, :, (b - 2) * 8:(b - 1) * 8], 0.0), writes=[cm.buf])
                    P.op("pool", lambda e, cm=cm, b=b: e.tensor_copy(out=cm[:, :, b * 8:(b + 1) * 8], in_=CT[:, :, b * 8:(b + 1) * 8]),
                         reads=[CT.buf], writes=[cm.buf])
                    P.op("act", lambda e, sn_=sn_: e.activation(out=S0b[:], in_=sn_[:].rearrange("p j n -> p (j n)"), func=AF.Copy),
                         reads=snb, writes=[S0b.buf])
                    for jp in range(8):
                        P.op("pe", lambda e, jp=jp: e.transpose(bankbf[0][:, jp * 128:(jp + 1) * 128], S0b[:, jp * 128:(jp + 1) * 128], ident_b),
                             reads=[S0b.buf, cb.buf], writes=[pb[0]])
                    P.op("act", lambda e: e.activation(out=S0T[:], in_=bankbf[0], func=AF.Copy), reads=[pb[0]], writes=[S0T.buf])
                    for g in range(4):
                        bi = 6 + g // 2
                        P.op("pe", lambda e, g=g, bi=bi, cm=cm, b=b: e.matmul(
                            bank[bi][:, (g % 2) * 256:(g % 2) * 256 + 256], lhsT=cm[:, g, :], rhs=S0T[:, g * 256:(g + 1) * 256],
                            start=False, stop=(b == NSEQ - 1 and g % 2 == 1)),
                            reads=[cm.buf, S0T.buf], writes=[pb[bi]])
                    P.op("dve", lambda e, b=b: e.tensor_scalar(out=Bm[:], in0=Btok[:], scalar1=cpc("seqsel", b), scalar2=None, op0=ALU.mult),
                         reads=[Btok.buf, cpk.buf, selb.buf], writes=[Bm.buf])
                    for jp in range(8):
                        bi = 2 + jp // 4
                        gq = jp // 2
                        P.op("pe", lambda e, jp=jp, bi=bi, gq=gq: e.matmul(
                            bank[bi][:, (jp % 4) * 128:(jp % 4 + 1) * 128], lhsT=xdtd[:, jp * 128:(jp + 1) * 128],
                            rhs=Bm[:, gq * 128:(gq + 1) * 128], start=True, stop=True),
                            reads=[xdtd.buf, Bm.buf], writes=[pb[bi]])
                    for jp in range(8):
                        bi = 2 + jp // 4
                        P.op("dve", lambda e, jp=jp, bi=bi, sn_=sn_, b=b: e.scalar_tensor_tensor(
                            out=sn_[:, jp, :], in0=sn_[:, jp, :], scalar=decT[:, jp, b:b + 1],
                            in1=bank[bi][:, (jp % 4) * 128:(jp % 4 + 1) * 128], op0=ALU.mult, op1=ALU.add),
                            reads=snb + [decT.buf, pb[bi]], writes=snb)
                P.dma("sp", nsts_d[NSEQ - 1].rearrange("(j p) n -> p j n", p=128), S0n[(NSEQ - 1) % 3][:], ch_s0[(NSEQ - 1) % 3],
                      reads=S0n_bufs[(NSEQ - 1) % 3])
                P.op("dve", lambda e: e.tensor_tensor(out=ycomb[:].rearrange("p (h q) -> p h q", h=16),
                                                      in0=yo.rearrange("p (h q) -> p h q", h=16),
                                                      in1=bc(s16(EAC).unsqueeze(2), [128, 16, 64]), op=ALU.mult),
                     reads=[pb[6], pb[7], sm16.buf], writes=[ycomb.buf])
                P.op("dve", lambda e: e.tensor_tensor(out=ycomb[:], in0=ycomb[:], in1=yd, op=ALU.add),
                     reads=[ycomb.buf, pb[4], pb[5]], writes=[ycomb.buf])
            yield
            for j in range(8):
                P.op("pe", lambda e, j=j: e.transpose(bankbf[6][:, j * 128:(j + 1) * 128], szT[:, j, :], ident_b),
                     reads=[szT.buf, cb.buf], writes=[pb[6]])
            P.op("dve", lambda e: e.tensor_tensor(out=ycomb[:], in0=ycomb[:], in1=bankbf[6], op=ALU.mult),
                 reads=[ycomb.buf, pb[6]], writes=[ycomb.buf])
            for g in range(4):
                P.op("act", lambda e, g=g: e.activation(out=ygn[:, g * 256:(g + 1) * 256], in_=ycomb[:, g * 256:(g + 1) * 256],
                                                        func=AF.Square, accum_out=ss4[:, g:g + 1]),
                     reads=[ycomb.buf], writes=[ygn.buf, ss4.buf])
            rstd_pool(ss4, 0, 4, 256, eps=4.0 * EPS)
            P.op("pool", lambda e: e.tensor_tensor(out=ygn[:].rearrange("p (g q) -> p g q", g=4),
                                                   in0=ycomb[:].rearrange("p (g q) -> p g q", g=4),
                                                   in1=bc(ss4[:, 8:12].unsqueeze(2), [128, 4, 256]), op=ALU.mult),
                 reads=[ycomb.buf, ss4.buf], writes=[ygn.buf])
            yield
            if not sample:
                st_ = pair(4)
                for g in range(4):
                    bi = 4 + g // 2
                    P.op("pe", lambda e, g=g, bi=bi: e.matmul(bank[bi][:, (g % 2) * 256:(g % 2) * 256 + 256],
                                                              lhsT=Btok[:, g * 128:(g + 1) * 128], rhs=xdtd[:, g * 256:(g + 1) * 256],
                                                              start=True, stop=True),
                         reads=[Btok.buf, xdtd.buf], writes=[pb[bi]])
                if first:
                    P.op("dve", lambda e: e.tensor_copy(out=ST[:], in_=st_), reads=[pb[4], pb[5]], writes=[ST.buf])
                else:
                    P.op("pool", lambda e: e.tensor_tensor(out=ST[:].rearrange("p (h q) -> p h q", h=16),
                                                           in0=ST[:].rearrange("p (h q) -> p h q", h=16),
                                                           in1=bc(s16(DEC).unsqueeze(2), [128, 16, 64]), op=ALU.mult),
                         reads=[ST.buf, sm16.buf], writes=[ST.buf])
                    P.op("dve", lambda e: e.tensor_tensor(out=ST[:], in0=ST[:], in1=st_, op=ALU.add),
                         reads=[ST.buf, pb[4], pb[5]], writes=[ST.buf])
                if cidx < NCH - 1:
                    P.op("act", lambda e: e.activation(out=STb[:], in_=ST[:], func=AF.Copy), reads=[ST.buf], writes=[STb.buf])
            for j in range(8):
                P.op("pe", lambda e, j=j: e.transpose(bankbf[7][:, j * 128:(j + 1) * 128], ygn[:, j * 128:(j + 1) * 128], ident_b),
                     reads=[ygn.buf, cb.buf], writes=[pb[7]])
            for j in range(8):
                P.op("act", lambda e, j=j: e.activation(out=yssdT[:, j, :], in_=bankbf[7][:, j * 128:(j + 1) * 128],
                                                        func=AF.Identity, scale=cpc("ng", j)),
                     reads=[pb[7], cpk.buf], writes=[yssdT.buf])
            yield
            for half in range(2):
                bi = 6 + half
                for k in range(16):
                    if k < 8:
                        lh = y_scT[:, k, tok0:tok0 + 128]
                        rb = [y_scT.buf]
                    else:
                        lh = yssdT[:, k - 8, :]
                        rb = [yssdT.buf]
                    P.op("pe", lambda e, lh=lh, k=k, half=half, bi=bi: e.matmul(
                        bank[bi], lhsT=lh, rhs=Wout[:, k, half * 512:(half + 1) * 512], start=(k == 0), stop=(k == 15)),
                        reads=rb + [wout_bufs[k // 4]], writes=[pb[bi]])
                yield
            mix = pair(6)
            P.op("act", lambda e: e.activation(out=ygn[:], in_=mix, func=AF.Square, accum_out=sttb[:, 4:5]),
                 reads=[pb[6], pb[7]], writes=[ygn.buf, sttb.buf])
            rstd_pool(sttb, 4, 1, D)
            gm = gmS if sample else gmP
            P.op("dve", lambda e, gm=gm: e.tensor_tensor(out=ycomb[:], in0=mix, in1=gm[:], op=ALU.mult),
                 reads=[pb[6], pb[7], gm.buf], writes=[ycomb.buf])
            P.op("dve", lambda e, xs_=xs_: e.scalar_tensor_tensor(out=xs_[:], in0=ycomb[:], scalar=sttb[:, 6:7], in1=xs_[:],
                                                                  op0=ALU.mult, op1=ALU.add),
                 reads=[ycomb.buf, sttb.buf, xs_.buf], writes=[xs_.buf])
            P.dma("sp", x1s_d[tok0:tok0 + 128, :], xs_[:], ch_xinb[ci % NXS], reads=[xs_.buf])
            load_x1b(ci + NXS)
            if sample:
                P.dma("sp", nxcs_d[:, :], nxcs[:].rearrange("p j b r -> p (j b r)"), ch_nxcs, reads=[nxcs.buf])
            if (not sample) and cidx == NCH - 1:
                P.dma("sp", nxcp_d[:, :], xhist[:].rearrange("p j r -> p (j r)"), ch_nxcp, reads=[xhist.buf])
                for jp in range(8):
                    bi = 4 + jp // 4
                    P.op("pe", lambda e, jp=jp, bi=bi: e.transpose(bank[bi][:, (jp % 4) * 128:(jp % 4 + 1) * 128],
                                                                   ST[:, jp * 128:(jp + 1) * 128], ident_f),
                         reads=[ST.buf, cpk.buf], writes=[pb[bi]])
                P.op("dve", lambda e: e.tensor_copy(out=ycomb[:], in_=pair(4)), reads=[pb[4], pb[5]], writes=[ycomb.buf])
                P.dma("sp", nstp_d[:, :].rearrange("(j p) n -> p j n", p=128), ycomb[:].rearrange("p (j n) -> p j n", j=8),
                      ch_nstp, reads=[ycomb.buf])
            yield

        def drain(g, cpbase=None):
            for i, _ in enumerate(g):
                if cpbase is not None:
                    cp(cpbase + i)

        def interleave(ga, gb, pattern="BFBBFBBFBBFBFBFBBFBF"):
            gens = {"B": ga, "F": gb}
            for ch in pattern:
                g = gens.get(ch)
                if g is None:
                    continue
                try:
                    next(g)
                except StopIteration:
                    gens[ch] = None
            for g in gens.values():
                if g is not None:
                    for _ in g:
                        pass

        for n in range(NXS):
            load_x1b(n)
        drain(front(0))
        cp(50)
        drain(back(0))
        cp(51)
        drain(front(1))
        cp(52)
        import os
        noil = os.environ.get("KNOIL", "0")
        for ci in range(1, len(chunk_list)):
            if noil == "1":
                drain(back(ci), 60 if ci == 1 else None)
                if ci == 1:
                    cp(55)
                if ci + 1 < len(chunk_list):
                    drain(front(ci + 1))
            else:
                interleave(back(ci), front(ci + 1) if ci + 1 < len(chunk_list) else None)
            if ci == 1:
                cp(53)
            if ci == 2:
                cp(54)

        P.barrier()
        A.off = mark_p2
        if stop == 2:
            P.emit(nc, final_chans=out_chans)
            return nc, P

        Wg = A.alloc("Wg", [128, 8, HID], BF16)
        Wu = A.alloc("Wu", [128, 8, HID], BF16)
        Wd = A.alloc("Wd", [128, NHT, 1024], BF16)
        ch_wg = [P.chan("wg%d" % i) for i in range(4)]
        ch_wu = [P.chan("wu%d" % i) for i in range(4)]
        ch_wd = [P.chan("wd%d" % i) for i in range(4)]
        wg_b = [Buf("wg%d" % i) for i in range(4)]
        wu_b = [Buf("wu%d" % i) for i in range(4)]
        wd_b = [Buf("wd%d" % i) for i in range(4)]
        wd_split = [(0, 6), (6, 12), (12, 17), (17, 22)]
        for i in range(4):
            P.dma("pool", Wg[:, :, i * 704:(i + 1) * 704], wg_d[i].rearrange("p (k c) -> p k c", k=8), ch_wg[i], writes=[wg_b[i]])
            P.dma("pool", Wu[:, :, i * 704:(i + 1) * 704], wu_d[i].rearrange("p (k c) -> p k c", k=8), ch_wu[i], writes=[wu_b[i]])
        for i, (a0, a1) in enumerate(wd_split):
            P.dma("pool", Wd[:, a0:a1, :], wd_d[a0 * 128:a1 * 128, :].rearrange("(i p) n -> p i n", p=128), ch_wd[i], writes=[wd_b[i]])

        def wd_buf(i):
            for n, (a0, a1) in enumerate(wd_split):
                if a0 <= i < a1:
                    return wd_b[n]

        gfP = A.alloc("gfP", [128, 1024])
        gfS = A.alloc("gfS", [128, 1024])
        ch_gf = [P.chan("gfload%d" % i) for i in range(2)]
        P.dma("sp", gfP[:], gsave_d[0], ch_gf[0], writes=[gfP.buf])
        P.dma("sp", gfS[:], gsave_d[1], ch_gf[1], writes=[gfS.buf])
        xf = [[A.alloc("xf%d_%d" % (s, t), [128, 1024]) for t in range(2)] for s in range(2)]
        ch_xf = [[P.chan("xf%d_%d" % (s, t)) for t in range(2)] for s in range(2)]
        out_chans += [c for r in ch_xf for c in r]
        xn2 = A.alloc("xn2", [128, 1024], BF16)
        h2T = A.alloc("h2T", [128, 8, 256], BF16)
        aT = A.alloc("aT", [128, NHT, 256], BF16)
        sg = [A.alloc("sg%d" % i, [128, 256]) for i in range(2)]
        ftmp = A.alloc("ftmp", [128, 1024])

        tiles = [(t * 256, 256, False) for t in range(8)] + [(LP, 128, True)]

        def load_x2(n):
            if n < len(tiles):
                t0, ts, _ = tiles[n]
                for tt in range(ts // 128):
                    P.dma("sp", xf[n % 2][tt][:], x1s_d[t0 + tt * 128:t0 + (tt + 1) * 128, :], ch_xf[n % 2][tt], writes=[xf[n % 2][tt].buf])

        h2Ts = [h2T, A.alloc("h2T1", [128, 8, 256], BF16)]

        xn2s = [xn2, A.alloc("xn2b", [128, 1024], BF16)]
        stq2 = [A.alloc("stq2_%d" % i, [128, 4]) for i in range(2)]

        def stage_A2_pre(ti):
            t0_, ts_, smp_ = tiles[ti]
            for tt in range(ts_ // 128):
                stage_A_pre(xf[ti % 2][tt], xn2s[tt], stq2[tt])

        def stage_A2_post(ti):
            t0_, ts_, smp_ = tiles[ti]
            for tt in range(ts_ // 128):
                stage_A_post(xn2s[tt], A_f, Bv_f, smp_, h2Ts[ti % 2][:, :, tt * 128:(tt + 1) * 128], h2Ts[ti % 2].buf, ftmp)

        def stage_A2(ti):
            stage_A2_pre(ti)
            stage_A2_post(ti)

        load_x2(0)
        stage_A2(0)
        for ti, (tok0, TS, sample) in enumerate(tiles):
            nt = TS // 128
            sl = ti % 2
            h2c = h2Ts[ti % 2]
            load_x2(ti + 1)
            for i in range(NHT):
                bi = 1 + i % 3
                wq = i * 128 // 704
                wq2 = (i * 128 + 127) // 704
                for k in range(8):
                    P.op("pe", lambda e, i=i, k=k, bi=bi, TS=TS, h2c=h2c: e.matmul(bank[bi][:, 0:TS], lhsT=Wg[:, k, i * 128:(i + 1) * 128],
                                                                                   rhs=h2c[:, k, 0:TS], start=(k == 0), stop=(k == 7)),
                         reads=[wg_b[wq], wg_b[wq2], h2c.buf], writes=[pb[bi]])
                for k in range(8):
                    P.op("pe", lambda e, i=i, k=k, bi=bi, TS=TS, h2c=h2c: e.matmul(bank[bi][:, 256:256 + TS], lhsT=Wu[:, k, i * 128:(i + 1) * 128],
                                                                                   rhs=h2c[:, k, 0:TS], start=(k == 0), stop=(k == 7)),
                         reads=[wu_b[wq], wu_b[wq2], h2c.buf], writes=[pb[bi]])
                s_ = sg[i % 2]
                P.op("act", lambda e, s_=s_, bi=bi, TS=TS: e.activation(out=s_[:, 0:TS], in_=bank[bi][:, 0:TS], func=AF.Silu),
                     reads=[pb[bi]], writes=[s_.buf])
                P.op("dve", lambda e, s_=s_, bi=bi, i=i, TS=TS: e.tensor_tensor(out=aT[:, i, 0:TS], in0=s_[:, 0:TS],
                                                                                 in1=bank[bi][:, 256:256 + TS], op=ALU.mult),
                     reads=[s_.buf, pb[bi]], writes=[aT.buf])
                if i == 8 and ti + 1 < len(tiles):
                    stage_A2_pre(ti + 1)
            for tt in range(nt):
                for half in range(2):
                    bi = 4 + tt * 2 + half
                    for i in range(NHT):
                        P.op("pe", lambda e, i=i, tt=tt, half=half, bi=bi: e.matmul(
                            bank[bi], lhsT=aT[:, i, tt * 128:(tt + 1) * 128], rhs=Wd[:, i, half * 512:(half + 1) * 512],
                            start=(i == 0), stop=(i == NHT - 1)),
                            reads=[aT.buf, wd_buf(i)], writes=[pb[bi]])
            if ti + 1 < len(tiles):
                stage_A2_post(ti + 1)
            for tt in range(nt):
                fp_ = pair(4 + tt * 2)
                pbs = [pb[4 + tt * 2], pb[5 + tt * 2]]
                xt = xf[sl][tt]
                P.op("act", lambda e, fp_=fp_: e.activation(out=xn2[:], in_=fp_, func=AF.Square, accum_out=stt[:, 4:5]),
                     reads=pbs, writes=[xn2.buf, stt.buf])
                rstd_pool(stt, 4, 1, D)
                gf = gfS if sample else gfP
                P.op("dve", lambda e, fp_=fp_, gf=gf: e.tensor_tensor(out=ftmp[:], in0=fp_, in1=gf[:], op=ALU.mult),
                     reads=pbs + [gf.buf], writes=[ftmp.buf])
                P.op("dve", lambda e, xt=xt: e.scalar_tensor_tensor(out=xt[:], in0=ftmp[:], scalar=stt[:, 6:7], in1=xt[:],
                                                                    op0=ALU.mult, op1=ALU.add),
                     reads=[ftmp.buf, stt.buf, xt.buf], writes=[xt.buf])
                P.dma("sp", y_d[tok0 + tt * 128:tok0 + (tt + 1) * 128, :], xt[:], ch_xf[sl][tt], reads=[xt.buf])

        P.emit(nc, final_chans=out_chans)
    return nc, P


def _fm(v, ntile):
    return np.ascontiguousarray(np.asarray(v, np.float32).reshape(ntile, 128).T)


def _host_consts():
    idx = np.arange(128)
    ident = np.eye(128, dtype=np.float32)
    tri = (idx[:, None] <= idx[None, :]).astype(np.float32)
    negm = np.where(idx[None, :] < idx[:, None], NEG, 0.0).astype(np.float32)
    same = (idx[:, None] // LS == idx[None, :] // LS).astype(np.float32)
    triBD = tri * same
    negmBD = np.where(triBD > 0, 0.0, NEG).astype(np.float32)
    seqsel = (idx[:, None] // LS == np.arange(NSEQ)[None, :]).astype(np.float32)
    return ident, tri, negm, triBD, negmBD, same, seqsel


_CACHE = {}


def kernel(x_prompt, x_sample, c_prompt, c_sample, state_sc_conv, state_ssm_conv, state_ssm,
           w_ada, b_ada, g_mix_pre, g_mix_post, g_ffn_pre, g_ffn_post, w_in, sc_conv_w,
           ssm_conv_w, ssm_conv_b, dt_bias, a_log, d_skip, ssm_norm_g, w_out, w_gate, w_up, w_down):
    f32 = np.float32
    x_prompt = np.asarray(x_prompt, f32)
    x_sample = np.asarray(x_sample, f32)
    w_in_ = np.asarray(w_in, f32)[0]
    w_ada_ = np.asarray(w_ada, f32)[0]
    wada = np.ascontiguousarray(w_ada_.reshape(8, 128, 6, 1024).transpose(2, 1, 0, 3).reshape(6, 128, 8192))
    sc = w_in_[:, 0:3072].reshape(8, 128, 3, 8, 128)
    wsc = np.ascontiguousarray(sc[:, :, [1, 2, 0], :, :].transpose(3, 1, 0, 2, 4).reshape(8, 128, 8 * 384))
    ssm = w_in_[:, 3072:6144].reshape(8, 128, 6, 512)
    wssm = np.ascontiguousarray(ssm.transpose(2, 1, 0, 3).reshape(6, 128, 8 * 512))
    wdt = np.ascontiguousarray(w_in_[:, 6144:6160].reshape(8, 128, 16).transpose(1, 0, 2).reshape(128, 128))
    wout = np.ascontiguousarray(np.asarray(w_out, f32)[0])
    wg = np.ascontiguousarray(np.asarray(w_gate, f32)[0].reshape(8, 128, 4, 704).transpose(2, 1, 0, 3).reshape(4, 128, 8 * 704))
    wu = np.ascontiguousarray(np.asarray(w_up, f32)[0].reshape(8, 128, 4, 704).transpose(2, 1, 0, 3).reshape(4, 128, 8 * 704))
    wd = np.ascontiguousarray(np.asarray(w_down, f32)[0])
    badar = np.ascontiguousarray(np.asarray(b_ada, f32).reshape(1, 6144))
    gpostr = np.ascontiguousarray(np.stack([np.asarray(g_mix_post, f32)[0], np.asarray(g_ffn_post, f32)[0]]))

    ident, tri, negm, triBD, negmBD, same, seqsel = _host_consts()
    scw = np.asarray(sc_conv_w, f32)[0]
    xcw = np.asarray(ssm_conv_w, f32)[0]
    base = np.zeros((128, CPW), f32)

    def put(name, arr):
        arr = np.asarray(arr, f32)
        base[:, _CP[name]:_CP[name] + arr.shape[1]] = arr

    put("ident", ident); put("tri", tri); put("negm", negm); put("triBD", triBD); put("negmBD", negmBD)
    put("same", same); put("seqsel", seqsel)
    put("gpre", _fm(np.asarray(g_mix_pre, f32)[0], 8))
    put("gfpre", _fm(np.asarray(g_ffn_pre, f32)[0], 8))
    put("scw", scw.reshape(3, 8, 128).transpose(2, 1, 0).reshape(128, 24))
    put("xcw", xcw.reshape(4, 16, 128).transpose(2, 1, 0).reshape(128, 64))
    put("xcb", _fm(np.asarray(ssm_conv_b, f32)[0], 16))
    put("dtb", np.broadcast_to(np.asarray(dt_bias, f32).reshape(1, 16), (128, 16)))
    put("alog", np.broadcast_to(np.asarray(a_log, f32).reshape(1, 16), (128, 16)))
    put("dcol", _fm(np.repeat(np.asarray(d_skip, f32)[0], 64), 8))
    put("ng", _fm(np.asarray(ssm_norm_g, f32)[0], 8))
    put("bada", _fm(np.asarray(b_ada, f32)[0], 48))
    put("eps", np.full((128, 1), EPS, f32))

    in_maps = []
    for i in range(NCORES):
        cpk = base.copy()
        call = np.concatenate([np.asarray(c_prompt, f32)[i:i + 1], np.asarray(c_sample, f32)[16 * i:16 * i + 16]], axis=0)
        cpk[:, _CP["cT"]:_CP["cT"] + 136] = call.reshape(17, 8, 128).transpose(2, 1, 0).reshape(128, 136)
        xall = np.concatenate([x_prompt[i], x_sample[16 * i:16 * i + 16].reshape(128, D)], axis=0)
        scs = np.asarray(state_sc_conv, f32)[0, 16 * i:16 * i + 16]
        scst = scs.reshape(16, 2, 8, 128).transpose(3, 2, 0, 1).reshape(128, 256)
        xcs = np.asarray(state_ssm_conv, f32)[0, 16 * i:16 * i + 16]
        xcst = xcs.reshape(16, 3, 16, 128).transpose(3, 2, 0, 1).reshape(128, 768)
        sst = np.asarray(state_ssm, f32)[0, 16 * i:16 * i + 16].reshape(16, 1024, 128)
        in_maps.append({
            "xall": np.ascontiguousarray(xall), "cpk": cpk, "scst": np.ascontiguousarray(scst),
            "xcst": np.ascontiguousarray(xcst), "sst": np.ascontiguousarray(sst),
            "wada": wada, "badar": badar, "gpostr": gpostr, "wsc": wsc, "wssm": wssm, "wdt": wdt,
            "wout": wout, "wg": wg, "wu": wu, "wd": wd,
        })

    if "nc" not in _CACHE:
        import os
        _CACHE["nc"] = build_program(int(os.environ.get("KSTOP", "99")))
    nc, _ = _CACHE["nc"]
    import os
    ncr = int(os.environ.get("KCORES", str(NCORES)))
    res = run_bass_kernel_spmd(nc, in_maps[:ncr], core_ids=list(range(ncr)))
    R = list(res.results)
    while len(R) < NCORES:
        R.append(R[0])

    y_prompt = np.stack([R[i]["y"][0:LP] for i in range(NCORES)]).astype(f32)
    y_sample = np.concatenate([R[i]["y"][LP:].reshape(16, LS, D) for i in range(NCORES)], axis=0).astype(f32)
    nscp = np.stack([R[i]["nscp"].reshape(128, 8, 2).transpose(2, 1, 0).reshape(2, 1024) for i in range(NCORES)])[None]
    nxcp = np.stack([R[i]["nxcp"].reshape(128, 16, 3).transpose(2, 1, 0).reshape(3, 2048) for i in range(NCORES)])[None]
    nstp = np.stack([R[i]["nstp"].reshape(16, 64, 128) for i in range(NCORES)])[None]
    nscs = np.concatenate([R[i]["nscs"].reshape(128, 8, 16, 2).transpose(2, 3, 1, 0).reshape(16, 2, 1024) for i in range(NCORES)])[None]
    nxcs = np.concatenate([R[i]["nxcs"].reshape(128, 16, 16, 3).transpose(2, 3, 1, 0).reshape(16, 3, 2048) for i in range(NCORES)])[None]
    nsts = np.concatenate([R[i]["nsts"].reshape(16, 16, 64, 128) for i in range(NCORES)])[None]
    return (y_prompt, y_sample, np.ascontiguousarray(nscp, f32), np.ascontiguousarray(nxcp, f32),
            np.ascontiguousarray(nstp, f32), np.ascontiguousarray(nscs, f32), np.ascontiguousarray(nxcs, f32),
            np.ascontiguousarray(nsts, f32))
```

```python
import contextlib
import numpy as np
import concourse.bass as bass
import concourse.mybir as mybir
from concourse.bass_utils import run_bass_kernel_spmd

F32 = mybir.dt.float32
BF16 = mybir.dt.bfloat16
AF = mybir.ActivationFunctionType
ALU = mybir.AluOpType

NCORES = 8
D = 1024
LP = 2048
NSEQ = 16
LS = 8
NTOK = LP + NSEQ * LS
NCH = LP // 128
HID = 2816
NHT = HID // 128
EPS = 1e-6
NEG = -30000.0
import random as _random
PNOISE = 0.15
PSEED = 2
_prng = _random.Random(PSEED)
XLAT = 0.7

_CP = {}
_off = 0
for _n, _w in [("ident", 128), ("tri", 128), ("negm", 128), ("triBD", 128), ("negmBD", 128),
               ("same", 128), ("seqsel", 16), ("gpre", 8), ("gfpre", 8), ("scw", 24),
               ("xcw", 64), ("xcb", 16), ("dtb", 16), ("alog", 16), ("dcol", 8), ("ng", 8),
               ("bada", 48), ("cT", 136), ("eps", 1)]:
    _CP[_n] = _off
    _off += _w
CPW = _off


class Buf:
    __slots__ = ("name", "writers", "readers", "excl", "gen_deps")

    def __init__(self, name, excl=False):
        self.name = name
        self.writers = []
        self.readers = []
        self.gen_deps = set()
        self.excl = excl


class Chan:
    def __init__(self, name):
        self.name = name
        self.sem = None
        self.n = 0
        self.last = None


class Op:
    __slots__ = ("eng", "fn", "deps", "idx", "marked", "chan", "seq", "cnt", "waits", "sdeps", "cost", "dlat", "site", "adeps")


class _FakeEng:
    def __getattr__(self, name):
        def f(*a, **k):
            out = k.get("out", a[0] if a else None)
            return (name, out, k)
        return f


def _est_cost(eng, fn, is_dma):
    try:
        name, out, k = fn(_FakeEng())
        n = 1
        for d in out.shape[1:]:
            n *= int(d)
        parts = int(out.shape[0])
    except Exception:
        return 0.3, 0.0
    if is_dma:
        return 0.15, 2.0 + n * parts * 4 / 250e3
    if eng == "pe":
        return max(0.058, (n + 12) / 2400.0), 0.0
    if eng == "act":
        return 0.17 + n / 1250.0 + (0.1 if k.get("accum_out") is not None else 0.0), 0.0
    if eng == "dve":
        return 0.19 + n / 960.0, 0.0
    if eng == "pool":
        return 0.2 + n / 560.0, 0.0
    return 0.2, 0.0


class Prog:
    ENGS = ("pe", "act", "dve", "pool", "sp")

    def __init__(self):
        self.ops = []
        self.chans = []

    def chan(self, name):
        c = Chan(name)
        self.chans.append(c)
        return c

    def _record(self, eng, fn, reads, writes, chan=None, extra=(), nowaw=False):
        op = Op()
        op.eng = eng
        op.fn = fn
        op.idx = len(self.ops)
        op.marked = False
        op.chan = chan
        op.seq = None
        op.cnt = None
        deps = set(extra)
        sdeps = set()
        for b in reads:
            deps.update(b.writers)
            if b.excl:
                for r in b.readers:
                    if self.ops[r].eng != eng:
                        deps.add(r)
        for b in writes:
            deps.update(b.readers)
            if nowaw and not b.readers:
                deps.update(b.gen_deps)
            for w in b.writers:
                wo = self.ops[w]
                if wo.eng != eng or wo.chan is not None or chan is not None:
                    if not nowaw:
                        deps.add(w)
                else:
                    sdeps.add(w)
        if chan is not None and getattr(chan, "last", None) is not None:
            sdeps.add(chan.last)
        if eng == "pe" and fn is not None:
            pp = {d for d in deps if (self.ops[d].eng == "pe" and self.ops[d].chan is None)}
            sdeps |= pp
            deps = deps - pp
        import sys as _sys
        op.site = _sys._getframe(2).f_lineno
        op.deps = deps
        op.adeps = set(deps)
        op.sdeps = sdeps | deps
        op.cost, op.dlat = (0.0, 0.0) if fn is None else _est_cost(eng, fn, chan is not None)
        if chan is not None:
            chan.n += 1
            op.seq = chan.n
            chan.last = op.idx
        self.ops.append(op)
        for b in reads:
            b.readers.append(op.idx)
        for b in writes:
            if b.readers:
                b.gen_deps = set(b.readers) | set(b.writers)
                b.writers = [op.idx]
                b.readers = []
            else:
                b.writers.append(op.idx)
        return op

    def op(self, eng, fn, reads=(), writes=(), nowaw=False):
        return self._record(eng, fn, reads, writes, nowaw=nowaw)

    def dma(self, eng, out, in_, chan, reads=(), writes=()):
        def fn(e, out=out, in_=in_):
            return e.dma_start(out=out, in_=in_)
        return self._record(eng, fn, reads, writes, chan=chan)

    def barrier(self):
        last = {}
        for op in self.ops:
            if op.fn is None:
                continue
            if op.chan is None:
                last[("e", op.eng)] = op.idx
            else:
                last[("c", id(op.chan))] = op.idx
        deps = set(last.values())
        for eng in self.ENGS:
            self._record(eng, None, (), (), extra=deps)

    def schedule(self):
        import heapq
        ops = self.ops
        order = []
        seg = []

        def flush():
            if not seg:
                return
            ids = set(o.idx for o in seg)
            preds = {o.idx: [d for d in o.sdeps if d in ids] for o in seg}
            succs = {o.idx: [] for o in seg}
            for o in seg:
                for d in preds[o.idx]:
                    succs[d].append(o.idx)

            def lat(d, o):
                do = ops[d]
                if do.chan is not None:
                    return do.dlat + 0.2
                return XLAT if do.eng != o.eng else 0.02

            prio = {}
            for o in reversed(seg):
                p = 0.0
                for sidx in succs[o.idx]:
                    q = lat(o.idx, ops[sidx]) + prio[sidx]
                    if q > p:
                        p = q
                prio[o.idx] = (p + o.cost) * (1.0 + PNOISE * (_prng.random() - 0.5))
            nun = {o.idx: len(preds[o.idx]) for o in seg}
            avail = {o.idx: 0.0 for o in seg}
            pending = {e: [] for e in self.ENGS}
            ready = {e: [] for e in self.ENGS}
            free = {e: 0.0 for e in self.ENGS}
            for o in seg:
                if nun[o.idx] == 0:
                    heapq.heappush(pending[o.eng], (0.0, -prio[o.idx], o.idx))
            start = {}
            left = len(seg)
            while left:
                best = None
                for e in self.ENGS:
                    while pending[e] and pending[e][0][0] <= free[e]:
                        a, np_, i = heapq.heappop(pending[e])
                        heapq.heappush(ready[e], (np_, i))
                    if ready[e]:
                        t = free[e]
                    elif pending[e]:
                        t = pending[e][0][0]
                    else:
                        continue
                    if best is None or t < best[0]:
                        best = (t, e)
                t, e = best
                while pending[e] and pending[e][0][0] <= t:
                    a, np_, i = heapq.heappop(pending[e])
                    heapq.heappush(ready[e], (np_, i))
                np_, i = heapq.heappop(ready[e])
                o = ops[i]
                start[i] = t
                fin = t + o.cost
                free[e] = fin
                left -= 1
                for sidx in succs[i]:
                    a = fin + lat(i, ops[sidx])
                    if a > avail[sidx]:
                        avail[sidx] = a
                    nun[sidx] -= 1
                    if nun[sidx] == 0:
                        heapq.heappush(pending[ops[sidx].eng], (avail[sidx], -prio[sidx], sidx))
            order.extend(sorted(seg, key=lambda o: (start[o.idx], o.idx)))
            self.sim_time = getattr(self, "sim_time", 0.0) + max(free.values())
            del seg[:]

        for op in ops:
            if op.fn is None:
                flush()
                order.append(op)
            else:
                seg.append(op)
        flush()
        self.order = order

    def emit(self, nc, final_chans=()):
        import os
        if os.environ.get("KSCHED", "1") == "1":
            self.schedule()
            ops_order = self.order
        else:
            ops_order = list(self.ops)
        allops = self.ops
        rank = {}
        for r, op in enumerate(ops_order):
            rank[op.idx] = r
        last = {}
        for op in ops_order:
            if op.fn is None:
                op.deps = set(last.values())
            elif op.chan is not None:
                last[("c", id(op.chan))] = op.idx
            else:
                last[("e", op.eng)] = op.idx
        ops = allops
        for op in ops_order:
            red = {}
            for d in op.deps:
                dop = ops[d]
                if dop.fn is None:
                    continue
                key = ("c", id(dop.chan)) if dop.chan is not None else ("e", dop.eng)
                if key not in red or rank[red[key]] < rank[d]:
                    red[key] = d
            op.deps = set(red.values())
            for d in op.deps:
                ops[d].marked = True
        with contextlib.ExitStack() as st:
            esem = {e: st.enter_context(nc.semaphore("s_" + e)) for e in self.ENGS}
            for c in self.chans:
                c.sem = st.enter_context(nc.semaphore("c_" + c.name))
            cnt = {e: 0 for e in self.ENGS}
            for op in ops_order:
                if op.chan is None and op.marked and op.fn is not None:
                    cnt[op.eng] += 1
                    op.cnt = cnt[op.eng]
            waited = {e: {} for e in self.ENGS}
            per_eng = {e: [] for e in self.ENGS}
            for op in ops_order:
                need = {}
                for d in op.deps:
                    dop = ops[d]
                    if dop.fn is None:
                        continue
                    if dop.chan is not None:
                        key, val = ("c", dop.chan), 16 * dop.seq
                    else:
                        key, val = ("e", dop.eng), dop.cnt
                    if need.get(key, 0) < val:
                        need[key] = val
                w = []
                for key, val in need.items():
                    if waited[op.eng].get(key, 0) >= val:
                        continue
                    waited[op.eng][key] = val
                    sem = key[1].sem if key[0] == "c" else esem[key[1]]
                    w.append((sem, val))
                op.waits = w
                per_eng[op.eng].append(op)
            self.stats = {e: len(per_eng[e]) for e in self.ENGS}
            self.stats["cnt"] = dict(cnt)
            with nc.Block() as block:
                def run(e, lst, is_sp=False):
                    for op in lst:
                        for sem, val in op.waits:
                            e.wait_ge(sem, val)
                        if op.fn is None:
                            continue
                        ins = op.fn(e)
                        if op.chan is not None:
                            ins.then_inc(op.chan.sem, 16)
                        elif op.marked:
                            ins.then_inc(esem[op.eng], 1)
                    if is_sp:
                        for c in final_chans:
                            if c.n:
                                e.wait_ge(c.sem, 16 * c.n)

                @block.tensor
                def _(e):
                    run(e, per_eng["pe"])

                @block.scalar
                def _(e):
                    run(e, per_eng["act"])

                @block.vector
                def _(e):
                    run(e, per_eng["dve"])

                @block.gpsimd
                def _(e):
                    run(e, per_eng["pool"])

                @block.sync
                def _(e):
                    run(e, per_eng["sp"], is_sp=True)


class _Stop(Exception):
    pass


class Tl:
    def __init__(self, ap, buf, off, nbytes):
        self.ap = ap
        self.buf = buf
        self.off = off
        self.nbytes = nbytes

    def __getitem__(self, k):
        return self.ap[k]


class SBA:
    def __init__(self, big, total_bytes):
        self.big = big
        self.total = total_bytes
        self.off = 0
        self.peak = 0

    def _view(self, off, shape, dt):
        nfree = 1
        for s in shape[1:]:
            nfree *= s
        esz = 4 if dt == F32 else 2
        nbytes = nfree * esz
        assert off % 4 == 0 and nbytes % 4 == 0, (off, nbytes)
        ap = self.big[0:shape[0], off // 4:(off + nbytes) // 4]
        if dt != F32:
            ap = ap.bitcast(dt)
        if len(shape) == 3:
            ap = ap.rearrange("p (a b) -> p a b", a=shape[1])
        elif len(shape) == 4:
            ap = ap.rearrange("p (a b c) -> p a b c", a=shape[1], b=shape[2])
        return ap, nbytes

    def alloc(self, name, shape, dt=F32):
        off = (self.off + 31) // 32 * 32
        ap, nbytes = self._view(off, shape, dt)
        self.off = off + nbytes
        self.peak = max(self.peak, self.off)
        assert self.off <= self.total, ("SBUF overflow", name, self.off, self.total)
        return Tl(ap, Buf(name), off, nbytes)

    def alias(self, t, shape, dt=F32, span=None):
        ap, nbytes = self._view(t.off, shape, dt)
        assert nbytes <= (span if span is not None else t.nbytes)
        return Tl(ap, t.buf, t.off, t.nbytes)


def bc(ap, shape):
    return ap.broadcast_to(list(shape))


def build_program(stop=99):
    try:
        return _build_inner(stop)
    except _Stop as e:
        return e.args


def _build_inner(stop=99):
    nc = bass.Bass("TRN2", target_bir_lowering=False)

    def din(name, shape):
        return nc.dram_tensor(name, list(shape), F32, kind="ExternalInput").ap()

    def dout(name, shape):
        return nc.dram_tensor(name, list(shape), F32, kind="ExternalOutput").ap()

    xall_d = din("xall", [NTOK, D])
    cpk_d = din("cpk", [128, CPW])
    scst_d = din("scst", [128, 8 * 16 * 2])
    xcst_d = din("xcst", [128, 16 * 16 * 3])
    sst_d = din("sst", [NSEQ, 1024, 128])
    wada_d = din("wada", [6, 128, 8 * 1024])
    badar_d = din("badar", [1, 6144])
    gpostr_d = din("gpostr", [2, 1024])
    wsc_d = din("wsc", [8, 128, 8 * 384])
    wssm_d = din("wssm", [6, 128, 8 * 512])
    wdt_d = din("wdt", [128, 8 * 16])
    wout_d = din("wout", [2048, 1024])
    wg_d = din("wg", [4, 128, 8 * 704])
    wu_d = din("wu", [4, 128, 8 * 704])
    wd_d = din("wd", [HID, 1024])

    y_d = dout("y", [NTOK, D])
    nscp_d = dout("nscp", [128, 16])
    nxcp_d = dout("nxcp", [128, 48])
    nstp_d = dout("nstp", [1024, 128])
    nscs_d = dout("nscs", [128, 256])
    nxcs_d = dout("nxcs", [128, 768])
    nsts_d = dout("nsts", [NSEQ, 1024, 128])

    x1s_d = nc.dram_tensor("x1s", [NTOK, D], F32).ap()
    gsave_d = nc.dram_tensor("gsave", [2, 128, 1024], F32).ap()

    P = Prog()
    SB_BYTES = 207 * 1024
    with contextlib.ExitStack() as es:
        big = es.enter_context(nc.sbuf_tensor("sbig", [128, SB_BYTES // 4], F32))
        ps = es.enter_context(nc.psum_tensor("ps", [128, 4096], F32))
        A = SBA(big, SB_BYTES)

        bank = [ps[:, i * 512:(i + 1) * 512] for i in range(8)]
        bankbf = [b.bitcast(BF16) for b in bank]
        pb = [Buf("pb%d" % i, excl=True) for i in range(8)]

        def pair(i):
            return ps[:, i * 512:(i + 2) * 512]

        out_chans = []

        def cp(n):
            if stop == n:
                P.barrier()
                P.emit(nc, final_chans=out_chans)
                raise _Stop(nc, P)

        cpk = A.alloc("cpk", [128, CPW])
        cb = A.alloc("cb", [128, 768], BF16)
        ones_b = A.alloc("ones_b", [128, 128], BF16)
        zero_b = A.alloc("zero_b", [128, 128], BF16)
        ntri = A.alloc("ntri", [128, 2, 128], BF16)
        diagD = A.alloc("diagD", [128, 8, 128], BF16)
        Abc = A.alloc("Abc", [128, 16])
        A_m = A.alloc("A_m", [128, 8, 17])
        Bv_m = A.alloc("Bv_m", [128, 8, 17])
        A_f = A.alloc("A_f", [128, 8, 17])
        Bv_f = A.alloc("Bv_f", [128, 8, 17])
        gmP = A.alloc("gmP", [128, 1024])
        gmS = A.alloc("gmS", [128, 1024])
        stt = A.alloc("stt", [128, 8])
        mhalf = A.alloc("mhalf", [128, 4])
        cvw = A.alloc("cvw", [128, 80])
        mark_p2 = A.off
        y_scT = A.alloc("y_scT", [128, 8, NTOK], BF16)
        Wssm = A.alloc("Wssm", [128, 8, 3088], BF16)

        ident_b = cb[:, 0:128]
        tri_b = cb[:, 128:256]
        negm_b = cb[:, 256:384]
        triBD_b = cb[:, 384:512]
        negmBD_b = cb[:, 512:640]
        same_b = cb[:, 640:768]
        ident_f = cpk[:, _CP["ident"]:_CP["ident"] + 128]
        tri_f = cpk[:, _CP["tri"]:_CP["tri"] + 128]
        triBD_f = cpk[:, _CP["triBD"]:_CP["triBD"] + 128]
        epscol = cpk[:, _CP["eps"]:_CP["eps"] + 1]

        def cpc(name, i, n=1):
            return cpk[:, _CP[name] + i:_CP[name] + i + n]

        ch_c = P.chan("cpk")
        P.dma("sp", cpk[:], cpk_d[:, :], ch_c, writes=[cpk.buf])
        P.op("dve", lambda e: e.tensor_copy(out=cb[:], in_=cpk[:, 0:768]), reads=[cpk.buf], writes=[cb.buf])
        P.op("pool", lambda e: e.memset(ones_b[:], 1.0), writes=[ones_b.buf])
        P.op("pool", lambda e: e.memset(zero_b[:], 0.0), writes=[zero_b.buf])
        for i_, nm_ in enumerate(("tri", "triBD")):
            P.op("dve", lambda e, i_=i_, nm_=nm_: e.tensor_scalar(out=ntri[:, i_, :], in0=cpk[:, _CP[nm_]:_CP[nm_] + 128], scalar1=-1.0,
                                                               scalar2=None, op0=ALU.mult),
                 reads=[cpk.buf], writes=[ntri.buf])
        P.op("pool", lambda e: e.memset(mhalf[:], -0.5), writes=[mhalf.buf])
        P.op("dve", lambda e: e.tensor_scalar(out=cvw[:], in0=cpk[:, _CP["xcw"]:_CP["xcw"] + 80], scalar1=0.5, scalar2=None, op0=ALU.mult),
             reads=[cpk.buf], writes=[cvw.buf])
        P.op("act", lambda e: e.activation(out=Abc[:], in_=cpk[:, _CP["alog"]:_CP["alog"] + 16], func=AF.Exp),
             reads=[cpk.buf], writes=[Abc.buf])
        P.op("dve", lambda e: e.tensor_scalar(out=Abc[:], in0=Abc[:], scalar1=-1.0, scalar2=None, op0=ALU.mult),
             reads=[Abc.buf], writes=[Abc.buf])
        for j in range(8):
            P.op("dve", lambda e, j=j: e.tensor_scalar(out=diagD[:, j, :], in0=ident_b, scalar1=cpc("dcol", j),
                                                       scalar2=None, op0=ALU.mult),
                 reads=[cb.buf, cpk.buf], writes=[diagD.buf])

        ch_wssm = [P.chan("wssm%d" % q) for q in range(7)]

        mark0 = A.off
        wsl = [A.alloc("wsl%d" % i, [128, 8, 1024], BF16) for i in range(2)]
        ch_wsl = [P.chan("wsl%d" % i) for i in range(2)]
        siluT = A.alloc("siluT", [128, 8, 17], BF16)
        siluPx = A.alloc("siluPx", [128, 8, 128], BF16)
        siluSx = A.alloc("siluSx", [128, 8, 128], BF16)
        badaT = A.alloc("badaT", [128, 1024])
        gpostT = A.alloc("gpostT", [128, 1024])
        tmpT = A.alloc("tmpT", [128, 1024])
        gtmp = [A.alloc("gtmp%d" % i, [128, 1024]) for i in range(2)]
        mtmp = A.alloc("mtmp", [128, 8, 17])
        ch_bada = P.chan("bada")
        ch_gpost = P.chan("gpost")
        ch_gs = [P.chan("gsave%d" % i) for i in range(2)]

        P.op("act", lambda e: e.activation(out=siluT[:], in_=cpk[:, _CP["cT"]:_CP["cT"] + 136].rearrange("p (k s) -> p k s", k=8),
                                           func=AF.Silu), reads=[cpk.buf], writes=[siluT.buf])
        P.op("pool", lambda e: e.tensor_copy(out=siluPx[:], in_=bc(siluT[:, :, 0:1], [128, 8, 128])),
             reads=[siluT.buf], writes=[siluPx.buf])
        P.op("pool", lambda e: e.tensor_copy(out=siluSx[:].rearrange("p k (b l) -> p k b l", l=8),
                                             in_=bc(siluT[:, :, 1:17].unsqueeze(3), [128, 8, 16, 8])),
             reads=[siluT.buf], writes=[siluSx.buf])

        order_v = [1, 0, 2, 4, 3, 5]
        for vi, v in enumerate(order_v):
            s = vi % 2
            P.dma("pool", wsl[s][:], wada_d[v].rearrange("p (k c) -> p k c", k=8), ch_wsl[s], writes=[wsl[s].buf])
            if v in (0, 1, 3, 4):
                bi = vi % 2
                for ct in range(8):
                    for k in range(8):
                        P.op("pe", lambda e, s=s, ct=ct, k=k, bi=bi: e.matmul(
                            bank[bi][:, ct * 17:(ct + 1) * 17], lhsT=wsl[s][:, k, ct * 128:(ct + 1) * 128],
                            rhs=siluT[:, k, :], start=(k == 0), stop=(k == 7)),
                            reads=[wsl[s].buf, siluT.buf], writes=[pb[bi]])
                pv = bank[bi][:, 0:136].rearrange("p (c s) -> p c s", c=8)
                bb = bc(cpk[:, _CP["bada"] + v * 8:_CP["bada"] + v * 8 + 8].unsqueeze(2), [128, 8, 17])
                if v in (0, 3):
                    dst = Bv_m if v == 0 else Bv_f
                    P.op("dve", lambda e, pv=pv, bb=bb, dst=dst: e.tensor_tensor(out=dst[:], in0=pv, in1=bb, op=ALU.add),
                         reads=[pb[bi], cpk.buf], writes=[dst.buf])
                else:
                    dst = A_m if v == 1 else A_f
                    gn = "gpre" if v == 1 else "gfpre"
                    gb = bc(cpk[:, _CP[gn]:_CP[gn] + 8].unsqueeze(2), [128, 8, 17])
                    P.op("dve", lambda e, pv=pv, bb=bb: e.tensor_tensor(out=mtmp[:], in0=pv, in1=bb, op=ALU.add),
                         reads=[pb[bi], cpk.buf], writes=[mtmp.buf])
                    P.op("dve", lambda e, dst=dst, gb=gb: e.scalar_tensor_tensor(out=dst[:], in0=mtmp[:], scalar=1.0, in1=gb,
                                                                               op0=ALU.add, op1=ALU.mult),
                         reads=[mtmp.buf, cpk.buf], writes=[dst.buf])
            else:
                gi = 0 if v == 2 else 1
                P.dma("sp", badaT[:], bc(badar_d[0:1, v * 1024:(v + 1) * 1024], [128, 1024]), ch_bada, writes=[badaT.buf])
                P.dma("sp", gpostT[:], bc(gpostr_d[gi:gi + 1, :], [128, 1024]), ch_gpost, writes=[gpostT.buf])
                for wi, sx in enumerate((siluPx, siluSx)):
                    if v == 2:
                        dst = gmP if wi == 0 else gmS
                    else:
                        dst = gtmp[wi]
                    for half in range(2):
                        bi = 2 + (wi * 2 + half) % 4
                        for k in range(8):
                            P.op("pe", lambda e, s=s, k=k, bi=bi, half=half, sx=sx: e.matmul(
                                bank[bi], lhsT=sx[:, k, :], rhs=wsl[s][:, k, half * 512:(half + 1) * 512],
                                start=(k == 0), stop=(k == 7)),
                                reads=[wsl[s].buf, sx.buf], writes=[pb[bi]])
                        hs = slice(half * 512, (half + 1) * 512)
                        P.op("dve", lambda e, bi=bi, hs=hs: e.tensor_tensor(out=tmpT[:, hs], in0=bank[bi], in1=badaT[:, hs], op=ALU.add),
                             reads=[pb[bi], badaT.buf], writes=[tmpT.buf])
                        P.op("pool", lambda e, dst=dst, hs=hs: e.tensor_tensor(out=dst[:, hs], in0=tmpT[:, hs], in1=gpostT[:, hs], op=ALU.mult),
                             reads=[tmpT.buf, gpostT.buf], writes=[dst.buf])
                    if v == 5:
                        P.dma("sp", gsave_d[wi], dst[:], ch_gs[wi], reads=[dst.buf])

        P.barrier()
        A.off = mark0
        if stop == 0:
            P.emit(nc, final_chans=out_chans)
            return nc, P

        def rstd_pool(t, c0, n, N, eps=EPS):
            P.op("pool", lambda e: e.tensor_scalar(out=t[:, c0 + n:c0 + 2 * n], in0=t[:, c0:c0 + n], scalar1=1.0 / N, scalar2=eps,
                                                   op0=ALU.mult, op1=ALU.add), reads=[t.buf], writes=[t.buf])
            P.op("pool", lambda e: e.tensor_tensor(out=t[:, c0 + 2 * n:c0 + 3 * n], in0=t[:, c0 + n:c0 + 2 * n], in1=mhalf[:, 0:n], op=ALU.pow),
                 reads=[t.buf, mhalf.buf], writes=[t.buf])

        def stage_A_pre(xt, xn, st_):
            P.op("act", lambda e: e.activation(out=xn[:], in_=xt[:], func=AF.Square, accum_out=st_[:, 0:1]),
                 reads=[xt.buf], writes=[xn.buf, st_.buf])
            rstd_pool(st_, 0, 1, D)
            P.op("act", lambda e: e.activation(out=xn[:], in_=xt[:], func=AF.Identity, scale=st_[:, 2:3]),
                 reads=[xt.buf, st_.buf], writes=[xn.buf])

        def stage_A(xt, xn, Amod, Bmod, sample, hT_dst, hT_buf, tmp4k):
            stage_A_pre(xt, xn, stt)
            stage_A_post(xn, Amod, Bmod, sample, hT_dst, hT_buf, tmp4k)

        def stage_A_post(xn, Amod, Bmod, sample, hT_dst, hT_buf, tmp4k):
            cp(20)
            for k in range(8):
                P.op("pe", lambda e, k=k: e.transpose(bankbf[0][:, k * 128:(k + 1) * 128], xn[:, k * 128:(k + 1) * 128], ident_b),
                     reads=[xn.buf, cb.buf], writes=[pb[0]])
            cp(21)
            if not sample:
                import os
                for k in range(8):
                    kev = os.environ.get("KEVAC", "act")
                    if kev == "dve":
                        P.op("dve", lambda e, k=k: e.tensor_scalar(out=hT_dst[:, k, :], in0=bankbf[0][:, k * 128:(k + 1) * 128],
                                                                   scalar1=Amod[:, k, 0:1], scalar2=Bmod[:, k, 0:1],
                                                                   op0=ALU.mult, op1=ALU.add),
                             reads=[pb[0], Amod.buf, Bmod.buf], writes=[hT_buf], nowaw=True)
                    else:
                        P.op("act", lambda e, k=k: e.activation(out=hT_dst[:, k, :], in_=bankbf[0][:, k * 128:(k + 1) * 128],
                                                                func=AF.Identity, scale=Amod[:, k, 0:1], bias=Bmod[:, k, 0:1]),
                             reads=[pb[0], Amod.buf, Bmod.buf], writes=[hT_buf], nowaw=True)
            else:
                pv = bankbf[0].rearrange("p (k b l) -> p k b l", k=8, b=16)
                tv = tmp4k[:].rearrange("p (k b l) -> p k b l", k=8, b=16)
                P.op("dve", lambda e: e.tensor_tensor(out=tv, in0=pv, in1=bc(Amod[:, :, 1:17].unsqueeze(3), [128, 8, 16, 8]), op=ALU.mult),
                     reads=[pb[0], Amod.buf], writes=[tmp4k.buf])
                P.op("dve", lambda e: e.tensor_tensor(out=hT_dst.rearrange("p k (b l) -> p k b l", b=16), in0=tv,
                                                      in1=bc(Bmod[:, :, 1:17].unsqueeze(3), [128, 8, 16, 8]), op=ALU.add),
                     reads=[tmp4k.buf, Bmod.buf], writes=[hT_buf])

        mark1 = A.off
        Wsc = A.alloc("Wsc", [128, 8, 3072], BF16)
        ch_wsc = [P.chan("wsc%d" % j) for j in range(8)]
        wsc_bufs = [Buf("wsc%d" % j) for j in range(8)]
        for j in range(8):
            P.dma("pool", Wsc[:, :, j * 384:(j + 1) * 384], wsc_d[j].rearrange("p (k c) -> p k c", k=8), ch_wsc[j],
                  writes=[wsc_bufs[j]])
        wssm_bufs = [Buf("wssmq%d" % q) for q in range(7)]
        for q in range(6):
            P.dma("pool", Wssm[:, :, q * 512:(q + 1) * 512], wssm_d[q].rearrange("p (k c) -> p k c", k=8), ch_wssm[q],
                  writes=[wssm_bufs[q]])
        P.dma("pool", Wssm[:, :, 3072:3088], wdt_d[:, :].rearrange("p (k c) -> p k c", k=8), ch_wssm[6], writes=[wssm_bufs[6]])
        xin = [A.alloc("xin%d" % i, [128, 1024]) for i in range(2)]
        ch_xin = [P.chan("xin%d" % i) for i in range(2)]
        xn = A.alloc("xn", [128, 1024], BF16)
        hT = A.alloc("hT", [128, 8, 512], BF16)
        pxs = A.alloc("pxs", [128, 512])
        uext = [A.alloc("uext%d" % i, [128, 516]) for i in range(2)]
        ucv = [A.alloc("ucv%d" % i, [128, 512]) for i in range(2)]
        uhist = A.alloc("uhist", [128, 8, 2])
        scst = A.alloc("scst", [128, 8, 16, 2])
        nscs = A.alloc("nscs", [128, 8, 16, 2])
        tmp4k = A.alloc("tmp4k", [128, 1024])
        ch_scst = P.chan("scst")
        ch_nscp = P.chan("nscp")
        ch_nscs = P.chan("nscs")
        out_chans += [ch_nscp, ch_nscs]
        P.dma("sp", scst[:].rearrange("p j b r -> p (j b r)"), scst_d[:, :], ch_scst, writes=[scst.buf])
        P.op("pool", lambda e: e.memset(uhist[:], 0.0), writes=[uhist.buf])

        groups = [(g * 512, 512, False) for g in range(4)] + [(LP, 128, True)]
        tile_toks = [t0 + tt * 128 for (t0, ts, _) in groups for tt in range(ts // 128)]

        def load_x1a(n):
            if n < len(tile_toks):
                P.dma("sp", xin[n % 2][:], xall_d[tile_toks[n]:tile_toks[n] + 128, :], ch_xin[n % 2], writes=[xin[n % 2].buf])

        cp(10)
        hTs = [hT, A.alloc("hT1", [128, 8, 512], BF16)]
        xns = [xn] + [A.alloc("xn_%d" % i, [128, 1024], BF16) for i in range(1, 4)]
        stq = [A.alloc("stq%d" % i, [128, 4]) for i in range(4)]
        xc = [0]

        def stage_A1_pre(gi):
            t0_, ts_, smp_ = groups[gi]
            for tt in range(ts_ // 128):
                xs_ = xin[xc[0] % 2]
                load_x1a(xc[0] + 1)
                xc[0] += 1
                stage_A_pre(xs_, xns[tt], stq[tt])

        def stage_A1_post(gi):
            t0_, ts_, smp_ = groups[gi]
            for tt in range(ts_ // 128):
                stage_A_post(xns[tt], A_m, Bv_m, smp_, hTs[gi % 2][:, :, tt * 128:(tt + 1) * 128], hTs[gi % 2].buf, tmp4k)

        def stage_A1(gi):
            stage_A1_pre(gi)
            stage_A1_post(gi)

        load_x1a(0)
        stage_A1(0)
        for gi, (tok0, TS, sample) in enumerate(groups):
            hTc = hTs[gi % 2]
            for j in range(8):
                bs = (1, 2, 3) if j % 2 == 0 else (4, 5, 6)
                for part in range(3):
                    bi = bs[part]
                    for k in range(8):
                        P.op("pe", lambda e, j=j, part=part, bi=bi, k=k, TS=TS, hTc=hTc: e.matmul(
                            bank[bi][:, 0:TS], lhsT=Wsc[:, k, j * 384 + part * 128:j * 384 + (part + 1) * 128],
                            rhs=hTc[:, k, 0:TS], start=(k == 0), stop=(k == 7)),
                            reads=[wsc_bufs[j], hTc.buf], writes=[pb[bi]])
                if j == 2 and gi + 1 < len(groups):
                    stage_A1_pre(gi + 1)
                if j == 5 and gi + 1 < len(groups):
                    stage_A1_post(gi + 1)
                bcn, bxn, bbn = bs
                u = uext[j % 2]
                uc = ucv[j % 2]
                P.op("act", lambda e, bxn=bxn, TS=TS: e.activation(out=pxs[:, 0:TS], in_=bank[bxn][:, 0:TS], func=AF.Copy),
                     reads=[pb[bxn]], writes=[pxs.buf])
                if not sample:
                    P.op("pool", lambda e, u=u, j=j: e.tensor_copy(out=u[:, 0:2], in_=uhist[:, j, :]),
                         reads=[uhist.buf], writes=[u.buf])
                    P.op("dve", lambda e, u=u, bcn=bcn, TS=TS: e.tensor_tensor(out=u[:, 2:2 + TS], in0=bank[bcn][:, 0:TS], in1=pxs[:, 0:TS], op=ALU.mult),
                         reads=[pb[bcn], pxs.buf], writes=[u.buf], nowaw=True)
                    for r in range(3):
                        if r == 0:
                            P.op("dve", lambda e, u=u, uc=uc, j=j, TS=TS: e.tensor_scalar(
                                out=uc[:, 0:TS], in0=u[:, 0:TS], scalar1=cpc("scw", j * 3), scalar2=None, op0=ALU.mult),
                                reads=[u.buf, cpk.buf], writes=[uc.buf])
                        else:
                            P.op("dve", lambda e, u=u, uc=uc, j=j, r=r, TS=TS: e.scalar_tensor_tensor(
                                out=uc[:, 0:TS], in0=u[:, r:r + TS], scalar=cpc("scw", j * 3 + r), in1=uc[:, 0:TS],
                                op0=ALU.mult, op1=ALU.add),
                                reads=[u.buf, uc.buf, cpk.buf], writes=[uc.buf])
                    P.op("dve", lambda e, uc=uc, bbn=bbn, j=j, tok0=tok0, TS=TS: e.tensor_tensor(
                        out=y_scT[:, j, tok0:tok0 + TS], in0=bank[bbn][:, 0:TS], in1=uc[:, 0:TS], op=ALU.mult),
                        reads=[pb[bbn], uc.buf], writes=[y_scT.buf])
                    P.op("pool", lambda e, u=u, j=j, TS=TS: e.tensor_copy(out=uhist[:, j, :], in_=u[:, TS:TS + 2]),
                         reads=[u.buf], writes=[uhist.buf])
                else:
                    u3 = u[:, 0:160].rearrange("p (b c) -> p b c", b=16)
                    uc3 = uc[:, 0:128].rearrange("p (b l) -> p b l", b=16)
                    P.op("pool", lambda e, u3=u3, j=j: e.tensor_copy(out=u3[:, :, 0:2], in_=scst[:, j, :, :]),
                         reads=[scst.buf], writes=[u.buf])
                    P.op("dve", lambda e, u3=u3, bcn=bcn: e.tensor_tensor(
                        out=u3[:, :, 2:10], in0=bank[bcn][:, 0:128].rearrange("p (b l) -> p b l", b=16),
                        in1=pxs[:, 0:128].rearrange("p (b l) -> p b l", b=16), op=ALU.mult),
                        reads=[pb[bcn], pxs.buf], writes=[u.buf], nowaw=True)
                    for r in range(3):
                        if r == 0:
                            P.op("dve", lambda e, u3=u3, uc3=uc3, j=j: e.tensor_scalar(
                                out=uc3, in0=u3[:, :, 0:8], scalar1=cpc("scw", j * 3), scalar2=None, op0=ALU.mult),
                                reads=[u.buf, cpk.buf], writes=[uc.buf])
                        else:
                            P.op("dve", lambda e, u3=u3, uc3=uc3, j=j, r=r: e.scalar_tensor_tensor(
                                out=uc3, in0=u3[:, :, r:r + 8], scalar=cpc("scw", j * 3 + r), in1=uc3,
                                op0=ALU.mult, op1=ALU.add),
                                reads=[u.buf, uc.buf, cpk.buf], writes=[uc.buf])
                    P.op("dve", lambda e, uc=uc, bbn=bbn, j=j: e.tensor_tensor(
                        out=y_scT[:, j, LP:LP + 128], in0=bank[bbn][:, 0:128], in1=uc[:, 0:128], op=ALU.mult),
                        reads=[pb[bbn], uc.buf], writes=[y_scT.buf])
                    P.op("pool", lambda e, u3=u3, j=j: e.tensor_copy(out=nscs[:, j, :, :], in_=u3[:, :, 8:10]),
                         reads=[u.buf], writes=[nscs.buf])
            if tok0 == 3 * 512:
                P.dma("sp", nscp_d[:, :], uhist[:].rearrange("p j r -> p (j r)"), ch_nscp, reads=[uhist.buf])
        P.dma("sp", nscs_d[:, :], nscs[:].rearrange("p j b r -> p (j b r)"), ch_nscs, reads=[nscs.buf])

        P.barrier()
        A.off = mark1
        if stop == 1:
            P.emit(nc, final_chans=out_chans)
            return nc, P

        Wout = A.alloc("Wout", [128, 16, 1024], BF16)
        ch_wout = [P.chan("wout%d" % i) for i in range(4)]
        wout_bufs = [Buf("wout%d" % i) for i in range(4)]
        for i in range(4):
            P.dma("pool", Wout[:, i * 4:(i + 1) * 4, :], wout_d[i * 512:(i + 1) * 512, :].rearrange("(k p) n -> p k n", p=128),
                  ch_wout[i], writes=[wout_bufs[i]])
        NXS = 3
        xinb = [A.alloc("xinb%d" % i, [128, 1024]) for i in range(NXS)]
        xnb = A.alloc("xnb", [128, 1024], BF16)
        hTb = A.alloc("hTb", [128, 8, 128], BF16)
        ext = [A.alloc("ext%d" % i, [128, 4, 176]) for i in range(2)]
        cv = [A.alloc("cv%d" % i, [128, 4, 128]) for i in range(2)]
        xsT0 = A.alloc("xsT", [128, 8, 128], BF16)
        BT0 = A.alloc("BT", [128, 4, 128], BF16)
        CT0 = A.alloc("CT", [128, 4, 128], BF16)
        szT0 = A.alloc("szT", [128, 8, 128], BF16)
        sm16_0 = A.alloc("sm16", [128, 12, 16])
        dhl0 = A.alloc("dhl", [128, 2, 16], BF16)
        xdt = A.alloc("xdt", [128, 1024], BF16)
        xdtd = A.alloc("xdtd", [128, 1024], BF16)
        Btok = A.alloc("Btok", [128, 512], BF16)
        smk = A.alloc("smk", [128, 4, 128])
        Lsb = [A.alloc("Lsb%d" % i, [128, 4, 128]) for i in range(2)]
        Mt = A.alloc("Mt", [128, 16, 128], BF16)
        ST = A.alloc("ST", [128, 1024])
        STb = A.alloc("STb", [128, 1024], BF16)
        ycomb = A.alloc("ycomb", [128, 1024])
        ygn = A.alloc("ygn", [128, 1024], BF16)
        yssdT = A.alloc("yssdT", [128, 8, 128], BF16)
        xhist = A.alloc("xhist", [128, 16, 3])
        xcst = A.alloc("xcst", [128, 16, 16, 3])
        nxcs = A.alloc("nxcs", [128, 16, 16, 3])
        ss4 = A.alloc("ss4", [128, 12])
        sttb = A.alloc("sttb", [128, 8])
        ptmp = A.alloc("ptmp", [128, 4, 128])
        CTm = [A.alloc("CTm%d" % i, [128, 4, 128], BF16) for i in range(2)]
        Bm = A.alloc("Bm", [128, 512], BF16)
        decT = A.alloc("decT", [128, 8, 16])
        S0n = [A.alias(ext[0], [128, 8, 128], span=ext[0].nbytes + ext[1].nbytes),
               A.alias(cv[0], [128, 8, 128], span=cv[0].nbytes + cv[1].nbytes),
               A.alias(ST, [128, 8, 128])]
        assert ext[1].off == ext[0].off + ext[0].nbytes and cv[1].off == cv[0].off + cv[0].nbytes
        S0n_bufs = [[ext[0].buf, ext[1].buf], [cv[0].buf, cv[1].buf], [ST.buf]]
        dtAx = A.alias(Mt, [128, 2, 1024], BF16)
        S0b = A.alias(ygn, [128, 1024], BF16)
        S0T = A.alias(STb, [128, 1024], BF16)
        bsets = [dict(xsT=xsT0, BT=BT0, CT=CT0, szT=szT0, sm16=sm16_0, dhl=dhl0),
                 dict(xsT=A.alias(xcst, [128, 8, 128], BF16), szT=A.alias(nxcs, [128, 8, 128], BF16),
                      BT=A.alias(CTm[0], [128, 4, 128], BF16), CT=A.alias(CTm[1], [128, 4, 128], BF16),
                      sm16=A.alias(Bm, [128, 12, 16]), dhl=A.alias(decT, [128, 2, 16], BF16))]
        ch_xinb = [P.chan("xinb%d" % i) for i in range(NXS)]
        ch_xcst = P.chan("xcst")
        ch_s0 = [P.chan("s0n%d" % i) for i in range(3)]
        ch_nxcp = P.chan("nxcp")
        ch_nxcs = P.chan("nxcs")
        ch_nstp = P.chan("nstp")
        out_chans += [ch_nxcp, ch_nxcs, ch_nstp] + ch_s0
        P.dma("sp", xcst[:].rearrange("p j b r -> p (j b r)"), xcst_d[:, :], ch_xcst, writes=[xcst.buf])
        P.op("pool", lambda e: e.memset(xhist[:], 0.0), writes=[xhist.buf])
        for i in range(2):
            P.op("pool", lambda e, i=i: e.memset(CTm[i][:], 0.0), writes=[CTm[i].buf])

        pb1f = pb[1]
        pb1b = pb[1]
        b1tok = bankbf[5][:, 0:512]

        DTR, DT, DTA, TMP, NAC, EAC, DLT, DTE, DEC, EXPX = range(10)

        chunk_list = [(LP, True, -1)] + [(c * 128, False, c) for c in range(NCH)]

        def load_x1b(n):
            if n < len(chunk_list):
                t0 = chunk_list[n][0]
                P.dma("sp", xinb[n % NXS][:], xall_d[t0:t0 + 128, :], ch_xinb[n % NXS], writes=[xinb[n % NXS].buf])

        def conv_wide(eng, wt, c_, taps, shp, j0, ebuf):
            nd = len(shp)

            def wb(r):
                w = cvw[:, j0 * 4 + r:j0 * 4 + 16:4]
                w = w.unsqueeze(2) if nd == 3 else w.unsqueeze(2).unsqueeze(3)
                return bc(w, shp)

            bb = cvw[:, 64 + j0:64 + j0 + 4]
            bb = bc(bb.unsqueeze(2) if nd == 3 else bb.unsqueeze(2).unsqueeze(3), shp)
            cview = c_[:] if nd == 3 else c_[:].rearrange("p a (b l) -> p a b l", b=16)
            wview = wt[:].bitcast(F32).rearrange("p (a t) -> p a t", a=4) if wt.ap.dtype != F32 else wt[:]
            if nd == 4:
                wview = wview.rearrange("p a (b l) -> p a b l", b=16)
            for jj in range(4):
                P.op("act", lambda e, jj=jj: e.activation(out=cview[:, jj], in_=taps[0][:, jj], func=AF.Identity,
                                                          scale=cvw[:, (j0 + jj) * 4:(j0 + jj) * 4 + 1],
                                                          bias=cvw[:, 64 + j0 + jj:64 + j0 + jj + 1]),
                     reads=[ebuf, cvw.buf], writes=[c_.buf])
            for jj in range(4):
                for r in range(1, 4):
                    P.op(eng, lambda e, r=r, jj=jj: e.scalar_tensor_tensor(
                        out=cview[:, jj], in0=taps[r][:, jj], scalar=cvw[:, (j0 + jj) * 4 + r:(j0 + jj) * 4 + r + 1], in1=cview[:, jj],
                        op0=ALU.mult, op1=ALU.add),
                        reads=[ebuf, cvw.buf, c_.buf], writes=[c_.buf])

        def front(ci):
            tok0, sample, cidx = chunk_list[ci]
            S = bsets[0] if sample else bsets[cidx % 2]
            xsT, BT, CT, szT, sm16, dhl = S["xsT"], S["BT"], S["CT"], S["szT"], S["sm16"], S["dhl"]

            def s16(i):
                return sm16[:, i, :]

            xs_ = xinb[ci % NXS]
            stage_A(xs_, xnb, A_m, Bv_m, sample, hTb[:], hTb.buf, ycomb if sample else None)
            yield
            for k in range(8):
                P.op("pe", lambda e, k=k: e.matmul(bank[1][:, 0:16], lhsT=hTb[:, k, :], rhs=Wssm[:, k, 3072:3088],
                                                   start=(k == 0), stop=(k == 7)),
                     reads=[wssm_bufs[6], hTb.buf], writes=[pb1f])
            P.op("dve", lambda e: e.tensor_tensor(out=s16(DTR), in0=bank[1][:, 0:16], in1=cpk[:, _CP["dtb"]:_CP["dtb"] + 16], op=ALU.add),
                 reads=[pb1f, cpk.buf], writes=[sm16.buf])
            P.op("act", lambda e: e.activation(out=s16(EXPX), in_=s16(DTR), func=AF.Exp), reads=[sm16.buf], writes=[sm16.buf])
            P.op("act", lambda e: e.activation(out=s16(DT), in_=s16(EXPX), func=AF.Ln, bias=1.0), reads=[sm16.buf], writes=[sm16.buf])
            P.op("dve", lambda e: e.tensor_tensor(out=s16(DTA), in0=s16(DT), in1=Abc[:], op=ALU.mult),
                 reads=[sm16.buf, Abc.buf], writes=[sm16.buf])
            P.op("dve", lambda e: e.tensor_copy(out=dhl[:, 0, :], in_=s16(DTA)), reads=[sm16.buf], writes=[dhl.buf])
            P.op("dve", lambda e: e.tensor_tensor(out=s16(TMP), in0=s16(DTA), in1=dhl[:, 0, :], op=ALU.subtract),
                 reads=[sm16.buf, dhl.buf], writes=[sm16.buf])
            P.op("dve", lambda e: e.tensor_copy(out=dhl[:, 1, :], in_=s16(TMP)), reads=[sm16.buf], writes=[dhl.buf])
            trm = triBD_b if sample else tri_b
            onm = same_b if sample else ones_b[:]
            for hl in range(2):
                P.op("pe", lambda e, hl=hl, trm=trm: e.matmul(bank[1][:, 16:32], lhsT=trm, rhs=dhl[:, hl, :], start=(hl == 0), stop=(hl == 1)),
                     reads=[dhl.buf, cb.buf], writes=[pb1f])
            for hl in range(2):
                P.op("pe", lambda e, hl=hl, onm=onm: e.matmul(bank[1][:, 32:48], lhsT=onm, rhs=dhl[:, hl, :], start=(hl == 0), stop=(hl == 1)),
                     reads=[dhl.buf, cb.buf, ones_b.buf], writes=[pb1f])
            P.op("dve", lambda e: e.tensor_scalar(out=s16(NAC), in0=bank[1][:, 16:32], scalar1=-1.0, scalar2=None, op0=ALU.mult),
                 reads=[pb1f], writes=[sm16.buf])
            P.op("act", lambda e: e.activation(out=s16(EAC), in_=bank[1][:, 16:32], func=AF.Exp), reads=[pb1f], writes=[sm16.buf])
            P.op("dve", lambda e: e.tensor_tensor(out=s16(DLT), in0=bank[1][:, 32:48], in1=s16(NAC), op=ALU.add),
                 reads=[pb1f, sm16.buf], writes=[sm16.buf])
            P.op("act", lambda e: e.activation(out=s16(DTE), in_=s16(DLT), func=AF.Exp), reads=[sm16.buf], writes=[sm16.buf])
            if not sample:
                P.op("act", lambda e: e.activation(out=s16(DEC), in_=bank[1][:, 32:48], func=AF.Exp), reads=[pb1f], writes=[sm16.buf])
            yield

            for qi, q in enumerate((0, 1, 2, 3, 4, 5)):
                bi = 2 + qi % 2
                for jj in range(4):
                    for k in range(8):
                        P.op("pe", lambda e, q=q, jj=jj, k=k, bi=bi: e.matmul(
                            bank[bi][:, jj * 128:(jj + 1) * 128], lhsT=Wssm[:, k, q * 512 + jj * 128:q * 512 + (jj + 1) * 128],
                            rhs=hTb[:, k, :], start=(k == 0), stop=(k == 7)),
                            reads=[wssm_bufs[q], hTb.buf], writes=[pb[bi]])
                bv = bank[bi].rearrange("p (a t) -> p a t", a=4)
                if q < 2:
                    thv = ext[q % 2][:, :, 0:128]
                    P.op("act", lambda e, bv=bv, thv=thv: e.activation(out=thv, in_=bv, func=AF.Tanh, scale=0.5),
                         reads=[pb[bi]], writes=[ext[q % 2].buf])
                    P.op("dve", lambda e, q=q, bv=bv, thv=thv, szT=szT: e.scalar_tensor_tensor(
                        out=szT[:, q * 4:(q + 1) * 4, :], in0=thv, scalar=1.0, in1=bv, op0=ALU.add, op1=ALU.mult),
                        reads=[ext[q % 2].buf, pb[bi]], writes=[szT.buf])
                    yield
                    continue
                jq = q - 2
                j0 = jq * 4
                e_ = ext[jq % 2]
                c_ = cv[jq % 2]
                on_pool = False
                ceng = "pool" if on_pool else "dve"
                wt_ = ptmp if on_pool else xnb
                if not sample:
                    e3 = e_[:, :, 0:131]
                    P.op("pool", lambda e, e3=e3, j0=j0: e.tensor_copy(out=e3[:, :, 0:3], in_=xhist[:, j0:j0 + 4, :]),
                         reads=[xhist.buf], writes=[e_.buf])
                    P.op("act", lambda e, e3=e3, bv=bv: e.activation(out=e3[:, :, 3:131], in_=bv, func=AF.Copy),
                         reads=[pb[bi]], writes=[e_.buf], nowaw=True)
                    conv_wide(ceng, wt_, c_, [e3[:, :, r:r + 128] for r in range(4)], [128, 4, 128], j0, e_.buf)
                    P.op("pool", lambda e, e3=e3, j0=j0: e.tensor_copy(out=xhist[:, j0:j0 + 4, :], in_=e3[:, :, 128:131]),
                         reads=[e_.buf], writes=[xhist.buf])
                else:
                    e4 = e_[:].rearrange("p a (b c) -> p a b c", b=16)
                    P.op("pool", lambda e, e4=e4, j0=j0: e.tensor_copy(out=e4[:, :, :, 0:3], in_=xcst[:, j0:j0 + 4, :, :]),
                         reads=[xcst.buf], writes=[e_.buf])
                    P.op("act", lambda e, e4=e4, bi=bi: e.activation(
                        out=e4[:, :, :, 3:11], in_=bank[bi].rearrange("p (a b l) -> p a b l", a=4, b=16), func=AF.Copy),
                        reads=[pb[bi]], writes=[e_.buf], nowaw=True)
                    conv_wide(ceng, wt_, c_, [e4[:, :, :, r:r + 8] for r in range(4)], [128, 4, 16, 8], j0, e_.buf)
                    P.op("pool", lambda e, e4=e4, j0=j0: e.tensor_copy(out=nxcs[:, j0:j0 + 4, :, :], in_=e4[:, :, :, 8:11]),
                         reads=[e_.buf], writes=[nxcs.buf])
                if jq < 2:
                    dst, dbuf = xsT[:, j0:j0 + 4, :], xsT.buf
                elif jq == 2:
                    dst, dbuf = BT[:], BT.buf
                else:
                    dst, dbuf = CT[:], CT.buf
                thv = e_[:, :, 0:128]
                P.op("act", lambda e, c_=c_, thv=thv: e.activation(out=thv, in_=c_[:], func=AF.Tanh),
                     reads=[c_.buf], writes=[e_.buf])
                P.op("dve", lambda e, c_=c_, dst=dst, thv=thv: e.scalar_tensor_tensor(
                    out=dst, in0=thv, scalar=1.0, in1=c_[:], op0=ALU.add, op1=ALU.mult),
                    reads=[e_.buf, c_.buf], writes=[dbuf])
                yield

        def back(ci):
            tok0, sample, cidx = chunk_list[ci]
            first = (not sample) and cidx == 0
            S = bsets[0] if sample else bsets[cidx % 2]
            xsT, BT, CT, szT, sm16, dhl = S["xsT"], S["BT"], S["CT"], S["szT"], S["sm16"], S["dhl"]

            def s16(i):
                return sm16[:, i, :]

            xs_ = xinb[ci % NXS]
            trm = triBD_b if sample else tri_b
            for j in range(8):
                P.op("pe", lambda e, j=j: e.transpose(bankbf[4][:, j * 128:(j + 1) * 128], xsT[:, j, :], ident_b),
                     reads=[xsT.buf, cb.buf], writes=[pb[4]])
            if ci == 1:
                cp(70)
            for g in range(4):
                P.op("pe", lambda e, g=g: e.transpose(b1tok[:, g * 128:(g + 1) * 128], BT[:, g, :], ident_b),
                     reads=[BT.buf, cb.buf], writes=[pb[5]])
            if ci == 1:
                cp(71)
            P.op("dve", lambda e: e.tensor_tensor(out=xdt[:].rearrange("p (h q) -> p h q", h=16),
                                                  in0=bankbf[4].rearrange("p (h q) -> p h q", h=16),
                                                  in1=bc(s16(DT).unsqueeze(2), [128, 16, 64]), op=ALU.mult),
                 reads=[pb[4], sm16.buf], writes=[xdt.buf])
            P.op("pool", lambda e: e.tensor_tensor(out=xdtd[:].rearrange("p (h q) -> p h q", h=16),
                                                   in0=xdt[:].rearrange("p (h q) -> p h q", h=16),
                                                   in1=bc(s16(DTE).unsqueeze(2), [128, 16, 64]), op=ALU.mult),
                 reads=[xdt.buf, sm16.buf], writes=[xdtd.buf])
            P.op("act", lambda e: e.activation(out=Btok[:], in_=b1tok, func=AF.Copy), reads=[pb[5]], writes=[Btok.buf])
            yield
            for g in range(4):
                P.op("pe", lambda e, g=g: e.matmul(bank[5][:, g * 128:(g + 1) * 128], lhsT=BT[:, g, :], rhs=CT[:, g, :], start=True, stop=True),
                     reads=[BT.buf, CT.buf], writes=[pb[5]])
            P.op("act", lambda e: e.activation(out=smk[:], in_=bank[5].rearrange("p (g l) -> p g l", g=4), func=AF.Copy),
                 reads=[pb[5]], writes=[smk.buf])
            if (not sample) and (not first):
                for g in range(4):
                    bi = 6 + g // 2
                    P.op("pe", lambda e, g=g, bi=bi: e.matmul(bank[bi][:, (g % 2) * 256:(g % 2) * 256 + 256], lhsT=CT[:, g, :],
                                                              rhs=STb[:, g * 256:(g + 1) * 256], start=True, stop=True),
                         reads=[CT.buf, STb.buf], writes=[pb[bi]])
            yield
            ngm = negmBD_b if sample else negm_b
            ntm = ntri[:, 1, :] if sample else ntri[:, 0, :]
            for g in range(4):
                bi = 4 + g % 2
                for r in range(4):
                    h = 4 * g + r
                    osl = bank[bi][:, r * 128:(r + 1) * 128]
                    for hl in range(2):
                        P.op("pe", lambda e, osl=osl, hl=hl, h=h, trm=trm: e.matmul(
                            osl, lhsT=bc(dhl[:, hl, h:h + 1], [128, 128]), rhs=trm, start=(hl == 0), stop=False),
                            reads=[dhl.buf, cb.buf], writes=[pb[bi]])
                    for hl in range(2):
                        P.op("pe", lambda e, osl=osl, hl=hl, h=h, ntm=ntm: e.matmul(
                            osl, lhsT=ntm, rhs=bc(dhl[:, hl, h:h + 1], [128, 128]), start=False, stop=False),
                            reads=[dhl.buf, ntri.buf], writes=[pb[bi]])
                    P.op("pe", lambda e, osl=osl, ngm=ngm: e.matmul(osl, lhsT=ident_b, rhs=ngm, start=False, stop=True),
                         reads=[cb.buf], writes=[pb[bi]])
                L_ = Lsb[g % 2]
                P.op("act", lambda e, L_=L_, bi=bi: e.activation(out=L_[:], in_=bank[bi].rearrange("p (r l) -> p r l", r=4), func=AF.Exp),
                     reads=[pb[bi]], writes=[L_.buf])
                P.op("dve", lambda e, L_=L_, g=g: e.tensor_tensor(out=Mt[:, 4 * g:4 * g + 4, :], in0=L_[:],
                                                                   in1=bc(smk[:, g, :].unsqueeze(1), [128, 4, 128]), op=ALU.mult),
                     reads=[L_.buf, smk.buf], writes=[Mt.buf])
                yield
            for j in range(8):
                bi = 4 + j // 4
                col = (j % 4) * 128
                P.op("pe", lambda e, j=j, bi=bi, col=col: e.matmul(bank[bi][:, col:col + 128], lhsT=xsT[:, j, :], rhs=diagD[:, j, :],
                                                                   start=True, stop=False),
                     reads=[xsT.buf, diagD.buf], writes=[pb[bi]])
                for hh in range(2):
                    h = 2 * j + hh
                    P.op("pe", lambda e, h=h, bi=bi, col=col, hh=hh: e.matmul(
                        bank[bi][:, col + hh * 64:col + (hh + 1) * 64], lhsT=Mt[:, h, :], rhs=xdt[:, h * 64:(h + 1) * 64],
                        start=False, stop=(hh == 1)),
                        reads=[Mt.buf, xdt.buf], writes=[pb[bi]])
            yd = pair(4)
            yo = pair(6)
            if first:
                P.op("dve", lambda e: e.tensor_copy(out=ycomb[:], in_=yd), reads=[pb[4], pb[5]], writes=[ycomb.buf])
            elif not sample:
                P.op("dve", lambda e: e.tensor_tensor(out=ycomb[:].rearrange("p (h q) -> p h q", h=16),
                                                      in0=yo.rearrange("p (h q) -> p h q", h=16),
                                                      in1=bc(s16(EAC).unsqueeze(2), [128, 16, 64]), op=ALU.mult),
                     reads=[pb[6], pb[7], sm16.buf], writes=[ycomb.buf])
                P.op("dve", lambda e: e.tensor_tensor(out=ycomb[:], in0=ycomb[:], in1=yd, op=ALU.add),
                     reads=[ycomb.buf, pb[4], pb[5]], writes=[ycomb.buf])
            else:
                for hl in range(2):
                    P.op("dve", lambda e, hl=hl: e.tensor_copy(out=dtAx[:, hl, :].rearrange("p (h q) -> p h q", h=16),
                                                               in_=bc(dhl[:, hl, :].unsqueeze(2), [128, 16, 64])),
                         reads=[dhl.buf, Mt.buf], writes=[dtAx.buf])
                selb = A.alias(Bm, [128, 16], BF16)
                P.op("dve", lambda e: e.tensor_copy(out=selb[:], in_=cpk[:, _CP["seqsel"]:_CP["seqsel"] + 16]),
                     reads=[cpk.buf], writes=[selb.buf])
                for jp in range(8):
                    for hl in range(2):
                        P.op("pe", lambda e, jp=jp, hl=hl: e.matmul(bank[1][:, 64 + jp * 16:64 + (jp + 1) * 16],
                                                                    lhsT=dtAx[:, hl, jp * 128:(jp + 1) * 128], rhs=selb[:],
                                                                    start=(hl == 0), stop=(hl == 1)),
                             reads=[dtAx.buf, selb.buf], writes=[pb1b])
                P.op("act", lambda e: e.activation(out=decT[:], in_=bank[1][:, 64:192].rearrange("p (j b) -> p j b", j=8), func=AF.Exp),
                     reads=[pb1b], writes=[decT.buf])
                for bi in (6, 7):
                    P.op("pe", lambda e, bi=bi: e.matmul(bank[bi], lhsT=zero_b[:], rhs=cb[:, 0:512], start=True, stop=False),
                         reads=[zero_b.buf, cb.buf], writes=[pb[bi]])

                def load_s0(b):
                    if b < NSEQ:
                        P.dma("sp", S0n[b % 3][:], sst_d[b].rearrange("(j p) n -> p j n", p=128), ch_s0[b % 3], writes=S0n_bufs[b % 3])

                load_s0(0)
                load_s0(1)
                for b in range(NSEQ):
                    sn_ = S0n[b % 3]
                    snb = S0n_bufs[b % 3]
                    cm = CTm[b % 2]
                    if b >= 1:
                        P.dma("sp", nsts_d[b - 1].rearrange("(j p) n -> p j n", p=128), S0n[(b - 1) % 3][:], ch_s0[(b - 1) % 3],
                              reads=S0n_bufs[(b - 1) % 3])
                    load_s0(b + 2)
                    if b >= 2:
                        P.op("pool", lambda e, cm=cm, b=b: e.memset(cm[:, :, (b - 2) * 8:(b - 1) * 8], 0.0), writes=[cm.buf])
                    P.op("pool", lambda e, cm=cm, b=b: e.tensor_copy(out=cm[:, :, b * 8:(b + 1) * 8], in_=CT[:, :, b * 8:(b + 1) * 8]),
                         reads=[CT.buf], writes=[cm.buf])
                    P.op("act", lambda e, sn_=sn_: e.activation(out=S0b[:], in_=sn_[:].rearrange("p j n -> p (j n)"), func=AF.Copy),
                         reads=snb, writes=[S0b.buf])
                    for jp in range(8):
                        P.op("pe", lambda e, jp=jp: e.transpose(bankbf[0][:, jp * 128:(jp + 1) * 128], S0b[:, jp * 128:(jp + 1) * 128], ident_b),
                             reads=[S0b.buf, cb.buf], writes=[pb[0]])
                    P.op("act", lambda e: e.activation(out=S0T[:], in_=bankbf[0], func=AF.Copy), reads=[pb[0]], writes=[S0T.buf])
                    for g in range(4):
                        bi = 6 + g // 2
                        P.op("pe", lambda e, g=g, bi=bi, cm=cm, b=b: e.matmul(
                            bank[bi][:, (g % 2) * 256:(g % 2) * 256 + 256], lhsT=cm[:, g, :], rhs=S0T[:, g * 256:(g + 1) * 256],
                            start=False, stop=(b == NSEQ - 1 and g % 2 == 1)),
                            reads=[cm.buf, S0T.buf], writes=[pb[bi]])
                    P.op("dve", lambda e, b=b: e.tensor_scalar(out=Bm[:], in0=Btok[:], scalar1=cpc("seqsel", b), scalar2=None, op0=ALU.mult),
                         reads=[Btok.buf, cpk.buf, selb.buf], writes=[Bm.buf])
                    for jp in range(8):
                        bi = 2 + jp // 4
                        gq = jp // 2
                        P.op("pe", lambda e, jp=jp, bi=bi, gq=gq: e.matmul(
                            bank[bi][:, (jp % 4) * 128:(jp % 4 + 1) * 128], lhsT=xdtd[:, jp * 128:(jp + 1) * 128],
                            rhs=Bm[:, gq * 128:(gq + 1) * 128], start=True, stop=True),
                            reads=[xdtd.buf, Bm.buf], writes=[pb[bi]])
                    for jp in range(8):
                        bi = 2 + jp // 4
                        P.op("dve", lambda e, jp=jp, bi=bi, sn_=sn_, b=b: e.scalar_tensor_tensor(
                            out=sn_[:, jp, :], in0=sn_[:, jp, :], scalar=decT[:, jp, b:b + 1],
                            in1=bank[bi][:, (jp % 4) * 128:(jp % 4 + 1) * 128], op0=ALU.mult, op1=ALU.add),
                            reads=snb + [decT.buf, pb[bi]], writes=snb)
                P.dma("sp", nsts_d[NSEQ - 1].rearrange("(j p) n -> p j n", p=128), S0n[(NSEQ - 1) % 3][:], ch_s0[(NSEQ - 1) % 3],
                      reads=S0n_bufs[(NSEQ - 1) % 3])
                P.op("dve", lambda e: e.tensor_tensor(out=ycomb[:].rearrange("p (h q) -> p h q", h=16),
                                                      in0=yo.rearrange("p (h q) -> p h q", h=16),
                                                      in1=bc(s16(EAC).unsqueeze(2), [128, 16, 64]), op=ALU.mult),
                     reads=[pb[6], pb[7], sm16.buf], writes=[ycomb.buf])
                P.op("dve", lambda e: e.tensor_tensor(out=ycomb[:], in0=ycomb[:], in1=yd, op=ALU.add),
                     reads=[ycomb.buf, pb[4], pb[5]], writes=[ycomb.buf])
            yield
            for j in range(8):
                P.op("pe", lambda e, j=j: e.transpose(bankbf[6][:, j * 128:(j + 1) * 128], szT[:, j, :], ident_b),
                     reads=[szT.buf, cb.buf], writes=[pb[6]])
            P.op("dve", lambda e: e.tensor_tensor(out=ycomb[:], in0=ycomb[:], in1=bankbf[6], op=ALU.mult),
                 reads=[ycomb.buf, pb[6]], writes=[ycomb.buf])
            for g in range(4):
                P.op("act", lambda e, g=g: e.activation(out=ygn[:, g * 256:(g + 1) * 256], in_=ycomb[:, g * 256:(g + 1) * 256],
                                                        func=AF.Square, accum_out=ss4[:, g:g + 1]),
                     reads=[ycomb.buf], writes=[ygn.buf, ss4.buf])
            rstd_pool(ss4, 0, 4, 256, eps=4.0 * EPS)
            P.op("pool", lambda e: e.tensor_tensor(out=ygn[:].rearrange("p (g q) -> p g q", g=4),
                                                   in0=ycomb[:].rearrange("p (g q) -> p g q", g=4),
                                                   in1=bc(ss4[:, 8:12].unsqueeze(2), [128, 4, 256]), op=ALU.mult),
                 reads=[ycomb.buf, ss4.buf], writes=[ygn.buf])
            yield
            if not sample:
                st_ = pair(4)
                for g in range(4):
                    bi = 4 + g // 2
                    P.op("pe", lambda e, g=g, bi=bi: e.matmul(bank[bi][:, (g % 2) * 256:(g % 2) * 256 + 256],
                                                              lhsT=Btok[:, g * 128:(g + 1) * 128], rhs=xdtd[:, g * 256:(g + 1) * 256],
                                                              start=True, stop=True),
                         reads=[Btok.buf, xdtd.buf], writes=[pb[bi]])
                if first:
                    P.op("dve", lambda e: e.tensor_copy(out=ST[:], in_=st_), reads=[pb[4], pb[5]], writes=[ST.buf])
                else:
                    P.op("pool", lambda e: e.tensor_tensor(out=ST[:].rearrange("p (h q) -> p h q", h=16),
                                                           in0=ST[:].rearrange("p (h q) -> p h q", h=16),
                                                           in1=bc(s16(DEC).unsqueeze(2), [128, 16, 64]), op=ALU.mult),
                         reads=[ST.buf, sm16.buf], writes=[ST.buf])
                    P.op("dve", lambda e: e.tensor_tensor(out=ST[:], in0=ST[:], in1=st_, op=ALU.add),
                         reads=[ST.buf, pb[4], pb[5]], writes=[ST.buf])
                if cidx < NCH - 1:
                    P.op("act", lambda e: e.activation(out=STb[:], in_=ST[:], func=AF.Copy), reads=[ST.buf], writes=[STb.buf])
            for j in range(8):
                P.op("pe", lambda e, j=j: e.transpose(bankbf[7][:, j * 128:(j + 1) * 128], ygn[:, j * 128:(j + 1) * 128], ident_b),
                     reads=[ygn.buf, cb.buf], writes=[pb[7]])
            for j in range(8):
                P.op("act", lambda e, j=j: e.activation(out=yssdT[:, j, :], in_=bankbf[7][:, j * 128:(j + 1) * 128],
                                                        func=AF.Identity, scale=cpc("ng", j)),
                     reads=[pb[7], cpk.buf], writes=[yssdT.buf])
            yield
            for half in range(2):
                bi = 6 + half
                for k in range(16):
                    if k < 8:
                        lh = y_scT[:, k, tok0:tok0 + 128]
                        rb = [y_scT.buf]
                    else:
                        lh = yssdT[:, k - 8, :]
                        rb = [yssdT.buf]
                    P.op("pe", lambda e, lh=lh, k=k, half=half, bi=bi: e.matmul(
                        bank[bi], lhsT=lh, rhs=Wout[:, k, half * 512:(half + 1) * 512], start=(k == 0), stop=(k == 15)),
                        reads=rb + [wout_bufs[k // 4]], writes=[pb[bi]])
                yield
            mix = pair(6)
            P.op("act", lambda e: e.activation(out=ygn[:], in_=mix, func=AF.Square, accum_out=sttb[:, 4:5]),
                 reads=[pb[6], pb[7]], writes=[ygn.buf, sttb.buf])
            rstd_pool(sttb, 4, 1, D)
            gm = gmS if sample else gmP
            P.op("dve", lambda e, gm=gm: e.tensor_tensor(out=ycomb[:], in0=mix, in1=gm[:], op=ALU.mult),
                 reads=[pb[6], pb[7], gm.buf], writes=[ycomb.buf])
            P.op("dve", lambda e, xs_=xs_: e.scalar_tensor_tensor(out=xs_[:], in0=ycomb[:], scalar=sttb[:, 6:7], in1=xs_[:],
                                                                  op0=ALU.mult, op1=ALU.add),
                 reads=[ycomb.buf, sttb.buf, xs_.buf], writes=[xs_.buf])
            P.dma("sp", x1s_d[tok0:tok0 + 128, :], xs_[:], ch_xinb[ci % NXS], reads=[xs_.buf])
            load_x1b(ci + NXS)
            if sample:
                P.dma("sp", nxcs_d[:, :], nxcs[:].rearrange("p j b r -> p (j b r)"), ch_nxcs, reads=[nxcs.buf])
            if (not sample) and cidx == NCH - 1:
                P.dma("sp", nxcp_d[:, :], xhist[:].rearrange("p j r -> p (j r)"), ch_nxcp, reads=[xhist.buf])
                for jp in range(8):
                    bi = 4 + jp // 4
                    P.op("pe", lambda e, jp=jp, bi=bi: e.transpose(bank[bi][:, (jp % 4) * 128:(jp % 4 + 1) * 128],
                                                                   ST[:, jp * 128:(jp + 1) * 128], ident_f),
                         reads=[ST.buf, cpk.buf], writes=[pb[bi]])
                P.op("dve", lambda e: e.tensor_copy(out=ycomb[:], in_=pair(4)), reads=[pb[4], pb[5]], writes=[ycomb.buf])
                P.dma("sp", nstp_d[:, :].rearrange("(j p) n -> p j n", p=128), ycomb[:].rearrange("p (j n) -> p j n", j=8),
                      ch_nstp, reads=[ycomb.buf])
            yield

        def drain(g, cpbase=None):
            for i, _ in enumerate(g):
                if cpbase is not None:
                    cp(cpbase + i)

        def interleave(ga, gb, pattern="BFBBFBBFBBFBFBFBBFBF"):
            gens = {"B": ga, "F": gb}
            for ch in pattern:
                g = gens.get(ch)
                if g is None:
                    continue
                try:
                    next(g)
                except StopIteration:
                    gens[ch] = None
            for g in gens.values():
                if g is not None:
                    for _ in g:
                        pass

        for n in range(NXS):
            load_x1b(n)
        drain(front(0))
        cp(50)
        drain(back(0))
        cp(51)
        drain(front(1))
        cp(52)
        import os
        noil = os.environ.get("KNOIL", "0")
        for ci in range(1, len(chunk_list)):
            if noil == "1":
                drain(back(ci), 60 if ci == 1 else None)
                if ci == 1:
                    cp(55)
                if ci + 1 < len(chunk_list):
                    drain(front(ci + 1))
            else:
                interleave(back(ci), front(ci + 1) if ci + 1 < len(chunk_list) else None)
            if ci == 1:
                cp(53)
            if ci == 2:
                cp(54)

        P.barrier()
        A.off = mark_p2
        if stop == 2:
            P.emit(nc, final_chans=out_chans)
            return nc, P

        Wg = A.alloc("Wg", [128, 8, HID], BF16)
        Wu = A.alloc("Wu", [128, 8, HID], BF16)
        Wd = A.alloc("Wd", [128, NHT, 1024], BF16)
        ch_wg = [P.chan("wg%d" % i) for i in range(4)]
        ch_wu = [P.chan("wu%d" % i) for i in range(4)]
        ch_wd = [P.chan("wd%d" % i) for i in range(4)]
        wg_b = [Buf("wg%d" % i) for i in range(4)]
        wu_b = [Buf("wu%d" % i) for i in range(4)]
        wd_b = [Buf("wd%d" % i) for i in range(4)]
        wd_split = [(0, 6), (6, 12), (12, 17), (17, 22)]
        for i in range(4):
            P.dma("pool", Wg[:, :, i * 704:(i + 1) * 704], wg_d[i].rearrange("p (k c) -> p k c", k=8), ch_wg[i], writes=[wg_b[i]])
            P.dma("pool", Wu[:, :, i * 704:(i + 1) * 704], wu_d[i].rearrange("p (k c) -> p k c", k=8), ch_wu[i], writes=[wu_b[i]])
        for i, (a0, a1) in enumerate(wd_split):
            P.dma("pool", Wd[:, a0:a1, :], wd_d[a0 * 128:a1 * 128, :].rearrange("(i p) n -> p i n", p=128), ch_wd[i], writes=[wd_b[i]])

        def wd_buf(i):
            for n, (a0, a1) in enumerate(wd_split):
                if a0 <= i < a1:
                    return wd_b[n]

        gfP = A.alloc("gfP", [128, 1024])
        gfS = A.alloc("gfS", [128, 1024])
        ch_gf = [P.chan("gfload%d" % i) for i in range(2)]
        P.dma("sp", gfP[:], gsave_d[0], ch_gf[0], writes=[gfP.buf])
        P.dma("sp", gfS[:], gsave_d[1], ch_gf[1], writes=[gfS.buf])
        xf = [[A.alloc("xf%d_%d" % (s, t), [128, 1024]) for t in range(2)] for s in range(2)]
        ch_xf = [[P.chan("xf%d_%d" % (s, t)) for t in range(2)] for s in range(2)]
        out_chans += [c for r in ch_xf for c in r]
        xn2 = A.alloc("xn2", [128, 1024], BF16)
        h2T = A.alloc("h2T", [128, 8, 256], BF16)
        aT = A.alloc("aT", [128, NHT, 256], BF16)
        sg = [A.alloc("sg%d" % i, [128, 256]) for i in range(2)]
        ftmp = A.alloc("ftmp", [128, 1024])

        tiles = [(t * 256, 256, False) for t in range(8)] + [(LP, 128, True)]

        def load_x2(n):
            if n < len(tiles):
                t0, ts, _ = tiles[n]
                for tt in range(ts // 128):
                    P.dma("sp", xf[n % 2][tt][:], x1s_d[t0 + tt * 128:t0 + (tt + 1) * 128, :], ch_xf[n % 2][tt], writes=[xf[n % 2][tt].buf])

        h2Ts = [h2T, A.alloc("h2T1", [128, 8, 256], BF16)]

        xn2s = [xn2, A.alloc("xn2b", [128, 1024], BF16)]
        stq2 = [A.alloc("stq2_%d" % i, [128, 4]) for i in range(2)]

        def stage_A2_pre(ti):
            t0_, ts_, smp_ = tiles[ti]
            for tt in range(ts_ // 128):
                stage_A_pre(xf[ti % 2][tt], xn2s[tt], stq2[tt])

        def stage_A2_post(ti):
            t0_, ts_, smp_ = tiles[ti]
            for tt in range(ts_ // 128):
                stage_A_post(xn2s[tt], A_f, Bv_f, smp_, h2Ts[ti % 2][:, :, tt * 128:(tt + 1) * 128], h2Ts[ti % 2].buf, ftmp)

        def stage_A2(ti):
            stage_A2_pre(ti)
            stage_A2_post(ti)

        load_x2(0)
        stage_A2(0)
        for ti, (tok0, TS, sample) in enumerate(tiles):
            nt = TS // 128
            sl = ti % 2
            h2c = h2Ts[ti % 2]
            load_x2(ti + 1)
            for i in range(NHT):
                bi = 1 + i % 3
                wq = i * 128 // 704
                wq2 = (i * 128 + 127) // 704
                for k in range(8):
                    P.op("pe", lambda e, i=i, k=k, bi=bi, TS=TS, h2c=h2c: e.matmul(bank[bi][:, 0:TS], lhsT=Wg[:, k, i * 128:(i + 1) * 128],
                                                                                   rhs=h2c[:, k, 0:TS], start=(k == 0), stop=(k == 7)),
                         reads=[wg_b[wq], wg_b[wq2], h2c.buf], writes=[pb[bi]])
                for k in range(8):
                    P.op("pe", lambda e, i=i, k=k, bi=bi, TS=TS, h2c=h2c: e.matmul(bank[bi][:, 256:256 + TS], lhsT=Wu[:, k, i * 128:(i + 1) * 128],
                                                                                   rhs=h2c[:, k, 0:TS], start=(k == 0), stop=(k == 7)),
                         reads=[wu_b[wq], wu_b[wq2], h2c.buf], writes=[pb[bi]])
                s_ = sg[i % 2]
                P.op("act", lambda e, s_=s_, bi=bi, TS=TS: e.activation(out=s_[:, 0:TS], in_=bank[bi][:, 0:TS], func=AF.Silu),
                     reads=[pb[bi]], writes=[s_.buf])
                P.op("dve", lambda e, s_=s_, bi=bi, i=i, TS=TS: e.tensor_tensor(out=aT[:, i, 0:TS], in0=s_[:, 0:TS],
                                                                                 in1=bank[bi][:, 256:256 + TS], op=ALU.mult),
                     reads=[s_.buf, pb[bi]], writes=[aT.buf])
                if i == 8 and ti + 1 < len(tiles):
                    stage_A2_pre(ti + 1)
            for tt in range(nt):
                for half in range(2):
                    bi = 4 + tt * 2 + half
                    for i in range(NHT):
                        P.op("pe", lambda e, i=i, tt=tt, half=half, bi=bi: e.matmul(
                            bank[bi], lhsT=aT[:, i, tt * 128:(tt + 1) * 128], rhs=Wd[:, i, half * 512:(half + 1) * 512],
                            start=(i == 0), stop=(i == NHT - 1)),
                            reads=[aT.buf, wd_buf(i)], writes=[pb[bi]])
            if ti + 1 < len(tiles):
                stage_A2_post(ti + 1)
            for tt in range(nt):
                fp_ = pair(4 + tt * 2)
                pbs = [pb[4 + tt * 2], pb[5 + tt * 2]]
                xt = xf[sl][tt]
                P.op("act", lambda e, fp_=fp_: e.activation(out=xn2[:], in_=fp_, func=AF.Square, accum_out=stt[:, 4:5]),
                     reads=pbs, writes=[xn2.buf, stt.buf])
                rstd_pool(stt, 4, 1, D)
                gf = gfS if sample else gfP
                P.op("dve", lambda e, fp_=fp_, gf=gf: e.tensor_tensor(out=ftmp[:], in0=fp_, in1=gf[:], op=ALU.mult),
                     reads=pbs + [gf.buf], writes=[ftmp.buf])
                P.op("dve", lambda e, xt=xt: e.scalar_tensor_tensor(out=xt[:], in0=ftmp[:], scalar=stt[:, 6:7], in1=xt[:],
                                                                    op0=ALU.mult, op1=ALU.add),
                     reads=[ftmp.buf, stt.buf, xt.buf], writes=[xt.buf])
                P.dma("sp", y_d[tok0 + tt * 128:tok0 + (tt + 1) * 128, :], xt[:], ch_xf[sl][tt], reads=[xt.buf])

        P.emit(nc, final_chans=out_chans)
    return nc, P


def _fm(v, ntile):
    return np.ascontiguousarray(np.asarray(v, np.float32).reshape(ntile, 128).T)


def _host_consts():
    idx = np.arange(128)
    ident = np.eye(128, dtype=np.float32)
    tri = (idx[:, None] <= idx[None, :]).astype(np.float32)
    negm = np.where(idx[None, :] < idx[:, None], NEG, 0.0).astype(np.float32)
    same = (idx[:, None] // LS == idx[None, :] // LS).astype(np.float32)
    triBD = tri * same
    negmBD = np.where(triBD > 0, 0.0, NEG).astype(np.float32)
    seqsel = (idx[:, None] // LS == np.arange(NSEQ)[None, :]).astype(np.float32)
    return ident, tri, negm, triBD, negmBD, same, seqsel


_CACHE = {}


def kernel(x_prompt, x_sample, c_prompt, c_sample, state_sc_conv, state_ssm_conv, state_ssm,
           w_ada, b_ada, g_mix_pre, g_mix_post, g_ffn_pre, g_ffn_post, w_in, sc_conv_w,
           ssm_conv_w, ssm_conv_b, dt_bias, a_log, d_skip, ssm_norm_g, w_out, w_gate, w_up, w_down):
    f32 = np.float32
    x_prompt = np.asarray(x_prompt, f32)
    x_sample = np.asarray(x_sample, f32)
    w_in_ = np.asarray(w_in, f32)[0]
    w_ada_ = np.asarray(w_ada, f32)[0]
    wada = np.ascontiguousarray(w_ada_.reshape(8, 128, 6, 1024).transpose(2, 1, 0, 3).reshape(6, 128, 8192))
    sc = w_in_[:, 0:3072].reshape(8, 128, 3, 8, 128)
    wsc = np.ascontiguousarray(sc[:, :, [1, 2, 0], :, :].transpose(3, 1, 0, 2, 4).reshape(8, 128, 8 * 384))
    ssm = w_in_[:, 3072:6144].reshape(8, 128, 6, 512)
    wssm = np.ascontiguousarray(ssm.transpose(2, 1, 0, 3).reshape(6, 128, 8 * 512))
    wdt = np.ascontiguousarray(w_in_[:, 6144:6160].reshape(8, 128, 16).transpose(1, 0, 2).reshape(128, 128))
    wout = np.ascontiguousarray(np.asarray(w_out, f32)[0])
    wg = np.ascontiguousarray(np.asarray(w_gate, f32)[0].reshape(8, 128, 4, 704).transpose(2, 1, 0, 3).reshape(4, 128, 8 * 704))
    wu = np.ascontiguousarray(np.asarray(w_up, f32)[0].reshape(8, 128, 4, 704).transpose(2, 1, 0, 3).reshape(4, 128, 8 * 704))
    wd = np.ascontiguousarray(np.asarray(w_down, f32)[0])
    badar = np.ascontiguousarray(np.asarray(b_ada, f32).reshape(1, 6144))
    gpostr = np.ascontiguousarray(np.stack([np.asarray(g_mix_post, f32)[0], np.asarray(g_ffn_post, f32)[0]]))

    ident, tri, negm, triBD, negmBD, same, seqsel = _host_consts()
    scw = np.asarray(sc_conv_w, f32)[0]
    xcw = np.asarray(ssm_conv_w, f32)[0]
    base = np.zeros((128, CPW), f32)

    def put(name, arr):
        arr = np.asarray(arr, f32)
        base[:, _CP[name]:_CP[name] + arr.shape[1]] = arr

    put("ident", ident); put("tri", tri); put("negm", negm); put("triBD", triBD); put("negmBD", negmBD)
    put("same", same); put("seqsel", seqsel)
    put("gpre", _fm(np.asarray(g_mix_pre, f32)[0], 8))
    put("gfpre", _fm(np.asarray(g_ffn_pre, f32)[0], 8))
    put("scw", scw.reshape(3, 8, 128).transpose(2, 1, 0).reshape(128, 24))
    put("xcw", xcw.reshape(4, 16, 128).transpose(2, 1, 0).reshape(128, 64))
    put("xcb", _fm(np.asarray(ssm_conv_b, f32)[0], 16))
    put("dtb", np.broadcast_to(np.asarray(dt_bias, f32).reshape(1, 16), (128, 16)))
    put("alog", np.broadcast_to(np.asarray(a_log, f32).reshape(1, 16), (128, 16)))
    put("dcol", _fm(np.repeat(np.asarray(d_skip, f32)[0], 64), 8))
    put("ng", _fm(np.asarray(ssm_norm_g, f32)[0], 8))
    put("bada", _fm(np.asarray(b_ada, f32)[0], 48))
    put("eps", np.full((128, 1), EPS, f32))

    in_maps = []
    for i in range(NCORES):
        cpk = base.copy()
        call = np.concatenate([np.asarray(c_prompt, f32)[i:i + 1], np.asarray(c_sample, f32)[16 * i:16 * i + 16]], axis=0)
        cpk[:, _CP["cT"]:_CP["cT"] + 136] = call.reshape(17, 8, 128).transpose(2, 1, 0).reshape(128, 136)
        xall = np.concatenate([x_prompt[i], x_sample[16 * i:16 * i + 16].reshape(128, D)], axis=0)
        scs = np.asarray(state_sc_conv, f32)[0, 16 * i:16 * i + 16]
        scst = scs.reshape(16, 2, 8, 128).transpose(3, 2, 0, 1).reshape(128, 256)
        xcs = np.asarray(state_ssm_conv, f32)[0, 16 * i:16 * i + 16]
        xcst = xcs.reshape(16, 3, 16, 128).transpose(3, 2, 0, 1).reshape(128, 768)
        sst = np.asarray(state_ssm, f32)[0, 16 * i:16 * i + 16].reshape(16, 1024, 128)
        in_maps.append({
            "xall": np.ascontiguousarray(xall), "cpk": cpk, "scst": np.ascontiguousarray(scst),
            "xcst": np.ascontiguousarray(xcst), "sst": np.ascontiguousarray(sst),
            "wada": wada, "badar": badar, "gpostr": gpostr, "wsc": wsc, "wssm": wssm, "wdt": wdt,
            "wout": wout, "wg": wg, "wu": wu, "wd": wd,
        })

    if "nc" not in _CACHE:
        import os
        _CACHE["nc"] = build_program(int(os.environ.get("KSTOP", "99")))
    nc, _ = _CACHE["nc"]
    import os
    ncr = int(os.environ.get("KCORES", str(NCORES)))
    res = run_bass_kernel_spmd(nc, in_maps[:ncr], core_ids=list(range(ncr)))
    R = list(res.results)
    while len(R) < NCORES:
        R.append(R[0])

    y_prompt = np.stack([R[i]["y"][0:LP] for i in range(NCORES)]).astype(f32)
    y_sample = np.concatenate([R[i]["y"][LP:].reshape(16, LS, D) for i in range(NCORES)], axis=0).astype(f32)
    nscp = np.stack([R[i]["nscp"].reshape(128, 8, 2).transpose(2, 1, 0).reshape(2, 1024) for i in range(NCORES)])[None]
    nxcp = np.stack([R[i]["nxcp"].reshape(128, 16, 3).transpose(2, 1, 0).reshape(3, 2048) for i in range(NCORES)])[None]
    nstp = np.stack([R[i]["nstp"].reshape(16, 64, 128) for i in range(NCORES)])[None]
    nscs = np.concatenate([R[i]["nscs"].reshape(128, 8, 16, 2).transpose(2, 3, 1, 0).reshape(16, 2, 1024) for i in range(NCORES)])[None]
    nxcs = np.concatenate([R[i]["nxcs"].reshape(128, 16, 16, 3).transpose(2, 3, 1, 0).reshape(16, 3, 2048) for i in range(NCORES)])[None]
    nsts = np.concatenate([R[i]["nsts"].reshape(16, 16, 64, 128) for i in range(NCORES)])[None]
    return (y_prompt, y_sample, np.ascontiguousarray(nscp, f32), np.ascontiguousarray(nxcp, f32),
            np.ascontiguousarray(nstp, f32), np.ascontiguousarray(nscs, f32), np.ascontiguousarray(nxcs, f32),
            np.ascontiguousarray(nsts, f32))
```

```python
import contextlib
import numpy as np
import concourse.bass as bass
import concourse.mybir as mybir
from concourse.bass_utils import run_bass_kernel_spmd

F32 = mybir.dt.float32
BF16 = mybir.dt.bfloat16
AF = mybir.ActivationFunctionType
ALU = mybir.AluOpType

NCORES = 8
D = 1024
LP = 2048
NSEQ = 16
LS = 8
NTOK = LP + NSEQ * LS
NCH = LP // 128
HID = 2816
NHT = HID // 128
EPS = 1e-6
NEG = -30000.0
XLAT = 0.7

_CP = {}
_off = 0
for _n, _w in [("ident", 128), ("tri", 128), ("negm", 128), ("triBD", 128), ("negmBD", 128),
               ("same", 128), ("seqsel", 16), ("gpre", 8), ("gfpre", 8), ("scw", 24),
               ("xcw", 64), ("xcb", 16), ("dtb", 16), ("alog", 16), ("dcol", 8), ("ng", 8),
               ("bada", 48), ("cT", 136), ("eps", 1)]:
    _CP[_n] = _off
    _off += _w
CPW = _off


class Buf:
    __slots__ = ("name", "writers", "readers", "excl", "gen_deps")

    def __init__(self, name, excl=False):
        self.name = name
        self.writers = []
        self.readers = []
        self.gen_deps = set()
        self.excl = excl


class Chan:
    def __init__(self, name):
        self.name = name
        self.sem = None
        self.n = 0
        self.last = None


class Op:
    __slots__ = ("eng", "fn", "deps", "idx", "marked", "chan", "seq", "cnt", "waits", "sdeps", "cost", "dlat", "site", "adeps")


class _FakeEng:
    def __getattr__(self, name):
        def f(*a, **k):
            out = k.get("out", a[0] if a else None)
            return (name, out, k)
        return f


def _est_cost(eng, fn, is_dma):
    try:
        name, out, k = fn(_FakeEng())
        n = 1
        for d in out.shape[1:]:
            n *= int(d)
        parts = int(out.shape[0])
    except Exception:
        return 0.3, 0.0
    if is_dma:
        return 0.15, 2.0 + n * parts * 4 / 250e3
    if eng == "pe":
        return max(0.058, (n + 12) / 2400.0), 0.0
    if eng == "act":
        return 0.17 + n / 1250.0 + (0.1 if k.get("accum_out") is not None else 0.0), 0.0
    if eng == "dve":
        return 0.19 + n / 960.0, 0.0
    if eng == "pool":
        return 0.2 + n / 560.0, 0.0
    return 0.2, 0.0


class Prog:
    ENGS = ("pe", "act", "dve", "pool", "sp")

    def __init__(self):
        self.ops = []
        self.chans = []

    def chan(self, name):
        c = Chan(name)
        self.chans.append(c)
        return c

    def _record(self, eng, fn, reads, writes, chan=None, extra=(), nowaw=False):
        op = Op()
        op.eng = eng
        op.fn = fn
        op.idx = len(self.ops)
        op.marked = False
        op.chan = chan
        op.seq = None
        op.cnt = None
        deps = set(extra)
        sdeps = set()
        for b in reads:
            deps.update(b.writers)
            if b.excl:
                for r in b.readers:
                    if self.ops[r].eng != eng:
                        deps.add(r)
        for b in writes:
            deps.update(b.readers)
            if nowaw and not b.readers:
                deps.update(b.gen_deps)
            for w in b.writers:
                wo = self.ops[w]
                if wo.eng != eng or wo.chan is not None or chan is not None:
                    if not nowaw:
                        deps.add(w)
                else:
                    sdeps.add(w)
        if chan is not None and getattr(chan, "last", None) is not None:
            sdeps.add(chan.last)
        if eng == "pe" and fn is not None:
            pp = {d for d in deps if (self.ops[d].eng == "pe" and self.ops[d].chan is None)}
            sdeps |= pp
            deps = deps - pp
        import sys as _sys
        op.site = _sys._getframe(2).f_lineno
        op.deps = deps
        op.adeps = set(deps)
        op.sdeps = sdeps | deps
        op.cost, op.dlat = (0.0, 0.0) if fn is None else _est_cost(eng, fn, chan is not None)
        if chan is not None:
            chan.n += 1
            op.seq = chan.n
            chan.last = op.idx
        self.ops.append(op)
        for b in reads:
            b.readers.append(op.idx)
        for b in writes:
            if b.readers:
                b.gen_deps = set(b.readers) | set(b.writers)
                b.writers = [op.idx]
                b.readers = []
            else:
                b.writers.append(op.idx)
        return op

    def op(self, eng, fn, reads=(), writes=(), nowaw=False):
        return self._record(eng, fn, reads, writes, nowaw=nowaw)

    def dma(self, eng, out, in_, chan, reads=(), writes=()):
        def fn(e, out=out, in_=in_):
            return e.dma_start(out=out, in_=in_)
        return self._record(eng, fn, reads, writes, chan=chan)

    def barrier(self):
        last = {}
        for op in self.ops:
            if op.fn is None:
                continue
            if op.chan is None:
                last[("e", op.eng)] = op.idx
            else:
                last[("c", id(op.chan))] = op.idx
        deps = set(last.values())
        for eng in self.ENGS:
            self._record(eng, None, (), (), extra=deps)

    def schedule(self):
        import heapq
        ops = self.ops
        order = []
        seg = []

        def flush():
            if not seg:
                return
            ids = set(o.idx for o in seg)
            preds = {o.idx: [d for d in o.sdeps if d in ids] for o in seg}
            succs = {o.idx: [] for o in seg}
            for o in seg:
                for d in preds[o.idx]:
                    succs[d].append(o.idx)

            def lat(d, o):
                do = ops[d]
                if do.chan is not None:
                    return do.dlat + 0.2
                return XLAT if do.eng != o.eng else 0.02

            prio = {}
            for o in reversed(seg):
                p = 0.0
                for sidx in succs[o.idx]:
                    q = lat(o.idx, ops[sidx]) + prio[sidx]
                    if q > p:
                        p = q
                prio[o.idx] = p + o.cost
            nun = {o.idx: len(preds[o.idx]) for o in seg}
            avail = {o.idx: 0.0 for o in seg}
            pending = {e: [] for e in self.ENGS}
            ready = {e: [] for e in self.ENGS}
            free = {e: 0.0 for e in self.ENGS}
            for o in seg:
                if nun[o.idx] == 0:
                    heapq.heappush(pending[o.eng], (0.0, -prio[o.idx], o.idx))
            start = {}
            left = len(seg)
            while left:
                best = None
                for e in self.ENGS:
                    while pending[e] and pending[e][0][0] <= free[e]:
                        a, np_, i = heapq.heappop(pending[e])
                        heapq.heappush(ready[e], (np_, i))
                    if ready[e]:
                        t = free[e]
                    elif pending[e]:
                        t = pending[e][0][0]
                    else:
                        continue
                    if best is None or t < best[0]:
                        best = (t, e)
                t, e = best
                while pending[e] and pending[e][0][0] <= t:
                    a, np_, i = heapq.heappop(pending[e])
                    heapq.heappush(ready[e], (np_, i))
                np_, i = heapq.heappop(ready[e])
                o = ops[i]
                start[i] = t
                fin = t + o.cost
                free[e] = fin
                left -= 1
                for sidx in succs[i]:
                    a = fin + lat(i, ops[sidx])
                    if a > avail[sidx]:
                        avail[sidx] = a
                    nun[sidx] -= 1
                    if nun[sidx] == 0:
                        heapq.heappush(pending[ops[sidx].eng], (avail[sidx], -prio[sidx], sidx))
            order.extend(sorted(seg, key=lambda o: (start[o.idx], o.idx)))
            self.sim_time = getattr(self, "sim_time", 0.0) + max(free.values())
            del seg[:]

        for op in ops:
            if op.fn is None:
                flush()
                order.append(op)
            else:
                seg.append(op)
        flush()
        self.order = order

    def emit(self, nc, final_chans=()):
        import os
        if os.environ.get("KSCHED", "1") == "1":
            self.schedule()
            ops_order = self.order
        else:
            ops_order = list(self.ops)
        allops = self.ops
        rank = {}
        for r, op in enumerate(ops_order):
            rank[op.idx] = r
        last = {}
        for op in ops_order:
            if op.fn is None:
                op.deps = set(last.values())
            elif op.chan is not None:
                last[("c", id(op.chan))] = op.idx
            else:
                last[("e", op.eng)] = op.idx
        ops = allops
        for op in ops_order:
            red = {}
            for d in op.deps:
                dop = ops[d]
                if dop.fn is None:
                    continue
                key = ("c", id(dop.chan)) if dop.chan is not None else ("e", dop.eng)
                if key not in red or rank[red[key]] < rank[d]:
                    red[key] = d
            op.deps = set(red.values())
            for d in op.deps:
                ops[d].marked = True
        with contextlib.ExitStack() as st:
            esem = {e: st.enter_context(nc.semaphore("s_" + e)) for e in self.ENGS}
            for c in self.chans:
                c.sem = st.enter_context(nc.semaphore("c_" + c.name))
            cnt = {e: 0 for e in self.ENGS}
            for op in ops_order:
                if op.chan is None and op.marked and op.fn is not None:
                    cnt[op.eng] += 1
                    op.cnt = cnt[op.eng]
            waited = {e: {} for e in self.ENGS}
            per_eng = {e: [] for e in self.ENGS}
            for op in ops_order:
                need = {}
                for d in op.deps:
                    dop = ops[d]
                    if dop.fn is None:
                        continue
                    if dop.chan is not None:
                        key, val = ("c", dop.chan), 16 * dop.seq
                    else:
                        key, val = ("e", dop.eng), dop.cnt
                    if need.get(key, 0) < val:
                        need[key] = val
                w = []
                for key, val in need.items():
                    if waited[op.eng].get(key, 0) >= val:
                        continue
                    waited[op.eng][key] = val
                    sem = key[1].sem if key[0] == "c" else esem[key[1]]
                    w.append((sem, val))
                op.waits = w
                per_eng[op.eng].append(op)
            self.stats = {e: len(per_eng[e]) for e in self.ENGS}
            self.stats["cnt"] = dict(cnt)
            with nc.Block() as block:
                def run(e, lst, is_sp=False):
                    for op in lst:
                        for sem, val in op.waits:
                            e.wait_ge(sem, val)
                        if op.fn is None:
                            continue
                        ins = op.fn(e)
                        if op.chan is not None:
                            ins.then_inc(op.chan.sem, 16)
                        elif op.marked:
                            ins.then_inc(esem[op.eng], 1)
                    if is_sp:
                        for c in final_chans:
                            if c.n:
                                e.wait_ge(c.sem, 16 * c.n)

                @block.tensor
                def _(e):
                    run(e, per_eng["pe"])

                @block.scalar
                def _(e):
                    run(e, per_eng["act"])

                @block.vector
                def _(e):
                    run(e, per_eng["dve"])

                @block.gpsimd
                def _(e):
                    run(e, per_eng["pool"])

                @block.sync
                def _(e):
                    run(e, per_eng["sp"], is_sp=True)


class _Stop(Exception):
    pass


class Tl:
    def __init__(self, ap, buf, off, nbytes):
        self.ap = ap
        self.buf = buf
        self.off = off
        self.nbytes = nbytes

    def __getitem__(self, k):
        return self.ap[k]


class SBA:
    def __init__(self, big, total_bytes):
        self.big = big
        self.total = total_bytes
        self.off = 0
        self.peak = 0

    def _view(self, off, shape, dt):
        nfree = 1
        for s in shape[1:]:
            nfree *= s
        esz = 4 if dt == F32 else 2
        nbytes = nfree * esz
        assert off % 4 == 0 and nbytes % 4 == 0, (off, nbytes)
        ap = self.big[0:shape[0], off // 4:(off + nbytes) // 4]
        if dt != F32:
            ap = ap.bitcast(dt)
        if len(shape) == 3:
            ap = ap.rearrange("p (a b) -> p a b", a=shape[1])
        elif len(shape) == 4:
            ap = ap.rearrange("p (a b c) -> p a b c", a=shape[1], b=shape[2])
        return ap, nbytes

    def alloc(self, name, shape, dt=F32):
        off = (self.off + 31) // 32 * 32
        ap, nbytes = self._view(off, shape, dt)
        self.off = off + nbytes
        self.peak = max(self.peak, self.off)
        assert self.off <= self.total, ("SBUF overflow", name, self.off, self.total)
        return Tl(ap, Buf(name), off, nbytes)

    def alias(self, t, shape, dt=F32, span=None):
        ap, nbytes = self._view(t.off, shape, dt)
        assert nbytes <= (span if span is not None else t.nbytes)
        return Tl(ap, t.buf, t.off, t.nbytes)


def bc(ap, shape):
    return ap.broadcast_to(list(shape))


def build_program(stop=99):
    try:
        return _build_inner(stop)
    except _Stop as e:
        return e.args


def _build_inner(stop=99):
    nc = bass.Bass("TRN2", target_bir_lowering=False)

    def din(name, shape):
        return nc.dram_tensor(name, list(shape), F32, kind="ExternalInput").ap()

    def dout(name, shape):
        return nc.dram_tensor(name, list(shape), F32, kind="ExternalOutput").ap()

    xall_d = din("xall", [NTOK, D])
    cpk_d = din("cpk", [128, CPW])
    scst_d = din("scst", [128, 8 * 16 * 2])
    xcst_d = din("xcst", [128, 16 * 16 * 3])
    sst_d = din("sst", [NSEQ, 1024, 128])
    wada_d = din("wada", [6, 128, 8 * 1024])
    badar_d = din("badar", [1, 6144])
    gpostr_d = din("gpostr", [2, 1024])
    wsc_d = din("wsc", [8, 128, 8 * 384])
    wssm_d = din("wssm", [6, 128, 8 * 512])
    wdt_d = din("wdt", [128, 8 * 16])
    wout_d = din("wout", [2048, 1024])
    wg_d = din("wg", [4, 128, 8 * 704])
    wu_d = din("wu", [4, 128, 8 * 704])
    wd_d = din("wd", [HID, 1024])

    y_d = dout("y", [NTOK, D])
    nscp_d = dout("nscp", [128, 16])
    nxcp_d = dout("nxcp", [128, 48])
    nstp_d = dout("nstp", [1024, 128])
    nscs_d = dout("nscs", [128, 256])
    nxcs_d = dout("nxcs", [128, 768])
    nsts_d = dout("nsts", [NSEQ, 1024, 128])

    x1s_d = nc.dram_tensor("x1s", [NTOK, D], F32).ap()
    gsave_d = nc.dram_tensor("gsave", [2, 128, 1024], F32).ap()

    P = Prog()
    SB_BYTES = 207 * 1024
    with contextlib.ExitStack() as es:
        big = es.enter_context(nc.sbuf_tensor("sbig", [128, SB_BYTES // 4], F32))
        ps = es.enter_context(nc.psum_tensor("ps", [128, 4096], F32))
        A = SBA(big, SB_BYTES)

        bank = [ps[:, i * 512:(i + 1) * 512] for i in range(8)]
        bankbf = [b.bitcast(BF16) for b in bank]
        pb = [Buf("pb%d" % i, excl=True) for i in range(8)]

        def pair(i):
            return ps[:, i * 512:(i + 2) * 512]

        out_chans = []

        def cp(n):
            if stop == n:
                P.barrier()
                P.emit(nc, final_chans=out_chans)
                raise _Stop(nc, P)

        cpk = A.alloc("cpk", [128, CPW])
        cb = A.alloc("cb", [128, 768], BF16)
        ones_b = A.alloc("ones_b", [128, 128], BF16)
        zero_b = A.alloc("zero_b", [128, 128], BF16)
        ntri = A.alloc("ntri", [128, 2, 128], BF16)
        diagD = A.alloc("diagD", [128, 8, 128], BF16)
        Abc = A.alloc("Abc", [128, 16])
        A_m = A.alloc("A_m", [128, 8, 17])
        Bv_m = A.alloc("Bv_m", [128, 8, 17])
        A_f = A.alloc("A_f", [128, 8, 17])
        Bv_f = A.alloc("Bv_f", [128, 8, 17])
        gmP = A.alloc("gmP", [128, 1024])
        gmS = A.alloc("gmS", [128, 1024])
        stt = A.alloc("stt", [128, 8])
        mhalf = A.alloc("mhalf", [128, 4])
        rstd_all = A.alloc("rstd_all", [128, 32])
        cvw = A.alloc("cvw", [128, 80])
        mark_p2 = A.off
        y_scT = A.alloc("y_scT", [128, 8, NTOK], BF16)
        Wssm = A.alloc("Wssm", [128, 8, 3088], BF16)

        ident_b = cb[:, 0:128]
        tri_b = cb[:, 128:256]
        negm_b = cb[:, 256:384]
        triBD_b = cb[:, 384:512]
        negmBD_b = cb[:, 512:640]
        same_b = cb[:, 640:768]
        ident_f = cpk[:, _CP["ident"]:_CP["ident"] + 128]
        tri_f = cpk[:, _CP["tri"]:_CP["tri"] + 128]
        triBD_f = cpk[:, _CP["triBD"]:_CP["triBD"] + 128]
        epscol = cpk[:, _CP["eps"]:_CP["eps"] + 1]

        def cpc(name, i, n=1):
            return cpk[:, _CP[name] + i:_CP[name] + i + n]

        ch_c = P.chan("cpk")
        P.dma("sp", cpk[:], cpk_d[:, :], ch_c, writes=[cpk.buf])
        P.op("dve", lambda e: e.tensor_copy(out=cb[:], in_=cpk[:, 0:768]), reads=[cpk.buf], writes=[cb.buf])
        P.op("pool", lambda e: e.memset(ones_b[:], 1.0), writes=[ones_b.buf])
        P.op("pool", lambda e: e.memset(zero_b[:], 0.0), writes=[zero_b.buf])
        for i_, nm_ in enumerate(("tri", "triBD")):
            P.op("dve", lambda e, i_=i_, nm_=nm_: e.tensor_scalar(out=ntri[:, i_, :], in0=cpk[:, _CP[nm_]:_CP[nm_] + 128], scalar1=-1.0,
                                                               scalar2=None, op0=ALU.mult),
                 reads=[cpk.buf], writes=[ntri.buf])
        P.op("pool", lambda e: e.memset(mhalf[:], -0.5), writes=[mhalf.buf])
        P.op("dve", lambda e: e.tensor_scalar(out=cvw[:], in0=cpk[:, _CP["xcw"]:_CP["xcw"] + 80], scalar1=0.5, scalar2=None, op0=ALU.mult),
             reads=[cpk.buf], writes=[cvw.buf])
        P.op("act", lambda e: e.activation(out=Abc[:], in_=cpk[:, _CP["alog"]:_CP["alog"] + 16], func=AF.Exp),
             reads=[cpk.buf], writes=[Abc.buf])
        P.op("dve", lambda e: e.tensor_scalar(out=Abc[:], in0=Abc[:], scalar1=-1.0, scalar2=None, op0=ALU.mult),
             reads=[Abc.buf], writes=[Abc.buf])
        for j in range(8):
            P.op("dve", lambda e, j=j: e.tensor_scalar(out=diagD[:, j, :], in0=ident_b, scalar1=cpc("dcol", j),
                                                       scalar2=None, op0=ALU.mult),
                 reads=[cb.buf, cpk.buf], writes=[diagD.buf])

        ch_wssm = [P.chan("wssm%d" % q) for q in range(7)]

        mark0 = A.off
        wsl = [A.alloc("wsl%d" % i, [128, 8, 1024], BF16) for i in range(2)]
        ch_wsl = [P.chan("wsl%d" % i) for i in range(2)]
        siluT = A.alloc("siluT", [128, 8, 17], BF16)
        siluPx = A.alloc("siluPx", [128, 8, 128], BF16)
        siluSx = A.alloc("siluSx", [128, 8, 128], BF16)
        badaT = A.alloc("badaT", [128, 1024])
        gpostT = A.alloc("gpostT", [128, 1024])
        tmpT = A.alloc("tmpT", [128, 1024])
        gtmp = [A.alloc("gtmp%d" % i, [128, 1024]) for i in range(2)]
        mtmp = A.alloc("mtmp", [128, 8, 17])
        ch_bada = P.chan("bada")
        ch_gpost = P.chan("gpost")
        ch_gs = [P.chan("gsave%d" % i) for i in range(2)]

        P.op("act", lambda e: e.activation(out=siluT[:], in_=cpk[:, _CP["cT"]:_CP["cT"] + 136].rearrange("p (k s) -> p k s", k=8),
                                           func=AF.Silu), reads=[cpk.buf], writes=[siluT.buf])
        P.op("pool", lambda e: e.tensor_copy(out=siluPx[:], in_=bc(siluT[:, :, 0:1], [128, 8, 128])),
             reads=[siluT.buf], writes=[siluPx.buf])
        P.op("pool", lambda e: e.tensor_copy(out=siluSx[:].rearrange("p k (b l) -> p k b l", l=8),
                                             in_=bc(siluT[:, :, 1:17].unsqueeze(3), [128, 8, 16, 8])),
             reads=[siluT.buf], writes=[siluSx.buf])

        order_v = [1, 0, 2, 4, 3, 5]
        for vi, v in enumerate(order_v):
            s = vi % 2
            P.dma("pool", wsl[s][:], wada_d[v].rearrange("p (k c) -> p k c", k=8), ch_wsl[s], writes=[wsl[s].buf])
            if v in (0, 1, 3, 4):
                bi = vi % 2
                for ct in range(8):
                    for k in range(8):
                        P.op("pe", lambda e, s=s, ct=ct, k=k, bi=bi: e.matmul(
                            bank[bi][:, ct * 17:(ct + 1) * 17], lhsT=wsl[s][:, k, ct * 128:(ct + 1) * 128],
                            rhs=siluT[:, k, :], start=(k == 0), stop=(k == 7)),
                            reads=[wsl[s].buf, siluT.buf], writes=[pb[bi]])
                pv = bank[bi][:, 0:136].rearrange("p (c s) -> p c s", c=8)
                bb = bc(cpk[:, _CP["bada"] + v * 8:_CP["bada"] + v * 8 + 8].unsqueeze(2), [128, 8, 17])
                if v in (0, 3):
                    dst = Bv_m if v == 0 else Bv_f
                    P.op("dve", lambda e, pv=pv, bb=bb, dst=dst: e.tensor_tensor(out=dst[:], in0=pv, in1=bb, op=ALU.add),
                         reads=[pb[bi], cpk.buf], writes=[dst.buf])
                else:
                    dst = A_m if v == 1 else A_f
                    gn = "gpre" if v == 1 else "gfpre"
                    gb = bc(cpk[:, _CP[gn]:_CP[gn] + 8].unsqueeze(2), [128, 8, 17])
                    P.op("dve", lambda e, pv=pv, bb=bb: e.tensor_tensor(out=mtmp[:], in0=pv, in1=bb, op=ALU.add),
                         reads=[pb[bi], cpk.buf], writes=[mtmp.buf])
                    P.op("dve", lambda e, dst=dst, gb=gb: e.scalar_tensor_tensor(out=dst[:], in0=mtmp[:], scalar=1.0, in1=gb,
                                                                               op0=ALU.add, op1=ALU.mult),
                         reads=[mtmp.buf, cpk.buf], writes=[dst.buf])
            else:
                gi = 0 if v == 2 else 1
                P.dma("sp", badaT[:], bc(badar_d[0:1, v * 1024:(v + 1) * 1024], [128, 1024]), ch_bada, writes=[badaT.buf])
                P.dma("sp", gpostT[:], bc(gpostr_d[gi:gi + 1, :], [128, 1024]), ch_gpost, writes=[gpostT.buf])
                for wi, sx in enumerate((siluPx, siluSx)):
                    if v == 2:
                        dst = gmP if wi == 0 else gmS
                    else:
                        dst = gtmp[wi]
                    for half in range(2):
                        bi = 2 + (wi * 2 + half) % 4
                        for k in range(8):
                            P.op("pe", lambda e, s=s, k=k, bi=bi, half=half, sx=sx: e.matmul(
                                bank[bi], lhsT=sx[:, k, :], rhs=wsl[s][:, k, half * 512:(half + 1) * 512],
                                start=(k == 0), stop=(k == 7)),
                                reads=[wsl[s].buf, sx.buf], writes=[pb[bi]])
                        hs = slice(half * 512, (half + 1) * 512)
                        P.op("dve", lambda e, bi=bi, hs=hs: e.tensor_tensor(out=tmpT[:, hs], in0=bank[bi], in1=badaT[:, hs], op=ALU.add),
                             reads=[pb[bi], badaT.buf], writes=[tmpT.buf])
                        P.op("pool", lambda e, dst=dst, hs=hs: e.tensor_tensor(out=dst[:, hs], in0=tmpT[:, hs], in1=gpostT[:, hs], op=ALU.mult),
                             reads=[tmpT.buf, gpostT.buf], writes=[dst.buf])
                    if v == 5:
                        P.dma("sp", gsave_d[wi], dst[:], ch_gs[wi], reads=[dst.buf])

        P.barrier()
        A.off = mark0
        if stop == 0:
            P.emit(nc, final_chans=out_chans)
            return nc, P

        def rstd_pool(t, c0, n, N, eps=EPS):
            P.op("pool", lambda e: e.tensor_scalar(out=t[:, c0 + n:c0 + 2 * n], in0=t[:, c0:c0 + n], scalar1=1.0 / N, scalar2=eps,
                                                   op0=ALU.mult, op1=ALU.add), reads=[t.buf], writes=[t.buf])
            P.op("pool", lambda e: e.tensor_tensor(out=t[:, c0 + 2 * n:c0 + 3 * n], in0=t[:, c0 + n:c0 + 2 * n], in1=mhalf[:, 0:n], op=ALU.pow),
                 reads=[t.buf, mhalf.buf], writes=[t.buf])

        def stage_A_pre(xt, xn, st_):
            P.op("act", lambda e: e.activation(out=xn[:], in_=xt[:], func=AF.Square, accum_out=st_[:, 0:1]),
                 reads=[xt.buf], writes=[xn.buf, st_.buf])
            rstd_pool(st_, 0, 1, D)
            P.op("act", lambda e: e.activation(out=xn[:], in_=xt[:], func=AF.Identity, scale=st_[:, 2:3]),
                 reads=[xt.buf, st_.buf], writes=[xn.buf])

        def stage_A(xt, xn, Amod, Bmod, sample, hT_dst, hT_buf, tmp4k):
            stage_A_pre(xt, xn, stt)
            stage_A_post(xn, Amod, Bmod, sample, hT_dst, hT_buf, tmp4k)

        def stage_A_post(xn, Amod, Bmod, sample, hT_dst, hT_buf, tmp4k):
            cp(20)
            for k in range(8):
                P.op("pe", lambda e, k=k: e.transpose(bankbf[0][:, k * 128:(k + 1) * 128], xn[:, k * 128:(k + 1) * 128], ident_b),
                     reads=[xn.buf, cb.buf], writes=[pb[0]])
            cp(21)
            if not sample:
                import os
                for k in range(8):
                    kev = os.environ.get("KEVAC", "act")
                    if kev == "dve":
                        P.op("dve", lambda e, k=k: e.tensor_scalar(out=hT_dst[:, k, :], in0=bankbf[0][:, k * 128:(k + 1) * 128],
                                                                   scalar1=Amod[:, k, 0:1], scalar2=Bmod[:, k, 0:1],
                                                                   op0=ALU.mult, op1=ALU.add),
                             reads=[pb[0], Amod.buf, Bmod.buf], writes=[hT_buf], nowaw=True)
                    else:
                        P.op("act", lambda e, k=k: e.activation(out=hT_dst[:, k, :], in_=bankbf[0][:, k * 128:(k + 1) * 128],
                                                                func=AF.Identity, scale=Amod[:, k, 0:1], bias=Bmod[:, k, 0:1]),
                             reads=[pb[0], Amod.buf, Bmod.buf], writes=[hT_buf], nowaw=True)
            else:
                pv = bankbf[0].rearrange("p (k b l) -> p k b l", k=8, b=16)
                tv = tmp4k[:].rearrange("p (k b l) -> p k b l", k=8, b=16)
                P.op("dve", lambda e: e.tensor_tensor(out=tv, in0=pv, in1=bc(Amod[:, :, 1:17].unsqueeze(3), [128, 8, 16, 8]), op=ALU.mult),
                     reads=[pb[0], Amod.buf], writes=[tmp4k.buf])
                P.op("dve", lambda e: e.tensor_tensor(out=hT_dst.rearrange("p k (b l) -> p k b l", b=16), in0=tv,
                                                      in1=bc(Bmod[:, :, 1:17].unsqueeze(3), [128, 8, 16, 8]), op=ALU.add),
                     reads=[tmp4k.buf, Bmod.buf], writes=[hT_buf])

        mark1 = A.off
        Wsc = A.alloc("Wsc", [128, 8, 3072], BF16)
        ch_wsc = [P.chan("wsc%d" % j) for j in range(8)]
        wsc_bufs = [Buf("wsc%d" % j) for j in range(8)]
        for j in range(8):
            P.dma("pool", Wsc[:, :, j * 384:(j + 1) * 384], wsc_d[j].rearrange("p (k c) -> p k c", k=8), ch_wsc[j],
                  writes=[wsc_bufs[j]])
        wssm_bufs = [Buf("wssmq%d" % q) for q in range(7)]
        for q in range(6):
            P.dma("pool", Wssm[:, :, q * 512:(q + 1) * 512], wssm_d[q].rearrange("p (k c) -> p k c", k=8), ch_wssm[q],
                  writes=[wssm_bufs[q]])
        P.dma("pool", Wssm[:, :, 3072:3088], wdt_d[:, :].rearrange("p (k c) -> p k c", k=8), ch_wssm[6], writes=[wssm_bufs[6]])
        xin = [A.alloc("xin%d" % i, [128, 1024]) for i in range(2)]
        ch_xin = [P.chan("xin%d" % i) for i in range(2)]
        xn = A.alloc("xn", [128, 1024], BF16)
        hT = A.alloc("hT", [128, 8, 512], BF16)
        pxs = A.alloc("pxs", [128, 512])
        uext = [A.alloc("uext%d" % i, [128, 516]) for i in range(2)]
        ucv = [A.alloc("ucv%d" % i, [128, 512]) for i in range(2)]
        uhist = A.alloc("uhist", [128, 8, 2])
        scst = A.alloc("scst", [128, 8, 16, 2])
        nscs = A.alloc("nscs", [128, 8, 16, 2])
        tmp4k = A.alloc("tmp4k", [128, 1024])
        ch_scst = P.chan("scst")
        ch_nscp = P.chan("nscp")
        ch_nscs = P.chan("nscs")
        out_chans += [ch_nscp, ch_nscs]
        P.dma("sp", scst[:].rearrange("p j b r -> p (j b r)"), scst_d[:, :], ch_scst, writes=[scst.buf])
        P.op("pool", lambda e: e.memset(uhist[:], 0.0), writes=[uhist.buf])

        groups = [(g * 512, 512, False) for g in range(4)] + [(LP, 128, True)]
        tile_toks = [t0 + tt * 128 for (t0, ts, _) in groups for tt in range(ts // 128)]

        def load_x1a(n):
            if n < len(tile_toks):
                P.dma("sp", xin[n % 2][:], xall_d[tile_toks[n]:tile_toks[n] + 128, :], ch_xin[n % 2], writes=[xin[n % 2].buf])

        cp(10)
        hTs = [hT, A.alloc("hT1", [128, 8, 512], BF16)]
        xns = [xn] + [A.alloc("xn_%d" % i, [128, 1024], BF16) for i in range(1, 4)]
        stq = [A.alloc("stq%d" % i, [128, 4]) for i in range(4)]
        xc = [0]

        def stage_A1_pre(gi):
            t0_, ts_, smp_ = groups[gi]
            for tt in range(ts_ // 128):
                xs_ = xin[xc[0] % 2]
                load_x1a(xc[0] + 1)
                xc[0] += 1
                stage_A_pre(xs_, xns[tt], stq[tt])
                tix = (t0_ + tt * 128) // 128
                P.op("pool", lambda e, tt=tt, tix=tix: e.tensor_copy(out=rstd_all[:, tix:tix + 1], in_=stq[tt][:, 2:3]),
                     reads=[stq[tt].buf], writes=[rstd_all.buf], nowaw=True)

        def stage_A1_post(gi):
            t0_, ts_, smp_ = groups[gi]
            for tt in range(ts_ // 128):
                stage_A_post(xns[tt], A_m, Bv_m, smp_, hTs[gi % 2][:, :, tt * 128:(tt + 1) * 128], hTs[gi % 2].buf, tmp4k)

        def stage_A1(gi):
            stage_A1_pre(gi)
            stage_A1_post(gi)

        load_x1a(0)
        stage_A1(0)
        for gi, (tok0, TS, sample) in enumerate(groups):
            hTc = hTs[gi % 2]
            for j in range(8):
                bs = (1, 2, 3) if j % 2 == 0 else (4, 5, 6)
                for part in range(3):
                    bi = bs[part]
                    for k in range(8):
                        P.op("pe", lambda e, j=j, part=part, bi=bi, k=k, TS=TS, hTc=hTc: e.matmul(
                            bank[bi][:, 0:TS], lhsT=Wsc[:, k, j * 384 + part * 128:j * 384 + (part + 1) * 128],
                            rhs=hTc[:, k, 0:TS], start=(k == 0), stop=(k == 7)),
                            reads=[wsc_bufs[j], hTc.buf], writes=[pb[bi]])
                if j == 2 and gi + 1 < len(groups):
                    stage_A1_pre(gi + 1)
                if j == 5 and gi + 1 < len(groups):
                    stage_A1_post(gi + 1)
                bcn, bxn, bbn = bs
                u = uext[j % 2]
                uc = ucv[j % 2]
                P.op("act", lambda e, bxn=bxn, TS=TS: e.activation(out=pxs[:, 0:TS], in_=bank[bxn][:, 0:TS], func=AF.Copy),
                     reads=[pb[bxn]], writes=[pxs.buf])
                if not sample:
                    P.op("pool", lambda e, u=u, j=j: e.tensor_copy(out=u[:, 0:2], in_=uhist[:, j, :]),
                         reads=[uhist.buf], writes=[u.buf])
                    P.op("dve", lambda e, u=u, bcn=bcn, TS=TS: e.tensor_tensor(out=u[:, 2:2 + TS], in0=bank[bcn][:, 0:TS], in1=pxs[:, 0:TS], op=ALU.mult),
                         reads=[pb[bcn], pxs.buf], writes=[u.buf], nowaw=True)
                    for r in range(3):
                        if r == 0:
                            P.op("dve", lambda e, u=u, uc=uc, j=j, TS=TS: e.tensor_scalar(
                                out=uc[:, 0:TS], in0=u[:, 0:TS], scalar1=cpc("scw", j * 3), scalar2=None, op0=ALU.mult),
                                reads=[u.buf, cpk.buf], writes=[uc.buf])
                        else:
                            P.op("dve", lambda e, u=u, uc=uc, j=j, r=r, TS=TS: e.scalar_tensor_tensor(
                                out=uc[:, 0:TS], in0=u[:, r:r + TS], scalar=cpc("scw", j * 3 + r), in1=uc[:, 0:TS],
                                op0=ALU.mult, op1=ALU.add),
                                reads=[u.buf, uc.buf, cpk.buf], writes=[uc.buf])
                    P.op("dve", lambda e, uc=uc, bbn=bbn, j=j, tok0=tok0, TS=TS: e.tensor_tensor(
                        out=y_scT[:, j, tok0:tok0 + TS], in0=bank[bbn][:, 0:TS], in1=uc[:, 0:TS], op=ALU.mult),
                        reads=[pb[bbn], uc.buf], writes=[y_scT.buf])
                    P.op("pool", lambda e, u=u, j=j, TS=TS: e.tensor_copy(out=uhist[:, j, :], in_=u[:, TS:TS + 2]),
                         reads=[u.buf], writes=[uhist.buf])
                else:
                    u3 = u[:, 0:160].rearrange("p (b c) -> p b c", b=16)
                    uc3 = uc[:, 0:128].rearrange("p (b l) -> p b l", b=16)
                    P.op("pool", lambda e, u3=u3, j=j: e.tensor_copy(out=u3[:, :, 0:2], in_=scst[:, j, :, :]),
                         reads=[scst.buf], writes=[u.buf])
                    P.op("dve", lambda e, u3=u3, bcn=bcn: e.tensor_tensor(
                        out=u3[:, :, 2:10], in0=bank[bcn][:, 0:128].rearrange("p (b l) -> p b l", b=16),
                        in1=pxs[:, 0:128].rearrange("p (b l) -> p b l", b=16), op=ALU.mult),
                        reads=[pb[bcn], pxs.buf], writes=[u.buf], nowaw=True)
                    for r in range(3):
                        if r == 0:
                            P.op("dve", lambda e, u3=u3, uc3=uc3, j=j: e.tensor_scalar(
                                out=uc3, in0=u3[:, :, 0:8], scalar1=cpc("scw", j * 3), scalar2=None, op0=ALU.mult),
                                reads=[u.buf, cpk.buf], writes=[uc.buf])
                        else:
                            P.op("dve", lambda e, u3=u3, uc3=uc3, j=j, r=r: e.scalar_tensor_tensor(
                                out=uc3, in0=u3[:, :, r:r + 8], scalar=cpc("scw", j * 3 + r), in1=uc3,
                                op0=ALU.mult, op1=ALU.add),
                                reads=[u.buf, uc.buf, cpk.buf], writes=[uc.buf])
                    P.op("dve", lambda e, uc=uc, bbn=bbn, j=j: e.tensor_tensor(
                        out=y_scT[:, j, LP:LP + 128], in0=bank[bbn][:, 0:128], in1=uc[:, 0:128], op=ALU.mult),
                        reads=[pb[bbn], uc.buf], writes=[y_scT.buf])
                    P.op("pool", lambda e, u3=u3, j=j: e.tensor_copy(out=nscs[:, j, :, :], in_=u3[:, :, 8:10]),
                         reads=[u.buf], writes=[nscs.buf])
            if tok0 == 3 * 512:
                P.dma("sp", nscp_d[:, :], uhist[:].rearrange("p j r -> p (j r)"), ch_nscp, reads=[uhist.buf])
        P.dma("sp", nscs_d[:, :], nscs[:].rearrange("p j b r -> p (j b r)"), ch_nscs, reads=[nscs.buf])

        P.barrier()
        A.off = mark1
        if stop == 1:
            P.emit(nc, final_chans=out_chans)
            return nc, P

        Wout = A.alloc("Wout", [128, 16, 1024], BF16)
        ch_wout = [P.chan("wout%d" % i) for i in range(4)]
        wout_bufs = [Buf("wout%d" % i) for i in range(4)]
        for i in range(4):
            P.dma("pool", Wout[:, i * 4:(i + 1) * 4, :], wout_d[i * 512:(i + 1) * 512, :].rearrange("(k p) n -> p k n", p=128),
                  ch_wout[i], writes=[wout_bufs[i]])
        NXS = 3
        xinb = [A.alloc("xinb%d" % i, [128, 1024]) for i in range(NXS)]
        xnb = A.alloc("xnb", [128, 1024], BF16)
        hTb = A.alloc("hTb", [128, 8, 128], BF16)
        ext = [A.alloc("ext%d" % i, [128, 4, 176]) for i in range(2)]
        cv = [A.alloc("cv%d" % i, [128, 4, 128]) for i in range(2)]
        xsT0 = A.alloc("xsT", [128, 8, 128], BF16)
        BT0 = A.alloc("BT", [128, 4, 128], BF16)
        CT0 = A.alloc("CT", [128, 4, 128], BF16)
        szT0 = A.alloc("szT", [128, 8, 128], BF16)
        sm16_0 = A.alloc("sm16", [128, 12, 16])
        dhl0 = A.alloc("dhl", [128, 2, 16], BF16)
        xdt = A.alloc("xdt", [128, 1024], BF16)
        xdtd = A.alloc("xdtd", [128, 1024], BF16)
        Btok = A.alloc("Btok", [128, 512], BF16)
        smk = A.alloc("smk", [128, 4, 128])
        Lsb = [A.alloc("Lsb%d" % i, [128, 4, 128]) for i in range(2)]
        Mt = A.alloc("Mt", [128, 16, 128], BF16)
        ST = A.alloc("ST", [128, 1024])
        STb = A.alloc("STb", [128, 1024], BF16)
        ycomb = A.alloc("ycomb", [128, 1024])
        ygn = A.alloc("ygn", [128, 1024], BF16)
        yssdT = A.alloc("yssdT", [128, 8, 128], BF16)
        xhist = A.alloc("xhist", [128, 16, 3])
        xcst = A.alloc("xcst", [128, 16, 16, 3])
        nxcs = A.alloc("nxcs", [128, 16, 16, 3])
        ss4 = A.alloc("ss4", [128, 12])
        sttb = A.alloc("sttb", [128, 8])
        ptmp = A.alloc("ptmp", [128, 4, 128])
        CTm = [A.alloc("CTm%d" % i, [128, 4, 128], BF16) for i in range(2)]
        Bm = A.alloc("Bm", [128, 512], BF16)
        decT = A.alloc("decT", [128, 8, 16])
        S0n = [A.alias(ext[0], [128, 8, 128], span=ext[0].nbytes + ext[1].nbytes),
               A.alias(cv[0], [128, 8, 128], span=cv[0].nbytes + cv[1].nbytes),
               A.alias(ST, [128, 8, 128])]
        assert ext[1].off == ext[0].off + ext[0].nbytes and cv[1].off == cv[0].off + cv[0].nbytes
        S0n_bufs = [[ext[0].buf, ext[1].buf], [cv[0].buf, cv[1].buf], [ST.buf]]
        dtAx = A.alias(Mt, [128, 2, 1024], BF16)
        S0b = A.alias(ygn, [128, 1024], BF16)
        S0T = A.alias(STb, [128, 1024], BF16)
        bsets = [dict(xsT=xsT0, BT=BT0, CT=CT0, szT=szT0, sm16=sm16_0, dhl=dhl0),
                 dict(xsT=A.alias(xcst, [128, 8, 128], BF16), szT=A.alias(nxcs, [128, 8, 128], BF16),
                      BT=A.alias(CTm[0], [128, 4, 128], BF16), CT=A.alias(CTm[1], [128, 4, 128], BF16),
                      sm16=A.alias(Bm, [128, 12, 16]), dhl=A.alias(decT, [128, 2, 16], BF16))]
        ch_xinb = [P.chan("xinb%d" % i) for i in range(NXS)]
        ch_xcst = P.chan("xcst")
        ch_s0 = [P.chan("s0n%d" % i) for i in range(3)]
        ch_nxcp = P.chan("nxcp")
        ch_nxcs = P.chan("nxcs")
        ch_nstp = P.chan("nstp")
        out_chans += [ch_nxcp, ch_nxcs, ch_nstp] + ch_s0
        P.dma("sp", xcst[:].rearrange("p j b r -> p (j b r)"), xcst_d[:, :], ch_xcst, writes=[xcst.buf])
        P.op("pool", lambda e: e.memset(xhist[:], 0.0), writes=[xhist.buf])
        for i in range(2):
            P.op("pool", lambda e, i=i: e.memset(CTm[i][:], 0.0), writes=[CTm[i].buf])

        pb1f = pb[1]
        pb1b = pb[1]
        b1tok = bankbf[5][:, 0:512]

        DTR, DT, DTA, TMP, NAC, EAC, DLT, DTE, DEC, EXPX = range(10)

        chunk_list = [(LP, True, -1)] + [(c * 128, False, c) for c in range(NCH)]

        def load_x1b(n):
            if n < len(chunk_list):
                t0 = chunk_list[n][0]
                P.dma("sp", xinb[n % NXS][:], xall_d[t0:t0 + 128, :], ch_xinb[n % NXS], writes=[xinb[n % NXS].buf])

        def conv_wide(eng, wt, c_, taps, shp, j0, ebuf):
            nd = len(shp)

            def wb(r):
                w = cvw[:, j0 * 4 + r:j0 * 4 + 16:4]
                w = w.unsqueeze(2) if nd == 3 else w.unsqueeze(2).unsqueeze(3)
                return bc(w, shp)

            bb = cvw[:, 64 + j0:64 + j0 + 4]
            bb = bc(bb.unsqueeze(2) if nd == 3 else bb.unsqueeze(2).unsqueeze(3), shp)
            cview = c_[:] if nd == 3 else c_[:].rearrange("p a (b l) -> p a b l", b=16)
            wview = wt[:].bitcast(F32).rearrange("p (a t) -> p a t", a=4) if wt.ap.dtype != F32 else wt[:]
            if nd == 4:
                wview = wview.rearrange("p a (b l) -> p a b l", b=16)
            for jj in range(4):
                P.op("act", lambda e, jj=jj: e.activation(out=cview[:, jj], in_=taps[0][:, jj], func=AF.Identity,
                                                          scale=cvw[:, (j0 + jj) * 4:(j0 + jj) * 4 + 1],
                                                          bias=cvw[:, 64 + j0 + jj:64 + j0 + jj + 1]),
                     reads=[ebuf, cvw.buf], writes=[c_.buf])
            for jj in range(4):
                for r in range(1, 4):
                    P.op(eng, lambda e, r=r, jj=jj: e.scalar_tensor_tensor(
                        out=cview[:, jj], in0=taps[r][:, jj], scalar=cvw[:, (j0 + jj) * 4 + r:(j0 + jj) * 4 + r + 1], in1=cview[:, jj],
                        op0=ALU.mult, op1=ALU.add),
                        reads=[ebuf, cvw.buf, c_.buf], writes=[c_.buf])

        def front(ci):
            tok0, sample, cidx = chunk_list[ci]
            S = bsets[0] if sample else bsets[cidx % 2]
            xsT, BT, CT, szT, sm16, dhl = S["xsT"], S["BT"], S["CT"], S["szT"], S["sm16"], S["dhl"]

            def s16(i):
                return sm16[:, i, :]

            xs_ = xinb[ci % NXS]
            tix = tok0 // 128
            P.op("act", lambda e, xs_=xs_, tix=tix: e.activation(out=xnb[:], in_=xs_[:], func=AF.Identity, scale=rstd_all[:, tix:tix + 1]),
                 reads=[xs_.buf, rstd_all.buf], writes=[xnb.buf])
            stage_A_post(xnb, A_m, Bv_m, sample, hTb[:], hTb.buf, ycomb if sample else None)
            yield
            for k in range(8):
                P.op("pe", lambda e, k=k: e.matmul(bank[1][:, 0:16], lhsT=hTb[:, k, :], rhs=Wssm[:, k, 3072:3088],
                                                   start=(k == 0), stop=(k == 7)),
                     reads=[wssm_bufs[6], hTb.buf], writes=[pb1f])
            P.op("dve", lambda e: e.tensor_tensor(out=s16(DTR), in0=bank[1][:, 0:16], in1=cpk[:, _CP["dtb"]:_CP["dtb"] + 16], op=ALU.add),
                 reads=[pb1f, cpk.buf], writes=[sm16.buf])
            P.op("act", lambda e: e.activation(out=s16(EXPX), in_=s16(DTR), func=AF.Exp), reads=[sm16.buf], writes=[sm16.buf])
            P.op("act", lambda e: e.activation(out=s16(DT), in_=s16(EXPX), func=AF.Ln, bias=1.0), reads=[sm16.buf], writes=[sm16.buf])
            P.op("dve", lambda e: e.tensor_tensor(out=s16(DTA), in0=s16(DT), in1=Abc[:], op=ALU.mult),
                 reads=[sm16.buf, Abc.buf], writes=[sm16.buf])
            P.op("dve", lambda e: e.tensor_copy(out=dhl[:, 0, :], in_=s16(DTA)), reads=[sm16.buf], writes=[dhl.buf])
            P.op("dve", lambda e: e.tensor_tensor(out=s16(TMP), in0=s16(DTA), in1=dhl[:, 0, :], op=ALU.subtract),
                 reads=[sm16.buf, dhl.buf], writes=[sm16.buf])
            P.op("dve", lambda e: e.tensor_copy(out=dhl[:, 1, :], in_=s16(TMP)), reads=[sm16.buf], writes=[dhl.buf])
            trm = triBD_b if sample else tri_b
            onm = same_b if sample else ones_b[:]
            for hl in range(2):
                P.op("pe", lambda e, hl=hl, trm=trm: e.matmul(bank[1][:, 16:32], lhsT=trm, rhs=dhl[:, hl, :], start=(hl == 0), stop=(hl == 1)),
                     reads=[dhl.buf, cb.buf], writes=[pb1f])
            for hl in range(2):
                P.op("pe", lambda e, hl=hl, onm=onm: e.matmul(bank[1][:, 32:48], lhsT=onm, rhs=dhl[:, hl, :], start=(hl == 0), stop=(hl == 1)),
                     reads=[dhl.buf, cb.buf, ones_b.buf], writes=[pb1f])
            P.op("dve", lambda e: e.tensor_scalar(out=s16(NAC), in0=bank[1][:, 16:32], scalar1=-1.0, scalar2=None, op0=ALU.mult),
                 reads=[pb1f], writes=[sm16.buf])
            P.op("act", lambda e: e.activation(out=s16(EAC), in_=bank[1][:, 16:32], func=AF.Exp), reads=[pb1f], writes=[sm16.buf])
            P.op("dve", lambda e: e.tensor_tensor(out=s16(DLT), in0=bank[1][:, 32:48], in1=s16(NAC), op=ALU.add),
                 reads=[pb1f, sm16.buf], writes=[sm16.buf])
            P.op("act", lambda e: e.activation(out=s16(DTE), in_=s16(DLT), func=AF.Exp), reads=[sm16.buf], writes=[sm16.buf])
            if not sample:
                P.op("act", lambda e: e.activation(out=s16(DEC), in_=bank[1][:, 32:48], func=AF.Exp), reads=[pb1f], writes=[sm16.buf])
            yield

            for qi, q in enumerate((0, 1, 2, 3, 4, 5)):
                bi = 2 + qi % 2
                for jj in range(4):
                    for k in range(8):
                        P.op("pe", lambda e, q=q, jj=jj, k=k, bi=bi: e.matmul(
                            bank[bi][:, jj * 128:(jj + 1) * 128], lhsT=Wssm[:, k, q * 512 + jj * 128:q * 512 + (jj + 1) * 128],
                            rhs=hTb[:, k, :], start=(k == 0), stop=(k == 7)),
                            reads=[wssm_bufs[q], hTb.buf], writes=[pb[bi]])
                bv = bank[bi].rearrange("p (a t) -> p a t", a=4)
                if q < 2:
                    thv = ext[q % 2][:, :, 0:128]
                    P.op("act", lambda e, bv=bv, thv=thv: e.activation(out=thv, in_=bv, func=AF.Tanh, scale=0.5),
                         reads=[pb[bi]], writes=[ext[q % 2].buf])
                    P.op("dve", lambda e, q=q, bv=bv, thv=thv, szT=szT: e.scalar_tensor_tensor(
                        out=szT[:, q * 4:(q + 1) * 4, :], in0=thv, scalar=1.0, in1=bv, op0=ALU.add, op1=ALU.mult),
                        reads=[ext[q % 2].buf, pb[bi]], writes=[szT.buf])
                    yield
                    continue
                jq = q - 2
                j0 = jq * 4
                e_ = ext[jq % 2]
                c_ = cv[jq % 2]
                on_pool = False
                ceng = "pool" if on_pool else "dve"
                wt_ = ptmp if on_pool else xnb
                if not sample:
                    e3 = e_[:, :, 0:131]
                    P.op("pool", lambda e, e3=e3, j0=j0: e.tensor_copy(out=e3[:, :, 0:3], in_=xhist[:, j0:j0 + 4, :]),
                         reads=[xhist.buf], writes=[e_.buf])
                    P.op("act", lambda e, e3=e3, bv=bv: e.activation(out=e3[:, :, 3:131], in_=bv, func=AF.Copy),
                         reads=[pb[bi]], writes=[e_.buf], nowaw=True)
                    conv_wide(ceng, wt_, c_, [e3[:, :, r:r + 128] for r in range(4)], [128, 4, 128], j0, e_.buf)
                    P.op("pool", lambda e, e3=e3, j0=j0: e.tensor_copy(out=xhist[:, j0:j0 + 4, :], in_=e3[:, :, 128:131]),
                         reads=[e_.buf], writes=[xhist.buf])
                else:
                    e4 = e_[:].rearrange("p a (b c) -> p a b c", b=16)
                    P.op("pool", lambda e, e4=e4, j0=j0: e.tensor_copy(out=e4[:, :, :, 0:3], in_=xcst[:, j0:j0 + 4, :, :]),
                         reads=[xcst.buf], writes=[e_.buf])
                    P.op("act", lambda e, e4=e4, bi=bi: e.activation(
                        out=e4[:, :, :, 3:11], in_=bank[bi].rearrange("p (a b l) -> p a b l", a=4, b=16), func=AF.Copy),
                        reads=[pb[bi]], writes=[e_.buf], nowaw=True)
                    conv_wide(ceng, wt_, c_, [e4[:, :, :, r:r + 8] for r in range(4)], [128, 4, 16, 8], j0, e_.buf)
                    P.op("pool", lambda e, e4=e4, j0=j0: e.tensor_copy(out=nxcs[:, j0:j0 + 4, :, :], in_=e4[:, :, :, 8:11]),
                         reads=[e_.buf], writes=[nxcs.buf])
                if jq < 2:
                    dst, dbuf = xsT[:, j0:j0 + 4, :], xsT.buf
                elif jq == 2:
                    dst, dbuf = BT[:], BT.buf
                else:
                    dst, dbuf = CT[:], CT.buf
                thv = e_[:, :, 0:128]
                P.op("act", lambda e, c_=c_, thv=thv: e.activation(out=thv, in_=c_[:], func=AF.Tanh),
                     reads=[c_.buf], writes=[e_.buf])
                P.op("dve", lambda e, c_=c_, dst=dst, thv=thv: e.scalar_tensor_tensor(
                    out=dst, in0=thv, scalar=1.0, in1=c_[:], op0=ALU.add, op1=ALU.mult),
                    reads=[e_.buf, c_.buf], writes=[dbuf])
                yield

        def back(ci):
            tok0, sample, cidx = chunk_list[ci]
            first = (not sample) and cidx == 0
            S = bsets[0] if sample else bsets[cidx % 2]
            xsT, BT, CT, szT, sm16, dhl = S["xsT"], S["BT"], S["CT"], S["szT"], S["sm16"], S["dhl"]

            def s16(i):
                return sm16[:, i, :]

            xs_ = xinb[ci % NXS]
            trm = triBD_b if sample else tri_b
            for j in range(8):
                P.op("pe", lambda e, j=j: e.transpose(bankbf[4][:, j * 128:(j + 1) * 128], xsT[:, j, :], ident_b),
                     reads=[xsT.buf, cb.buf], writes=[pb[4]])
            if ci == 1:
                cp(70)
            for g in range(4):
                P.op("pe", lambda e, g=g: e.transpose(b1tok[:, g * 128:(g + 1) * 128], BT[:, g, :], ident_b),
                     reads=[BT.buf, cb.buf], writes=[pb[5]])
            if ci == 1:
                cp(71)
            P.op("dve", lambda e: e.tensor_tensor(out=xdt[:].rearrange("p (h q) -> p h q", h=16),
                                                  in0=bankbf[4].rearrange("p (h q) -> p h q", h=16),
                                                  in1=bc(s16(DT).unsqueeze(2), [128, 16, 64]), op=ALU.mult),
                 reads=[pb[4], sm16.buf], writes=[xdt.buf])
            P.op("pool", lambda e: e.tensor_tensor(out=xdtd[:].rearrange("p (h q) -> p h q", h=16),
                                                   in0=xdt[:].rearrange("p (h q) -> p h q", h=16),
                                                   in1=bc(s16(DTE).unsqueeze(2), [128, 16, 64]), op=ALU.mult),
                 reads=[xdt.buf, sm16.buf], writes=[xdtd.buf])
            P.op("act", lambda e: e.activation(out=Btok[:], in_=b1tok, func=AF.Copy), reads=[pb[5]], writes=[Btok.buf])
            yield
            for g in range(4):
                P.op("pe", lambda e, g=g: e.matmul(bank[5][:, g * 128:(g + 1) * 128], lhsT=BT[:, g, :], rhs=CT[:, g, :], start=True, stop=True),
                     reads=[BT.buf, CT.buf], writes=[pb[5]])
            P.op("act", lambda e: e.activation(out=smk[:], in_=bank[5].rearrange("p (g l) -> p g l", g=4), func=AF.Copy),
                 reads=[pb[5]], writes=[smk.buf])
            if (not sample) and (not first):
                for g in range(4):
                    bi = 6 + g // 2
                    P.op("pe", lambda e, g=g, bi=bi: e.matmul(bank[bi][:, (g % 2) * 256:(g % 2) * 256 + 256], lhsT=CT[:, g, :],
                                                              rhs=STb[:, g * 256:(g + 1) * 256], start=True, stop=True),
                         reads=[CT.buf, STb.buf], writes=[pb[bi]])
            yield
            ngm = negmBD_b if sample else negm_b
            ntm = ntri[:, 1, :] if sample else ntri[:, 0, :]
            for g in range(4):
                bi = 4 + g % 2
                for r in range(4):
                    h = 4 * g + r
                    osl = bank[bi][:, r * 128:(r + 1) * 128]
                    for hl in range(2):
                        P.op("pe", lambda e, osl=osl, hl=hl, h=h, trm=trm: e.matmul(
                            osl, lhsT=bc(dhl[:, hl, h:h + 1], [128, 128]), rhs=trm, start=(hl == 0), stop=False),
                            reads=[dhl.buf, cb.buf], writes=[pb[bi]])
                    for hl in range(2):
                        P.op("pe", lambda e, osl=osl, hl=hl, h=h, ntm=ntm: e.matmul(
                            osl, lhsT=ntm, rhs=bc(dhl[:, hl, h:h + 1], [128, 128]), start=False, stop=False),
                            reads=[dhl.buf, ntri.buf], writes=[pb[bi]])
                    P.op("pe", lambda e, osl=osl, ngm=ngm: e.matmul(osl, lhsT=ident_b, rhs=ngm, start=False, stop=True),
                         reads=[cb.buf], writes=[pb[bi]])
                L_ = Lsb[g % 2]
                P.op("act", lambda e, L_=L_, bi=bi: e.activation(out=L_[:], in_=bank[bi].rearrange("p (r l) -> p r l", r=4), func=AF.Exp),
                     reads=[pb[bi]], writes=[L_.buf])
                P.op("dve", lambda e, L_=L_, g=g: e.tensor_tensor(out=Mt[:, 4 * g:4 * g + 4, :], in0=L_[:],
                                                                   in1=bc(smk[:, g, :].unsqueeze(1), [128, 4, 128]), op=ALU.mult),
                     reads=[L_.buf, smk.buf], writes=[Mt.buf])
                yield
            for j in range(8):
                bi = 4 + j // 4
                col = (j % 4) * 128
                P.op("pe", lambda e, j=j, bi=bi, col=col: e.matmul(bank[bi][:, col:col + 128], lhsT=xsT[:, j, :], rhs=diagD[:, j, :],
                                                                   start=True, stop=False),
                     reads=[xsT.buf, diagD.buf], writes=[pb[bi]])
                for hh in range(2):
                    h = 2 * j + hh
                    P.op("pe", lambda e, h=h, bi=bi, col=col, hh=hh: e.matmul(
                        bank[bi][:, col + hh * 64:col + (hh + 1) * 64], lhsT=Mt[:, h, :], rhs=xdt[:, h * 64:(h + 1) * 64],
                        start=False, stop=(hh == 1)),
                        reads=[Mt.buf, xdt.buf], writes=[pb[bi]])
            yd = pair(4)
            yo = pair(6)
            if first:
                P.op("dve", lambda e: e.tensor_copy(out=ycomb[:], in_=yd), reads=[pb[4], pb[5]], writes=[ycomb.buf])
            elif not sample:
                P.op("dve", lambda e: e.tensor_tensor(out=ycomb[:].rearrange("p (h q) -> p h q", h=16),
                                                      in0=yo.rearrange("p (h q) -> p h q", h=16),
                                                      in1=bc(s16(EAC).unsqueeze(2), [128, 16, 64]), op=ALU.mult),
                     reads=[pb[6], pb[7], sm16.buf], writes=[ycomb.buf])
                P.op("dve", lambda e: e.tensor_tensor(out=ycomb[:], in0=ycomb[:], in1=yd, op=ALU.add),
                     reads=[ycomb.buf, pb[4], pb[5]], writes=[ycomb.buf])
            else:
                for hl in range(2):
                    P.op("dve", lambda e, hl=hl: e.tensor_copy(out=dtAx[:, hl, :].rearrange("p (h q) -> p h q", h=16),
                                                               in_=bc(dhl[:, hl, :].unsqueeze(2), [128, 16, 64])),
                         reads=[dhl.buf, Mt.buf], writes=[dtAx.buf])
                selb = A.alias(Bm, [128, 16], BF16)
                P.op("dve", lambda e: e.tensor_copy(out=selb[:], in_=cpk[:, _CP["seqsel"]:_CP["seqsel"] + 16]),
                     reads=[cpk.buf], writes=[selb.buf])
                for jp in range(8):
                    for hl in range(2):
                        P.op("pe", lambda e, jp=jp, hl=hl: e.matmul(bank[1][:, 64 + jp * 16:64 + (jp + 1) * 16],
                                                                    lhsT=dtAx[:, hl, jp * 128:(jp + 1) * 128], rhs=selb[:],
                                                                    start=(hl == 0), stop=(hl == 1)),
                             reads=[dtAx.buf, selb.buf], writes=[pb1b])
                P.op("act", lambda e: e.activation(out=decT[:], in_=bank[1][:, 64:192].rearrange("p (j b) -> p j b", j=8), func=AF.Exp),
                     reads=[pb1b], writes=[decT.buf])
                for bi in (6, 7):
                    P.op("pe", lambda e, bi=bi: e.matmul(bank[bi], lhsT=zero_b[:], rhs=cb[:, 0:512], start=True, stop=False),
                         reads=[zero_b.buf, cb.buf], writes=[pb[bi]])

                def load_s0(b):
                    if b < NSEQ:
                        P.dma("sp", S0n[b % 3][:], sst_d[b].rearrange("(j p) n -> p j n", p=128), ch_s0[b % 3], writes=S0n_bufs[b % 3])

                load_s0(0)
                load_s0(1)
                for b in range(NSEQ):
                    sn_ = S0n[b % 3]
                    snb = S0n_bufs[b % 3]
                    cm = CTm[b % 2]
                    if b >= 1:
                        P.dma("sp", nsts_d[b - 1].rearrange("(j p) n -> p j n", p=128), S0n[(b - 1) % 3][:], ch_s0[(b - 1) % 3],
                              reads=S0n_bufs[(b - 1) % 3])
                    load_s0(b + 2)
                    if b >= 2:
                        P.op("pool", lambda e, cm=cm, b=b: e.memset(cm[:, :, (b - 2) * 8:(b - 1) * 8], 0.0), writes=[cm.buf])
                    P.op("pool", lambda e, cm=cm, b=b: e.tensor_copy(out=cm[:, :, b * 8:(b + 1) * 8], in_=CT[:, :, b * 8:(b + 1) * 8]),
                         reads=[CT.buf], writes=[cm.buf])
                    P.op("act", lambda e, sn_=sn_: e.activation(out=S0b[:], in_=sn_[:].rearrange("p j n -> p (j n)"), func=AF.Copy),
                         reads=snb, writes=[S0b.buf])
                    for jp in range(8):
                        P.op("pe", lambda e, jp=jp: e.transpose(bankbf[0][:, jp * 128:(jp + 1) * 128], S0b[:, jp * 128:(jp + 1) * 128], ident_b),
                             reads=[S0b.buf, cb.buf], writes=[pb[0]])
                    P.op("act", lambda e: e.activation(out=S0T[:], in_=bankbf[0], func=AF.Copy), reads=[pb[0]], writes=[S0T.buf])
                    for g in range(4):
                        bi = 6 + g // 2
                        P.op("pe", lambda e, g=g, bi=bi, cm=cm, b=b: e.matmul(
                            bank[bi][:, (g % 2) * 256:(g % 2) * 256 + 256], lhsT=cm[:, g, :], rhs=S0T[:, g * 256:(g + 1) * 256],
                            start=False, stop=(b == NSEQ - 1 and g % 2 == 1)),
                            reads=[cm.buf, S0T.buf], writes=[pb[bi]])
                    P.op("dve", lambda e, b=b: e.tensor_scalar(out=Bm[:], in0=Btok[:], scalar1=cpc("seqsel", b), scalar2=None, op0=ALU.mult),
                         reads=[Btok.buf, cpk.buf, selb.buf], writes=[Bm.buf])
                    for jp in range(8):
                        bi = 2 + jp // 4
                        gq = jp // 2
                        P.op("pe", lambda e, jp=jp, bi=bi, gq=gq: e.matmul(
                            bank[bi][:, (jp % 4) * 128:(jp % 4 + 1) * 128], lhsT=xdtd[:, jp * 128:(jp + 1) * 128],
                            rhs=Bm[:, gq * 128:(gq + 1) * 128], start=True, stop=True),
                            reads=[xdtd.buf, Bm.buf], writes=[pb[bi]])
                    for jp in range(8):
                        bi = 2 + jp // 4
                        P.op("dve", lambda e, jp=jp, bi=bi, sn_=sn_, b=b: e.scalar_tensor_tensor(
                            out=sn_[:, jp, :], in0=sn_[:, jp, :], scalar=decT[:, jp, b:b + 1],
                            in1=bank[bi][:, (jp % 4) * 128:(jp % 4 + 1) * 128], op0=ALU.mult, op1=ALU.add),
                            reads=snb + [decT.buf, pb[bi]], writes=snb)
                P.dma("sp", nsts_d[NSEQ - 1].rearrange("(j p) n -> p j n", p=128), S0n[(NSEQ - 1) % 3][:], ch_s0[(NSEQ - 1) % 3],
                      reads=S0n_bufs[(NSEQ - 1) % 3])
                P.op("dve", lambda e: e.tensor_tensor(out=ycomb[:].rearrange("p (h q) -> p h q", h=16),
                                                      in0=yo.rearrange("p (h q) -> p h q", h=16),
                                                      in1=bc(s16(EAC).unsqueeze(2), [128, 16, 64]), op=ALU.mult),
                     reads=[pb[6], pb[7], sm16.buf], writes=[ycomb.buf])
                P.op("dve", lambda e: e.tensor_tensor(out=ycomb[:], in0=ycomb[:], in1=yd, op=ALU.add),
                     reads=[ycomb.buf, pb[4], pb[5]], writes=[ycomb.buf])
            yield
            for j in range(8):
                P.op("pe", lambda e, j=j: e.transpose(bankbf[6][:, j * 128:(j + 1) * 128], szT[:, j, :], ident_b),
                     reads=[szT.buf, cb.buf], writes=[pb[6]])
            P.op("dve", lambda e: e.tensor_tensor(out=ycomb[:], in0=ycomb[:], in1=bankbf[6], op=ALU.mult),
                 reads=[ycomb.buf, pb[6]], writes=[ycomb.buf])
            for g in range(4):
                P.op("act", lambda e, g=g: e.activation(out=ygn[:, g * 256:(g + 1) * 256], in_=ycomb[:, g * 256:(g + 1) * 256],
                                                        func=AF.Square, accum_out=ss4[:, g:g + 1]),
                     reads=[ycomb.buf], writes=[ygn.buf, ss4.buf])
            rstd_pool(ss4, 0, 4, 256, eps=4.0 * EPS)
            P.op("pool", lambda e: e.tensor_tensor(out=ygn[:].rearrange("p (g q) -> p g q", g=4),
                                                   in0=ycomb[:].rearrange("p (g q) -> p g q", g=4),
                                                   in1=bc(ss4[:, 8:12].unsqueeze(2), [128, 4, 256]), op=ALU.mult),
                 reads=[ycomb.buf, ss4.buf], writes=[ygn.buf])
            yield
            if not sample:
                st_ = pair(4)
                for g in range(4):
                    bi = 4 + g // 2
                    P.op("pe", lambda e, g=g, bi=bi: e.matmul(bank[bi][:, (g % 2) * 256:(g % 2) * 256 + 256],
                                                              lhsT=Btok[:, g * 128:(g + 1) * 128], rhs=xdtd[:, g * 256:(g + 1) * 256],
                                                              start=True, stop=True),
                         reads=[Btok.buf, xdtd.buf], writes=[pb[bi]])
                if first:
                    P.op("dve", lambda e: e.tensor_copy(out=ST[:], in_=st_), reads=[pb[4], pb[5]], writes=[ST.buf])
                else:
                    P.op("pool", lambda e: e.tensor_tensor(out=ST[:].rearrange("p (h q) -> p h q", h=16),
                                                           in0=ST[:].rearrange("p (h q) -> p h q", h=16),
                                                           in1=bc(s16(DEC).unsqueeze(2), [128, 16, 64]), op=ALU.mult),
                         reads=[ST.buf, sm16.buf], writes=[ST.buf])
                    P.op("dve", lambda e: e.tensor_tensor(out=ST[:], in0=ST[:], in1=st_, op=ALU.add),
                         reads=[ST.buf, pb[4], pb[5]], writes=[ST.buf])
                if cidx < NCH - 1:
                    P.op("act", lambda e: e.activation(out=STb[:], in_=ST[:], func=AF.Copy), reads=[ST.buf], writes=[STb.buf])
            for j in range(8):
                P.op("pe", lambda e, j=j: e.transpose(bankbf[7][:, j * 128:(j + 1) * 128], ygn[:, j * 128:(j + 1) * 128], ident_b),
                     reads=[ygn.buf, cb.buf], writes=[pb[7]])
            for j in range(8):
                P.op("act", lambda e, j=j: e.activation(out=yssdT[:, j, :], in_=bankbf[7][:, j * 128:(j + 1) * 128],
                                                        func=AF.Identity, scale=cpc("ng", j)),
                     reads=[pb[7], cpk.buf], writes=[yssdT.buf])
            yield
            for half in range(2):
                bi = 6 + half
                for k in range(16):
                    if k < 8:
                        lh = y_scT[:, k, tok0:tok0 + 128]
                        rb = [y_scT.buf]
                    else:
                        lh = yssdT[:, k - 8, :]
                        rb = [yssdT.buf]
                    P.op("pe", lambda e, lh=lh, k=k, half=half, bi=bi: e.matmul(
                        bank[bi], lhsT=lh, rhs=Wout[:, k, half * 512:(half + 1) * 512], start=(k == 0), stop=(k == 15)),
                        reads=rb + [wout_bufs[k // 4]], writes=[pb[bi]])
                yield
            mix = pair(6)
            P.op("act", lambda e: e.activation(out=ygn[:], in_=mix, func=AF.Square, accum_out=sttb[:, 4:5]),
                 reads=[pb[6], pb[7]], writes=[ygn.buf, sttb.buf])
            rstd_pool(sttb, 4, 1, D)
            gm = gmS if sample else gmP
            P.op("dve", lambda e, gm=gm: e.tensor_tensor(out=ycomb[:], in0=mix, in1=gm[:], op=ALU.mult),
                 reads=[pb[6], pb[7], gm.buf], writes=[ycomb.buf])
            P.op("dve", lambda e, xs_=xs_: e.scalar_tensor_tensor(out=xs_[:], in0=ycomb[:], scalar=sttb[:, 6:7], in1=xs_[:],
                                                                  op0=ALU.mult, op1=ALU.add),
                 reads=[ycomb.buf, sttb.buf, xs_.buf], writes=[xs_.buf])
            P.dma("sp", x1s_d[tok0:tok0 + 128, :], xs_[:], ch_xinb[ci % NXS], reads=[xs_.buf])
            load_x1b(ci + NXS)
            if sample:
                P.dma("sp", nxcs_d[:, :], nxcs[:].rearrange("p j b r -> p (j b r)"), ch_nxcs, reads=[nxcs.buf])
            if (not sample) and cidx == NCH - 1:
                P.dma("sp", nxcp_d[:, :], xhist[:].rearrange("p j r -> p (j r)"), ch_nxcp, reads=[xhist.buf])
                for jp in range(8):
                    bi = 4 + jp // 4
                    P.op("pe", lambda e, jp=jp, bi=bi: e.transpose(bank[bi][:, (jp % 4) * 128:(jp % 4 + 1) * 128],
                                                                   ST[:, jp * 128:(jp + 1) * 128], ident_f),
                         reads=[ST.buf, cpk.buf], writes=[pb[bi]])
                P.op("dve", lambda e: e.tensor_copy(out=ycomb[:], in_=pair(4)), reads=[pb[4], pb[5]], writes=[ycomb.buf])
                P.dma("sp", nstp_d[:, :].rearrange("(j p) n -> p j n", p=128), ycomb[:].rearrange("p (j n) -> p j n", j=8),
                      ch_nstp, reads=[ycomb.buf])
            yield

        def drain(g, cpbase=None):
            for i, _ in enumerate(g):
                if cpbase is not None:
                    cp(cpbase + i)

        def interleave(ga, gb, pattern="BFBBFBBFBBFBFBFBBFBF"):
            gens = {"B": ga, "F": gb}
            for ch in pattern:
                g = gens.get(ch)
                if g is None:
                    continue
                try:
                    next(g)
                except StopIteration:
                    gens[ch] = None
            for g in gens.values():
                if g is not None:
                    for _ in g:
                        pass

        for n in range(NXS):
            load_x1b(n)
        drain(front(0))
        cp(50)
        drain(back(0))
        cp(51)
        drain(front(1))
        cp(52)
        import os
        noil = os.environ.get("KNOIL", "0")
        for ci in range(1, len(chunk_list)):
            if noil == "1":
                drain(back(ci), 60 if ci == 1 else None)
                if ci == 1:
                    cp(55)
                if ci + 1 < len(chunk_list):
                    drain(front(ci + 1))
            else:
                interleave(back(ci), front(ci + 1) if ci + 1 < len(chunk_list) else None)
            if ci == 1:
                cp(53)
            if ci == 2:
                cp(54)

        P.barrier()
        A.off = mark_p2
        if stop == 2:
            P.emit(nc, final_chans=out_chans)
            return nc, P

        Wg = A.alloc("Wg", [128, 8, HID], BF16)
        Wu = A.alloc("Wu", [128, 8, HID], BF16)
        Wd = A.alloc("Wd", [128, NHT, 1024], BF16)
        ch_wg = [P.chan("wg%d" % i) for i in range(4)]
        ch_wu = [P.chan("wu%d" % i) for i in range(4)]
        ch_wd = [P.chan("wd%d" % i) for i in range(4)]
        wg_b = [Buf("wg%d" % i) for i in range(4)]
        wu_b = [Buf("wu%d" % i) for i in range(4)]
        wd_b = [Buf("wd%d" % i) for i in range(4)]
        wd_split = [(0, 6), (6, 12), (12, 17), (17, 22)]
        for i in range(4):
            P.dma("pool", Wg[:, :, i * 704:(i + 1) * 704], wg_d[i].rearrange("p (k c) -> p k c", k=8), ch_wg[i], writes=[wg_b[i]])
            P.dma("pool", Wu[:, :, i * 704:(i + 1) * 704], wu_d[i].rearrange("p (k c) -> p k c", k=8), ch_wu[i], writes=[wu_b[i]])
        for i, (a0, a1) in enumerate(wd_split):
            P.dma("pool", Wd[:, a0:a1, :], wd_d[a0 * 128:a1 * 128, :].rearrange("(i p) n -> p i n", p=128), ch_wd[i], writes=[wd_b[i]])

        def wd_buf(i):
            for n, (a0, a1) in enumerate(wd_split):
                if a0 <= i < a1:
                    return wd_b[n]

        gfP = A.alloc("gfP", [128, 1024])
        gfS = A.alloc("gfS", [128, 1024])
        ch_gf = [P.chan("gfload%d" % i) for i in range(2)]
        P.dma("sp", gfP[:], gsave_d[0], ch_gf[0], writes=[gfP.buf])
        P.dma("sp", gfS[:], gsave_d[1], ch_gf[1], writes=[gfS.buf])
        xf = [[A.alloc("xf%d_%d" % (s, t), [128, 1024]) for t in range(2)] for s in range(2)]
        ch_xf = [[P.chan("xf%d_%d" % (s, t)) for t in range(2)] for s in range(2)]
        out_chans += [c for r in ch_xf for c in r]
        xn2 = A.alloc("xn2", [128, 1024], BF16)
        h2T = A.alloc("h2T", [128, 8, 256], BF16)
        aT = A.alloc("aT", [128, NHT, 256], BF16)
        sg = [A.alloc("sg%d" % i, [128, 256]) for i in range(2)]
        ftmp = A.alloc("ftmp", [128, 1024])

        tiles = [(t * 256, 256, False) for t in range(8)] + [(LP, 128, True)]

        def load_x2(n):
            if n < len(tiles):
                t0, ts, _ = tiles[n]
                for tt in range(ts // 128):
                    P.dma("sp", xf[n % 2][tt][:], x1s_d[t0 + tt * 128:t0 + (tt + 1) * 128, :], ch_xf[n % 2][tt], writes=[xf[n % 2][tt].buf])

        h2Ts = [h2T, A.alloc("h2T1", [128, 8, 256], BF16)]

        xn2s = [xn2, A.alloc("xn2b", [128, 1024], BF16)]
        stq2 = [A.alloc("stq2_%d" % i, [128, 4]) for i in range(2)]

        def stage_A2_pre(ti):
            t0_, ts_, smp_ = tiles[ti]
            for tt in range(ts_ // 128):
                stage_A_pre(xf[ti % 2][tt], xn2s[tt], stq2[tt])

        def stage_A2_post(ti):
            t0_, ts_, smp_ = tiles[ti]
            for tt in range(ts_ // 128):
                stage_A_post(xn2s[tt], A_f, Bv_f, smp_, h2Ts[ti % 2][:, :, tt * 128:(tt + 1) * 128], h2Ts[ti % 2].buf, ftmp)

        def stage_A2(ti):
            stage_A2_pre(ti)
            stage_A2_post(ti)

        load_x2(0)
        stage_A2(0)
        for ti, (tok0, TS, sample) in enumerate(tiles):
            nt = TS // 128
            sl = ti % 2
            h2c = h2Ts[ti % 2]
            load_x2(ti + 1)
            for i in range(NHT):
                bi = 1 + i % 3
                wq = i * 128 // 704
                wq2 = (i * 128 + 127) // 704
                for k in range(8):
                    P.op("pe", lambda e, i=i, k=k, bi=bi, TS=TS, h2c=h2c: e.matmul(bank[bi][:, 0:TS], lhsT=Wg[:, k, i * 128:(i + 1) * 128],
                                                                                   rhs=h2c[:, k, 0:TS], start=(k == 0), stop=(k == 7)),
                         reads=[wg_b[wq], wg_b[wq2], h2c.buf], writes=[pb[bi]])
                for k in range(8):
                    P.op("pe", lambda e, i=i, k=k, bi=bi, TS=TS, h2c=h2c: e.matmul(bank[bi][:, 256:256 + TS], lhsT=Wu[:, k, i * 128:(i + 1) * 128],
                                                                                   rhs=h2c[:, k, 0:TS], start=(k == 0), stop=(k == 7)),
                         reads=[wu_b[wq], wu_b[wq2], h2c.buf], writes=[pb[bi]])
                s_ = sg[i % 2]
                P.op("act", lambda e, s_=s_, bi=bi, TS=TS: e.activation(out=s_[:, 0:TS], in_=bank[bi][:, 0:TS], func=AF.Silu),
                     reads=[pb[bi]], writes=[s_.buf])
                P.op("dve", lambda e, s_=s_, bi=bi, i=i, TS=TS: e.tensor_tensor(out=aT[:, i, 0:TS], in0=s_[:, 0:TS],
                                                                                 in1=bank[bi][:, 256:256 + TS], op=ALU.mult),
                     reads=[s_.buf, pb[bi]], writes=[aT.buf])
                if i == 8 and ti + 1 < len(tiles):
                    stage_A2_pre(ti + 1)
            for tt in range(nt):
                for half in range(2):
                    bi = 4 + tt * 2 + half
                    for i in range(NHT):
                        P.op("pe", lambda e, i=i, tt=tt, half=half, bi=bi: e.matmul(
                            bank[bi], lhsT=aT[:, i, tt * 128:(tt + 1) * 128], rhs=Wd[:, i, half * 512:(half + 1) * 512],
                            start=(i == 0), stop=(i == NHT - 1)),
                            reads=[aT.buf, wd_buf(i)], writes=[pb[bi]])
            if ti + 1 < len(tiles):
                stage_A2_post(ti + 1)
            for tt in range(nt):
                fp_ = pair(4 + tt * 2)
                pbs = [pb[4 + tt * 2], pb[5 + tt * 2]]
                xt = xf[sl][tt]
                P.op("act", lambda e, fp_=fp_: e.activation(out=xn2[:], in_=fp_, func=AF.Square, accum_out=stt[:, 4:5]),
                     reads=pbs, writes=[xn2.buf, stt.buf])
                rstd_pool(stt, 4, 1, D)
                gf = gfS if sample else gfP
                P.op("dve", lambda e, fp_=fp_, gf=gf: e.tensor_tensor(out=ftmp[:], in0=fp_, in1=gf[:], op=ALU.mult),
                     reads=pbs + [gf.buf], writes=[ftmp.buf])
                P.op("dve", lambda e, xt=xt: e.scalar_tensor_tensor(out=xt[:], in0=ftmp[:], scalar=stt[:, 6:7], in1=xt[:],
                                                                    op0=ALU.mult, op1=ALU.add),
                     reads=[ftmp.buf, stt.buf, xt.buf], writes=[xt.buf])
                P.dma("sp", y_d[tok0 + tt * 128:tok0 + (tt + 1) * 128, :], xt[:], ch_xf[sl][tt], reads=[xt.buf])

        P.emit(nc, final_chans=out_chans)
    return nc, P


def _fm(v, ntile):
    return np.ascontiguousarray(np.asarray(v, np.float32).reshape(ntile, 128).T)


def _host_consts():
    idx = np.arange(128)
    ident = np.eye(128, dtype=np.float32)
    tri = (idx[:, None] <= idx[None, :]).astype(np.float32)
    negm = np.where(idx[None, :] < idx[:, None], NEG, 0.0).astype(np.float32)
    same = (idx[:, None] // LS == idx[None, :] // LS).astype(np.float32)
    triBD = tri * same
    negmBD = np.where(triBD > 0, 0.0, NEG).astype(np.float32)
    seqsel = (idx[:, None] // LS == np.arange(NSEQ)[None, :]).astype(np.float32)
    return ident, tri, negm, triBD, negmBD, same, seqsel


_CACHE = {}


def kernel(x_prompt, x_sample, c_prompt, c_sample, state_sc_conv, state_ssm_conv, state_ssm,
           w_ada, b_ada, g_mix_pre, g_mix_post, g_ffn_pre, g_ffn_post, w_in, sc_conv_w,
           ssm_conv_w, ssm_conv_b, dt_bias, a_log, d_skip, ssm_norm_g, w_out, w_gate, w_up, w_down):
    f32 = np.float32
    x_prompt = np.asarray(x_prompt, f32)
    x_sample = np.asarray(x_sample, f32)
    w_in_ = np.asarray(w_in, f32)[0]
    w_ada_ = np.asarray(w_ada, f32)[0]
    wada = np.ascontiguousarray(w_ada_.reshape(8, 128, 6, 1024).transpose(2, 1, 0, 3).reshape(6, 128, 8192))
    sc = w_in_[:, 0:3072].reshape(8, 128, 3, 8, 128)
    wsc = np.ascontiguousarray(sc[:, :, [1, 2, 0], :, :].transpose(3, 1, 0, 2, 4).reshape(8, 128, 8 * 384))
    ssm = w_in_[:, 3072:6144].reshape(8, 128, 6, 512)
    wssm = np.ascontiguousarray(ssm.transpose(2, 1, 0, 3).reshape(6, 128, 8 * 512))
    wdt = np.ascontiguousarray(w_in_[:, 6144:6160].reshape(8, 128, 16).transpose(1, 0, 2).reshape(128, 128))
    wout = np.ascontiguousarray(np.asarray(w_out, f32)[0])
    wg = np.ascontiguousarray(np.asarray(w_gate, f32)[0].reshape(8, 128, 4, 704).transpose(2, 1, 0, 3).reshape(4, 128, 8 * 704))
    wu = np.ascontiguousarray(np.asarray(w_up, f32)[0].reshape(8, 128, 4, 704).transpose(2, 1, 0, 3).reshape(4, 128, 8 * 704))
    wd = np.ascontiguousarray(np.asarray(w_down, f32)[0])
    badar = np.ascontiguousarray(np.asarray(b_ada, f32).reshape(1, 6144))
    gpostr = np.ascontiguousarray(np.stack([np.asarray(g_mix_post, f32)[0], np.asarray(g_ffn_post, f32)[0]]))

    ident, tri, negm, triBD, negmBD, same, seqsel = _host_consts()
    scw = np.asarray(sc_conv_w, f32)[0]
    xcw = np.asarray(ssm_conv_w, f32)[0]
    base = np.zeros((128, CPW), f32)

    def put(name, arr):
        arr = np.asarray(arr, f32)
        base[:, _CP[name]:_CP[name] + arr.shape[1]] = arr

    put("ident", ident); put("tri", tri); put("negm", negm); put("triBD", triBD); put("negmBD", negmBD)
    put("same", same); put("seqsel", seqsel)
    put("gpre", _fm(np.asarray(g_mix_pre, f32)[0], 8))
    put("gfpre", _fm(np.asarray(g_ffn_pre, f32)[0], 8))
    put("scw", scw.reshape(3, 8, 128).transpose(2, 1, 0).reshape(128, 24))
    put("xcw", xcw.reshape(4, 16, 128).transpose(2, 1, 0).reshape(128, 64))
    put("xcb", _fm(np.asarray(ssm_conv_b, f32)[0], 16))
    put("dtb", np.broadcast_to(np.asarray(dt_bias, f32).reshape(1, 16), (128, 16)))
    put("alog", np.broadcast_to(np.asarray(a_log, f32).reshape(1, 16), (128, 16)))
    put("dcol", _fm(np.repeat(np.asarray(d_skip, f32)[0], 64), 8))
    put("ng", _fm(np.asarray(ssm_norm_g, f32)[0], 8))
    put("bada", _fm(np.asarray(b_ada, f32)[0], 48))
    put("eps", np.full((128, 1), EPS, f32))

    in_maps = []
    for i in range(NCORES):
        cpk = base.copy()
        call = np.concatenate([np.asarray(c_prompt, f32)[i:i + 1], np.asarray(c_sample, f32)[16 * i:16 * i + 16]], axis=0)
        cpk[:, _CP["cT"]:_CP["cT"] + 136] = call.reshape(17, 8, 128).transpose(2, 1, 0).reshape(128, 136)
        xall = np.concatenate([x_prompt[i], x_sample[16 * i:16 * i + 16].reshape(128, D)], axis=0)
        scs = np.asarray(state_sc_conv, f32)[0, 16 * i:16 * i + 16]
        scst = scs.reshape(16, 2, 8, 128).transpose(3, 2, 0, 1).reshape(128, 256)
        xcs = np.asarray(state_ssm_conv, f32)[0, 16 * i:16 * i + 16]
        xcst = xcs.reshape(16, 3, 16, 128).transpose(3, 2, 0, 1).reshape(128, 768)
        sst = np.asarray(state_ssm, f32)[0, 16 * i:16 * i + 16].reshape(16, 1024, 128)
        in_maps.append({
            "xall": np.ascontiguousarray(xall), "cpk": cpk, "scst": np.ascontiguousarray(scst),
            "xcst": np.ascontiguousarray(xcst), "sst": np.ascontiguousarray(sst),
            "wada": wada, "badar": badar, "gpostr": gpostr, "wsc": wsc, "wssm": wssm, "wdt": wdt,
            "wout": wout, "wg": wg, "wu": wu, "wd": wd,
        })

    if "nc" not in _CACHE:
        import os
        _CACHE["nc"] = build_program(int(os.environ.get("KSTOP", "99")))
    nc, _ = _CACHE["nc"]
    import os
    ncr = int(os.environ.get("KCORES", str(NCORES)))
    res = run_bass_kernel_spmd(nc, in_maps[:ncr], core_ids=list(range(ncr)))
    R = list(res.results)
    while len(R) < NCORES:
        R.append(R[0])

    y_prompt = np.stack([R[i]["y"][0:LP] for i in range(NCORES)]).astype(f32)
    y_sample = np.concatenate([R[i]["y"][LP:].reshape(16, LS, D) for i in range(NCORES)], axis=0).astype(f32)
    nscp = np.stack([R[i]["nscp"].reshape(128, 8, 2).transpose(2, 1, 0).reshape(2, 1024) for i in range(NCORES)])[None]
    nxcp = np.stack([R[i]["nxcp"].reshape(128, 16, 3).transpose(2, 1, 0).reshape(3, 2048) for i in range(NCORES)])[None]
    nstp = np.stack([R[i]["nstp"].reshape(16, 64, 128) for i in range(NCORES)])[None]
    nscs = np.concatenate([R[i]["nscs"].reshape(128, 8, 16, 2).transpose(2, 3, 1, 0).reshape(16, 2, 1024) for i in range(NCORES)])[None]
    nxcs = np.concatenate([R[i]["nxcs"].reshape(128, 16, 16, 3).transpose(2, 3, 1, 0).reshape(16, 3, 2048) for i in range(NCORES)])[None]
    nsts = np.concatenate([R[i]["nsts"].reshape(16, 16, 64, 128) for i in range(NCORES)])[None]
    return (y_prompt, y_sample, np.ascontiguousarray(nscp, f32), np.ascontiguousarray(nxcp, f32),
            np.ascontiguousarray(nstp, f32), np.ascontiguousarray(nscs, f32), np.ascontiguousarray(nxcs, f32),
            np.ascontiguousarray(nsts, f32))
```

```python
import contextlib
import numpy as np
import concourse.bass as bass
import concourse.mybir as mybir
from concourse.bass_utils import run_bass_kernel_spmd

F32 = mybir.dt.float32
BF16 = mybir.dt.bfloat16
AF = mybir.ActivationFunctionType
ALU = mybir.AluOpType

NCORES = 8
D = 1024
LP = 2048
NSEQ = 16
LS = 8
NTOK = LP + NSEQ * LS
NCH = LP // 128
HID = 2816
NHT = HID // 128
EPS = 1e-6
NEG = -30000.0
XLAT = 0.7

_CP = {}
_off = 0
for _n, _w in [("ident", 128), ("tri", 128), ("negm", 128), ("triBD", 128), ("negmBD", 128),
               ("same", 128), ("seqsel", 16), ("gpre", 8), ("gfpre", 8), ("scw", 24),
               ("xcw", 64), ("xcb", 16), ("dtb", 16), ("alog", 16), ("dcol", 8), ("ng", 8),
               ("bada", 48), ("cT", 136), ("eps", 1)]:
    _CP[_n] = _off
    _off += _w
CPW = _off


class Buf:
    __slots__ = ("name", "writers", "readers", "excl", "gen_deps")

    def __init__(self, name, excl=False):
        self.name = name
        self.writers = []
        self.readers = []
        self.gen_deps = set()
        self.excl = excl


class Chan:
    def __init__(self, name):
        self.name = name
        self.sem = None
        self.n = 0
        self.last = None


class Op:
    __slots__ = ("eng", "fn", "deps", "idx", "marked", "chan", "seq", "cnt", "waits", "sdeps", "cost", "dlat", "site", "adeps")


class _FakeEng:
    def __getattr__(self, name):
        def f(*a, **k):
            out = k.get("out", a[0] if a else None)
            return (name, out, k)
        return f


def _est_cost(eng, fn, is_dma):
    try:
        name, out, k = fn(_FakeEng())
        n = 1
        for d in out.shape[1:]:
            n *= int(d)
        parts = int(out.shape[0])
    except Exception:
        return 0.3, 0.0
    if is_dma:
        return 0.15, 2.0 + n * parts * 4 / 250e3
    if eng == "pe":
        return max(0.058, (n + 12) / 2400.0), 0.0
    if eng == "act":
        return 0.17 + n / 1250.0 + (0.1 if k.get("accum_out") is not None else 0.0), 0.0
    if eng == "dve":
        return 0.19 + n / 960.0, 0.0
    if eng == "pool":
        return 0.2 + n / 560.0, 0.0
    return 0.2, 0.0


class Prog:
    ENGS = ("pe", "act", "dve", "pool", "sp")

    def __init__(self):
        self.ops = []
        self.chans = []

    def chan(self, name):
        c = Chan(name)
        self.chans.append(c)
        return c

    def _record(self, eng, fn, reads, writes, chan=None, extra=(), nowaw=False):
        op = Op()
        op.eng = eng
        op.fn = fn
        op.idx = len(self.ops)
        op.marked = False
        op.chan = chan
        op.seq = None
        op.cnt = None
        deps = set(extra)
        sdeps = set()
        for b in reads:
            deps.update(b.writers)
            if b.excl:
                for r in b.readers:
                    if self.ops[r].eng != eng:
                        deps.add(r)
        for b in writes:
            deps.update(b.readers)
            if nowaw and not b.readers:
                deps.update(b.gen_deps)
            for w in b.writers:
                wo = self.ops[w]
                if wo.eng != eng or wo.chan is not None or chan is not None:
                    if not nowaw:
                        deps.add(w)
                else:
                    sdeps.add(w)
        if chan is not None and getattr(chan, "last", None) is not None:
            sdeps.add(chan.last)
        if eng == "pe" and fn is not None:
            pp = {d for d in deps if (self.ops[d].eng == "pe" and self.ops[d].chan is None)}
            sdeps |= pp
            deps = deps - pp
        import sys as _sys
        op.site = _sys._getframe(2).f_lineno
        op.deps = deps
        op.adeps = set(deps)
        op.sdeps = sdeps | deps
        op.cost, op.dlat = (0.0, 0.0) if fn is None else _est_cost(eng, fn, chan is not None)
        if chan is not None:
            chan.n += 1
            op.seq = chan.n
            chan.last = op.idx
        self.ops.append(op)
        for b in reads:
            b.readers.append(op.idx)
        for b in writes:
            if b.readers:
                b.gen_deps = set(b.readers) | set(b.writers)
                b.writers = [op.idx]
                b.readers = []
            else:
                b.writers.append(op.idx)
        return op

    def op(self, eng, fn, reads=(), writes=(), nowaw=False):
        return self._record(eng, fn, reads, writes, nowaw=nowaw)

    def dma(self, eng, out, in_, chan, reads=(), writes=()):
        def fn(e, out=out, in_=in_):
            return e.dma_start(out=out, in_=in_)
        return self._record(eng, fn, reads, writes, chan=chan)

    def barrier(self):
        last = {}
        for op in self.ops:
            if op.fn is None:
                continue
            if op.chan is None:
                last[("e", op.eng)] = op.idx
            else:
                last[("c", id(op.chan))] = op.idx
        deps = set(last.values())
        for eng in self.ENGS:
            self._record(eng, None, (), (), extra=deps)

    def schedule(self):
        import heapq
        ops = self.ops
        order = []
        seg = []

        def flush():
            if not seg:
                return
            ids = set(o.idx for o in seg)
            preds = {o.idx: [d for d in o.sdeps if d in ids] for o in seg}
            succs = {o.idx: [] for o in seg}
            for o in seg:
                for d in preds[o.idx]:
                    succs[d].append(o.idx)

            def lat(d, o):
                do = ops[d]
                if do.chan is not None:
                    return do.dlat + 0.2
                return XLAT if do.eng != o.eng else 0.02

            prio = {}
            for o in reversed(seg):
                p = 0.0
                for sidx in succs[o.idx]:
                    q = lat(o.idx, ops[sidx]) + prio[sidx]
                    if q > p:
                        p = q
                prio[o.idx] = p + o.cost
            nun = {o.idx: len(preds[o.idx]) for o in seg}
            avail = {o.idx: 0.0 for o in seg}
            pending = {e: [] for e in self.ENGS}
            ready = {e: [] for e in self.ENGS}
            free = {e: 0.0 for e in self.ENGS}
            for o in seg:
                if nun[o.idx] == 0:
                    heapq.heappush(pending[o.eng], (0.0, -prio[o.idx], o.idx))
            start = {}
            left = len(seg)
            while left:
                best = None
                for e in self.ENGS:
                    while pending[e] and pending[e][0][0] <= free[e]:
                        a, np_, i = heapq.heappop(pending[e])
                        heapq.heappush(ready[e], (np_, i))
                    if ready[e]:
                        t = free[e]
                    elif pending[e]:
                        t = pending[e][0][0]
                    else:
                        continue
                    if best is None or t < best[0]:
                        best = (t, e)
                t, e = best
                while pending[e] and pending[e][0][0] <= t:
                    a, np_, i = heapq.heappop(pending[e])
                    heapq.heappush(ready[e], (np_, i))
                np_, i = heapq.heappop(ready[e])
                o = ops[i]
                start[i] = t
                fin = t + o.cost
                free[e] = fin
                left -= 1
                for sidx in succs[i]:
                    a = fin + lat(i, ops[sidx])
                    if a > avail[sidx]:
                        avail[sidx] = a
                    nun[sidx] -= 1
                    if nun[sidx] == 0:
                        heapq.heappush(pending[ops[sidx].eng], (avail[sidx], -prio[sidx], sidx))
            order.extend(sorted(seg, key=lambda o: (start[o.idx], o.idx)))
            self.sim_time = getattr(self, "sim_time", 0.0) + max(free.values())
            del seg[:]

        for op in ops:
            if op.fn is None:
                flush()
                order.append(op)
            else:
                seg.append(op)
        flush()
        self.order = order

    def emit(self, nc, final_chans=()):
        import os
        if os.environ.get("KSCHED", "1") == "1":
            self.schedule()
            ops_order = self.order
        else:
            ops_order = list(self.ops)
        allops = self.ops
        rank = {}
        for r, op in enumerate(ops_order):
            rank[op.idx] = r
        last = {}
        for op in ops_order:
            if op.fn is None:
                op.deps = set(last.values())
            elif op.chan is not None:
                last[("c", id(op.chan))] = op.idx
            else:
                last[("e", op.eng)] = op.idx
        ops = allops
        for op in ops_order:
            red = {}
            for d in op.deps:
                dop = ops[d]
                if dop.fn is None:
                    continue
                key = ("c", id(dop.chan)) if dop.chan is not None else ("e", dop.eng)
                if key not in red or rank[red[key]] < rank[d]:
                    red[key] = d
            op.deps = set(red.values())
            for d in op.deps:
                ops[d].marked = True
        with contextlib.ExitStack() as st:
            esem = {e: st.enter_context(nc.semaphore("s_" + e)) for e in self.ENGS}
            for c in self.chans:
                c.sem = st.enter_context(nc.semaphore("c_" + c.name))
            cnt = {e: 0 for e in self.ENGS}
            for op in ops_order:
                if op.chan is None and op.marked and op.fn is not None:
                    cnt[op.eng] += 1
                    op.cnt = cnt[op.eng]
            waited = {e: {} for e in self.ENGS}
            per_eng = {e: [] for e in self.ENGS}
            for op in ops_order:
                need = {}
                for d in op.deps:
                    dop = ops[d]
                    if dop.fn is None:
                        continue
                    if dop.chan is not None:
                        key, val = ("c", dop.chan), 16 * dop.seq
                    else:
                        key, val = ("e", dop.eng), dop.cnt
                    if need.get(key, 0) < val:
                        need[key] = val
                w = []
                for key, val in need.items():
                    if waited[op.eng].get(key, 0) >= val:
                        continue
                    waited[op.eng][key] = val
                    sem = key[1].sem if key[0] == "c" else esem[key[1]]
                    w.append((sem, val))
                op.waits = w
                per_eng[op.eng].append(op)
            self.stats = {e: len(per_eng[e]) for e in self.ENGS}
            self.stats["cnt"] = dict(cnt)
            with nc.Block() as block:
                def run(e, lst, is_sp=False):
                    for op in lst:
                        for sem, val in op.waits:
                            e.wait_ge(sem, val)
                        if op.fn is None:
                            continue
                        ins = op.fn(e)
                        if op.chan is not None:
                            ins.then_inc(op.chan.sem, 16)
                        elif op.marked:
                            ins.then_inc(esem[op.eng], 1)
                    if is_sp:
                        for c in final_chans:
                            if c.n:
                                e.wait_ge(c.sem, 16 * c.n)

                @block.tensor
                def _(e):
                    run(e, per_eng["pe"])

                @block.scalar
                def _(e):
                    run(e, per_eng["act"])

                @block.vector
                def _(e):
                    run(e, per_eng["dve"])

                @block.gpsimd
                def _(e):
                    run(e, per_eng["pool"])

                @block.sync
                def _(e):
                    run(e, per_eng["sp"], is_sp=True)


class _Stop(Exception):
    pass


class Tl:
    def __init__(self, ap, buf, off, nbytes):
        self.ap = ap
        self.buf = buf
        self.off = off
        self.nbytes = nbytes

    def __getitem__(self, k):
        return self.ap[k]


class SBA:
    def __init__(self, big, total_bytes):
        self.big = big
        self.total = total_bytes
        self.off = 0
        self.peak = 0

    def _view(self, off, shape, dt):
        nfree = 1
        for s in shape[1:]:
            nfree *= s
        esz = 4 if dt == F32 else 2
        nbytes = nfree * esz
        assert off % 4 == 0 and nbytes % 4 == 0, (off, nbytes)
        ap = self.big[0:shape[0], off // 4:(off + nbytes) // 4]
        if dt != F32:
            ap = ap.bitcast(dt)
        if len(shape) == 3:
            ap = ap.rearrange("p (a b) -> p a b", a=shape[1])
        elif len(shape) == 4:
            ap = ap.rearrange("p (a b c) -> p a b c", a=shape[1], b=shape[2])
        return ap, nbytes

    def alloc(self, name, shape, dt=F32):
        off = (self.off + 31) // 32 * 32
        ap, nbytes = self._view(off, shape, dt)
        self.off = off + nbytes
        self.peak = max(self.peak, self.off)
        assert self.off <= self.total, ("SBUF overflow", name, self.off, self.total)
        return Tl(ap, Buf(name), off, nbytes)

    def alias(self, t, shape, dt=F32, span=None):
        ap, nbytes = self._view(t.off, shape, dt)
        assert nbytes <= (span if span is not None else t.nbytes)
        return Tl(ap, t.buf, t.off, t.nbytes)


def bc(ap, shape):
    return ap.broadcast_to(list(shape))


def build_program(stop=99):
    try:
        return _build_inner(stop)
    except _Stop as e:
        return e.args


def _build_inner(stop=99):
    nc = bass.Bass("TRN2", target_bir_lowering=False)

    def din(name, shape):
        return nc.dram_tensor(name, list(shape), F32, kind="ExternalInput").ap()

    def dout(name, shape):
        return nc.dram_tensor(name, list(shape), F32, kind="ExternalOutput").ap()

    xall_d = din("xall", [NTOK, D])
    cpk_d = din("cpk", [128, CPW])
    scst_d = din("scst", [128, 8 * 16 * 2])
    xcst_d = din("xcst", [128, 16 * 16 * 3])
    sst_d = din("sst", [NSEQ, 1024, 128])
    wada_d = din("wada", [6, 128, 8 * 1024])
    badar_d = din("badar", [1, 6144])
    gpostr_d = din("gpostr", [2, 1024])
    wsc_d = din("wsc", [8, 128, 8 * 384])
    wssm_d = din("wssm", [6, 128, 8 * 512])
    wdt_d = din("wdt", [128, 8 * 16])
    wout_d = din("wout", [2048, 1024])
    wg_d = din("wg", [4, 128, 8 * 704])
    wu_d = din("wu", [4, 128, 8 * 704])
    wd_d = din("wd", [HID, 1024])

    y_d = dout("y", [NTOK, D])
    nscp_d = dout("nscp", [128, 16])
    nxcp_d = dout("nxcp", [128, 48])
    nstp_d = dout("nstp", [1024, 128])
    nscs_d = dout("nscs", [128, 256])
    nxcs_d = dout("nxcs", [128, 768])
    nsts_d = dout("nsts", [NSEQ, 1024, 128])

    x1s_d = nc.dram_tensor("x1s", [NTOK, D], F32).ap()
    gsave_d = nc.dram_tensor("gsave", [2, 128, 1024], F32).ap()

    P = Prog()
    SB_BYTES = 207 * 1024
    with contextlib.ExitStack() as es:
        big = es.enter_context(nc.sbuf_tensor("sbig", [128, SB_BYTES // 4], F32))
        ps = es.enter_context(nc.psum_tensor("ps", [128, 4096], F32))
        A = SBA(big, SB_BYTES)

        bank = [ps[:, i * 512:(i + 1) * 512] for i in range(8)]
        bankbf = [b.bitcast(BF16) for b in bank]
        pb = [Buf("pb%d" % i, excl=True) for i in range(8)]

        def pair(i):
            return ps[:, i * 512:(i + 2) * 512]

        out_chans = []

        def cp(n):
            if stop == n:
                P.barrier()
                P.emit(nc, final_chans=out_chans)
                raise _Stop(nc, P)

        cpk = A.alloc("cpk", [128, CPW])
        cb = A.alloc("cb", [128, 768], BF16)
        ones_b = A.alloc("ones_b", [128, 128], BF16)
        zero_b = A.alloc("zero_b", [128, 128], BF16)
        ntri = A.alloc("ntri", [128, 2, 128], BF16)
        diagD = A.alloc("diagD", [128, 8, 128], BF16)
        Abc = A.alloc("Abc", [128, 16])
        A_m = A.alloc("A_m", [128, 8, 17])
        Bv_m = A.alloc("Bv_m", [128, 8, 17])
        A_f = A.alloc("A_f", [128, 8, 17])
        Bv_f = A.alloc("Bv_f", [128, 8, 17])
        gmP = A.alloc("gmP", [128, 1024])
        gmS = A.alloc("gmS", [128, 1024])
        stt = A.alloc("stt", [128, 8])
        mhalf = A.alloc("mhalf", [128, 4])
        rstd_all = A.alloc("rstd_all", [128, 32])
        dt_all = A.alloc("dt_all", [128, 17, 16])
        cvw = A.alloc("cvw", [128, 80])
        mark_p2 = A.off
        y_scT = A.alloc("y_scT", [128, 8, NTOK], BF16)
        Wssm = A.alloc("Wssm", [128, 8, 3088], BF16)

        ident_b = cb[:, 0:128]
        tri_b = cb[:, 128:256]
        negm_b = cb[:, 256:384]
        triBD_b = cb[:, 384:512]
        negmBD_b = cb[:, 512:640]
        same_b = cb[:, 640:768]
        ident_f = cpk[:, _CP["ident"]:_CP["ident"] + 128]
        tri_f = cpk[:, _CP["tri"]:_CP["tri"] + 128]
        triBD_f = cpk[:, _CP["triBD"]:_CP["triBD"] + 128]
        epscol = cpk[:, _CP["eps"]:_CP["eps"] + 1]

        def cpc(name, i, n=1):
            return cpk[:, _CP[name] + i:_CP[name] + i + n]

        ch_c = P.chan("cpk")
        P.dma("sp", cpk[:], cpk_d[:, :], ch_c, writes=[cpk.buf])
        P.op("dve", lambda e: e.tensor_copy(out=cb[:], in_=cpk[:, 0:768]), reads=[cpk.buf], writes=[cb.buf])
        P.op("pool", lambda e: e.memset(ones_b[:], 1.0), writes=[ones_b.buf])
        P.op("pool", lambda e: e.memset(zero_b[:], 0.0), writes=[zero_b.buf])
        for i_, nm_ in enumerate(("tri", "triBD")):
            P.op("dve", lambda e, i_=i_, nm_=nm_: e.tensor_scalar(out=ntri[:, i_, :], in0=cpk[:, _CP[nm_]:_CP[nm_] + 128], scalar1=-1.0,
                                                               scalar2=None, op0=ALU.mult),
                 reads=[cpk.buf], writes=[ntri.buf])
        P.op("pool", lambda e: e.memset(mhalf[:], -0.5), writes=[mhalf.buf])
        P.op("dve", lambda e: e.tensor_scalar(out=cvw[:], in0=cpk[:, _CP["xcw"]:_CP["xcw"] + 80], scalar1=0.5, scalar2=None, op0=ALU.mult),
             reads=[cpk.buf], writes=[cvw.buf])
        P.op("act", lambda e: e.activation(out=Abc[:], in_=cpk[:, _CP["alog"]:_CP["alog"] + 16], func=AF.Exp),
             reads=[cpk.buf], writes=[Abc.buf])
        P.op("dve", lambda e: e.tensor_scalar(out=Abc[:], in0=Abc[:], scalar1=-1.0, scalar2=None, op0=ALU.mult),
             reads=[Abc.buf], writes=[Abc.buf])
        for j in range(8):
            P.op("dve", lambda e, j=j: e.tensor_scalar(out=diagD[:, j, :], in0=ident_b, scalar1=cpc("dcol", j),
                                                       scalar2=None, op0=ALU.mult),
                 reads=[cb.buf, cpk.buf], writes=[diagD.buf])

        ch_wssm = [P.chan("wssm%d" % q) for q in range(7)]

        mark0 = A.off
        wsl = [A.alloc("wsl%d" % i, [128, 8, 1024], BF16) for i in range(2)]
        ch_wsl = [P.chan("wsl%d" % i) for i in range(2)]
        siluT = A.alloc("siluT", [128, 8, 17], BF16)
        siluPx = A.alloc("siluPx", [128, 8, 128], BF16)
        siluSx = A.alloc("siluSx", [128, 8, 128], BF16)
        badaT = A.alloc("badaT", [128, 1024])
        gpostT = A.alloc("gpostT", [128, 1024])
        tmpT = A.alloc("tmpT", [128, 1024])
        gtmp = [A.alloc("gtmp%d" % i, [128, 1024]) for i in range(2)]
        mtmp = A.alloc("mtmp", [128, 8, 17])
        ch_bada = P.chan("bada")
        ch_gpost = P.chan("gpost")
        ch_gs = [P.chan("gsave%d" % i) for i in range(2)]

        P.op("act", lambda e: e.activation(out=siluT[:], in_=cpk[:, _CP["cT"]:_CP["cT"] + 136].rearrange("p (k s) -> p k s", k=8),
                                           func=AF.Silu), reads=[cpk.buf], writes=[siluT.buf])
        P.op("pool", lambda e: e.tensor_copy(out=siluPx[:], in_=bc(siluT[:, :, 0:1], [128, 8, 128])),
             reads=[siluT.buf], writes=[siluPx.buf])
        P.op("pool", lambda e: e.tensor_copy(out=siluSx[:].rearrange("p k (b l) -> p k b l", l=8),
                                             in_=bc(siluT[:, :, 1:17].unsqueeze(3), [128, 8, 16, 8])),
             reads=[siluT.buf], writes=[siluSx.buf])

        order_v = [1, 0, 2, 4, 3, 5]
        for vi, v in enumerate(order_v):
            s = vi % 2
            P.dma("pool", wsl[s][:], wada_d[v].rearrange("p (k c) -> p k c", k=8), ch_wsl[s], writes=[wsl[s].buf])
            if v in (0, 1, 3, 4):
                bi = vi % 2
                for ct in range(8):
                    for k in range(8):
                        P.op("pe", lambda e, s=s, ct=ct, k=k, bi=bi: e.matmul(
                            bank[bi][:, ct * 17:(ct + 1) * 17], lhsT=wsl[s][:, k, ct * 128:(ct + 1) * 128],
                            rhs=siluT[:, k, :], start=(k == 0), stop=(k == 7)),
                            reads=[wsl[s].buf, siluT.buf], writes=[pb[bi]])
                pv = bank[bi][:, 0:136].rearrange("p (c s) -> p c s", c=8)
                bb = bc(cpk[:, _CP["bada"] + v * 8:_CP["bada"] + v * 8 + 8].unsqueeze(2), [128, 8, 17])
                if v in (0, 3):
                    dst = Bv_m if v == 0 else Bv_f
                    P.op("dve", lambda e, pv=pv, bb=bb, dst=dst: e.tensor_tensor(out=dst[:], in0=pv, in1=bb, op=ALU.add),
                         reads=[pb[bi], cpk.buf], writes=[dst.buf])
                else:
                    dst = A_m if v == 1 else A_f
                    gn = "gpre" if v == 1 else "gfpre"
                    gb = bc(cpk[:, _CP[gn]:_CP[gn] + 8].unsqueeze(2), [128, 8, 17])
                    P.op("dve", lambda e, pv=pv, bb=bb: e.tensor_tensor(out=mtmp[:], in0=pv, in1=bb, op=ALU.add),
                         reads=[pb[bi], cpk.buf], writes=[mtmp.buf])
                    P.op("dve", lambda e, dst=dst, gb=gb: e.scalar_tensor_tensor(out=dst[:], in0=mtmp[:], scalar=1.0, in1=gb,
                                                                               op0=ALU.add, op1=ALU.mult),
                         reads=[mtmp.buf, cpk.buf], writes=[dst.buf])
            else:
                gi = 0 if v == 2 else 1
                P.dma("sp", badaT[:], bc(badar_d[0:1, v * 1024:(v + 1) * 1024], [128, 1024]), ch_bada, writes=[badaT.buf])
                P.dma("sp", gpostT[:], bc(gpostr_d[gi:gi + 1, :], [128, 1024]), ch_gpost, writes=[gpostT.buf])
                for wi, sx in enumerate((siluPx, siluSx)):
                    if v == 2:
                        dst = gmP if wi == 0 else gmS
                    else:
                        dst = gtmp[wi]
                    for half in range(2):
                        bi = 2 + (wi * 2 + half) % 4
                        for k in range(8):
                            P.op("pe", lambda e, s=s, k=k, bi=bi, half=half, sx=sx: e.matmul(
                                bank[bi], lhsT=sx[:, k, :], rhs=wsl[s][:, k, half * 512:(half + 1) * 512],
                                start=(k == 0), stop=(k == 7)),
                                reads=[wsl[s].buf, sx.buf], writes=[pb[bi]])
                        hs = slice(half * 512, (half + 1) * 512)
                        P.op("dve", lambda e, bi=bi, hs=hs: e.tensor_tensor(out=tmpT[:, hs], in0=bank[bi], in1=badaT[:, hs], op=ALU.add),
                             reads=[pb[bi], badaT.buf], writes=[tmpT.buf])
                        P.op("pool", lambda e, dst=dst, hs=hs: e.tensor_tensor(out=dst[:, hs], in0=tmpT[:, hs], in1=gpostT[:, hs], op=ALU.mult),
                             reads=[tmpT.buf, gpostT.buf], writes=[dst.buf])
                    if v == 5:
                        P.dma("sp", gsave_d[wi], dst[:], ch_gs[wi], reads=[dst.buf])

        P.barrier()
        A.off = mark0
        if stop == 0:
            P.emit(nc, final_chans=out_chans)
            return nc, P

        def rstd_pool(t, c0, n, N, eps=EPS):
            P.op("pool", lambda e: e.tensor_scalar(out=t[:, c0 + n:c0 + 2 * n], in0=t[:, c0:c0 + n], scalar1=1.0 / N, scalar2=eps,
                                                   op0=ALU.mult, op1=ALU.add), reads=[t.buf], writes=[t.buf])
            P.op("pool", lambda e: e.tensor_tensor(out=t[:, c0 + 2 * n:c0 + 3 * n], in0=t[:, c0 + n:c0 + 2 * n], in1=mhalf[:, 0:n], op=ALU.pow),
                 reads=[t.buf, mhalf.buf], writes=[t.buf])

        def stage_A_pre(xt, xn, st_):
            P.op("act", lambda e: e.activation(out=xn[:], in_=xt[:], func=AF.Square, accum_out=st_[:, 0:1]),
                 reads=[xt.buf], writes=[xn.buf, st_.buf])
            rstd_pool(st_, 0, 1, D)
            P.op("act", lambda e: e.activation(out=xn[:], in_=xt[:], func=AF.Identity, scale=st_[:, 2:3]),
                 reads=[xt.buf, st_.buf], writes=[xn.buf])

        def stage_A(xt, xn, Amod, Bmod, sample, hT_dst, hT_buf, tmp4k):
            stage_A_pre(xt, xn, stt)
            stage_A_post(xn, Amod, Bmod, sample, hT_dst, hT_buf, tmp4k)

        def stage_A_post(xn, Amod, Bmod, sample, hT_dst, hT_buf, tmp4k):
            cp(20)
            for k in range(8):
                P.op("pe", lambda e, k=k: e.transpose(bankbf[0][:, k * 128:(k + 1) * 128], xn[:, k * 128:(k + 1) * 128], ident_b),
                     reads=[xn.buf, cb.buf], writes=[pb[0]])
            cp(21)
            if not sample:
                import os
                for k in range(8):
                    kev = os.environ.get("KEVAC", "act")
                    if kev == "dve":
                        P.op("dve", lambda e, k=k: e.tensor_scalar(out=hT_dst[:, k, :], in0=bankbf[0][:, k * 128:(k + 1) * 128],
                                                                   scalar1=Amod[:, k, 0:1], scalar2=Bmod[:, k, 0:1],
                                                                   op0=ALU.mult, op1=ALU.add),
                             reads=[pb[0], Amod.buf, Bmod.buf], writes=[hT_buf], nowaw=True)
                    else:
                        P.op("act", lambda e, k=k: e.activation(out=hT_dst[:, k, :], in_=bankbf[0][:, k * 128:(k + 1) * 128],
                                                                func=AF.Identity, scale=Amod[:, k, 0:1], bias=Bmod[:, k, 0:1]),
                             reads=[pb[0], Amod.buf, Bmod.buf], writes=[hT_buf], nowaw=True)
            else:
                pv = bankbf[0].rearrange("p (k b l) -> p k b l", k=8, b=16)
                tv = tmp4k[:].rearrange("p (k b l) -> p k b l", k=8, b=16)
                P.op("dve", lambda e: e.tensor_tensor(out=tv, in0=pv, in1=bc(Amod[:, :, 1:17].unsqueeze(3), [128, 8, 16, 8]), op=ALU.mult),
                     reads=[pb[0], Amod.buf], writes=[tmp4k.buf])
                P.op("dve", lambda e: e.tensor_tensor(out=hT_dst.rearrange("p k (b l) -> p k b l", b=16), in0=tv,
                                                      in1=bc(Bmod[:, :, 1:17].unsqueeze(3), [128, 8, 16, 8]), op=ALU.add),
                     reads=[tmp4k.buf, Bmod.buf], writes=[hT_buf])

        mark1 = A.off
        Wsc = A.alloc("Wsc", [128, 8, 3072], BF16)
        ch_wsc = [P.chan("wsc%d" % j) for j in range(8)]
        wsc_bufs = [Buf("wsc%d" % j) for j in range(8)]
        for j in range(8):
            P.dma("pool", Wsc[:, :, j * 384:(j + 1) * 384], wsc_d[j].rearrange("p (k c) -> p k c", k=8), ch_wsc[j],
                  writes=[wsc_bufs[j]])
        wssm_bufs = [Buf("wssmq%d" % q) for q in range(7)]
        for q in range(6):
            P.dma("pool", Wssm[:, :, q * 512:(q + 1) * 512], wssm_d[q].rearrange("p (k c) -> p k c", k=8), ch_wssm[q],
                  writes=[wssm_bufs[q]])
        P.dma("pool", Wssm[:, :, 3072:3088], wdt_d[:, :].rearrange("p (k c) -> p k c", k=8), ch_wssm[6], writes=[wssm_bufs[6]])
        xin = [A.alloc("xin%d" % i, [128, 1024]) for i in range(2)]
        ch_xin = [P.chan("xin%d" % i) for i in range(2)]
        xn = A.alloc("xn", [128, 1024], BF16)
        hT = A.alloc("hT", [128, 8, 512], BF16)
        pxs = A.alloc("pxs", [128, 512])
        uext = [A.alloc("uext%d" % i, [128, 516]) for i in range(2)]
        ucv = [A.alloc("ucv%d" % i, [128, 512]) for i in range(2)]
        uhist = A.alloc("uhist", [128, 8, 2])
        scst = A.alloc("scst", [128, 8, 16, 2])
        nscs = A.alloc("nscs", [128, 8, 16, 2])
        tmp4k = A.alloc("tmp4k", [128, 1024])
        ch_scst = P.chan("scst")
        ch_nscp = P.chan("nscp")
        ch_nscs = P.chan("nscs")
        out_chans += [ch_nscp, ch_nscs]
        P.dma("sp", scst[:].rearrange("p j b r -> p (j b r)"), scst_d[:, :], ch_scst, writes=[scst.buf])
        P.op("pool", lambda e: e.memset(uhist[:], 0.0), writes=[uhist.buf])

        groups = [(g * 512, 512, False) for g in range(4)] + [(LP, 128, True)]
        tile_toks = [t0 + tt * 128 for (t0, ts, _) in groups for tt in range(ts // 128)]

        def load_x1a(n):
            if n < len(tile_toks):
                P.dma("sp", xin[n % 2][:], xall_d[tile_toks[n]:tile_toks[n] + 128, :], ch_xin[n % 2], writes=[xin[n % 2].buf])

        cp(10)
        hTs = [hT, A.alloc("hT1", [128, 8, 512], BF16)]
        xns = [xn] + [A.alloc("xn_%d" % i, [128, 1024], BF16) for i in range(1, 4)]
        stq = [A.alloc("stq%d" % i, [128, 4]) for i in range(4)]
        xc = [0]

        def stage_A1_pre(gi):
            t0_, ts_, smp_ = groups[gi]
            for tt in range(ts_ // 128):
                xs_ = xin[xc[0] % 2]
                load_x1a(xc[0] + 1)
                xc[0] += 1
                stage_A_pre(xs_, xns[tt], stq[tt])
                tix = (t0_ + tt * 128) // 128
                P.op("pool", lambda e, tt=tt, tix=tix: e.tensor_copy(out=rstd_all[:, tix:tix + 1], in_=stq[tt][:, 2:3]),
                     reads=[stq[tt].buf], writes=[rstd_all.buf], nowaw=True)

        def stage_A1_post(gi):
            t0_, ts_, smp_ = groups[gi]
            for tt in range(ts_ // 128):
                stage_A_post(xns[tt], A_m, Bv_m, smp_, hTs[gi % 2][:, :, tt * 128:(tt + 1) * 128], hTs[gi % 2].buf, tmp4k)
                tix = (t0_ + tt * 128) // 128
                for k in range(8):
                    P.op("pe", lambda e, k=k, gi=gi, tt=tt, tix=tix: e.matmul(
                        bank[7][:, tix * 16:(tix + 1) * 16], lhsT=hTs[gi % 2][:, k, tt * 128:(tt + 1) * 128],
                        rhs=Wssm[:, k, 3072:3088], start=(k == 0), stop=(k == 7)),
                        reads=[wssm_bufs[6], hTs[gi % 2].buf], writes=[pb[7]])

        def stage_A1(gi):
            stage_A1_pre(gi)
            stage_A1_post(gi)

        load_x1a(0)
        stage_A1(0)
        for gi, (tok0, TS, sample) in enumerate(groups):
            hTc = hTs[gi % 2]
            for j in range(8):
                bs = (1, 2, 3) if j % 2 == 0 else (4, 5, 6)
                for part in range(3):
                    bi = bs[part]
                    for k in range(8):
                        P.op("pe", lambda e, j=j, part=part, bi=bi, k=k, TS=TS, hTc=hTc: e.matmul(
                            bank[bi][:, 0:TS], lhsT=Wsc[:, k, j * 384 + part * 128:j * 384 + (part + 1) * 128],
                            rhs=hTc[:, k, 0:TS], start=(k == 0), stop=(k == 7)),
                            reads=[wsc_bufs[j], hTc.buf], writes=[pb[bi]])
                if j == 2 and gi + 1 < len(groups):
                    stage_A1_pre(gi + 1)
                if j == 5 and gi + 1 < len(groups):
                    stage_A1_post(gi + 1)
                bcn, bxn, bbn = bs
                u = uext[j % 2]
                uc = ucv[j % 2]
                P.op("act", lambda e, bxn=bxn, TS=TS: e.activation(out=pxs[:, 0:TS], in_=bank[bxn][:, 0:TS], func=AF.Copy),
                     reads=[pb[bxn]], writes=[pxs.buf])
                if not sample:
                    P.op("pool", lambda e, u=u, j=j: e.tensor_copy(out=u[:, 0:2], in_=uhist[:, j, :]),
                         reads=[uhist.buf], writes=[u.buf])
                    P.op("dve", lambda e, u=u, bcn=bcn, TS=TS: e.tensor_tensor(out=u[:, 2:2 + TS], in0=bank[bcn][:, 0:TS], in1=pxs[:, 0:TS], op=ALU.mult),
                         reads=[pb[bcn], pxs.buf], writes=[u.buf], nowaw=True)
                    for r in range(3):
                        if r == 0:
                            P.op("dve", lambda e, u=u, uc=uc, j=j, TS=TS: e.tensor_scalar(
                                out=uc[:, 0:TS], in0=u[:, 0:TS], scalar1=cpc("scw", j * 3), scalar2=None, op0=ALU.mult),
                                reads=[u.buf, cpk.buf], writes=[uc.buf])
                        else:
                            P.op("dve", lambda e, u=u, uc=uc, j=j, r=r, TS=TS: e.scalar_tensor_tensor(
                                out=uc[:, 0:TS], in0=u[:, r:r + TS], scalar=cpc("scw", j * 3 + r), in1=uc[:, 0:TS],
                                op0=ALU.mult, op1=ALU.add),
                                reads=[u.buf, uc.buf, cpk.buf], writes=[uc.buf])
                    P.op("dve", lambda e, uc=uc, bbn=bbn, j=j, tok0=tok0, TS=TS: e.tensor_tensor(
                        out=y_scT[:, j, tok0:tok0 + TS], in0=bank[bbn][:, 0:TS], in1=uc[:, 0:TS], op=ALU.mult),
                        reads=[pb[bbn], uc.buf], writes=[y_scT.buf])
                    P.op("pool", lambda e, u=u, j=j, TS=TS: e.tensor_copy(out=uhist[:, j, :], in_=u[:, TS:TS + 2]),
                         reads=[u.buf], writes=[uhist.buf])
                else:
                    u3 = u[:, 0:160].rearrange("p (b c) -> p b c", b=16)
                    uc3 = uc[:, 0:128].rearrange("p (b l) -> p b l", b=16)
                    P.op("pool", lambda e, u3=u3, j=j: e.tensor_copy(out=u3[:, :, 0:2], in_=scst[:, j, :, :]),
                         reads=[scst.buf], writes=[u.buf])
                    P.op("dve", lambda e, u3=u3, bcn=bcn: e.tensor_tensor(
                        out=u3[:, :, 2:10], in0=bank[bcn][:, 0:128].rearrange("p (b l) -> p b l", b=16),
                        in1=pxs[:, 0:128].rearrange("p (b l) -> p b l", b=16), op=ALU.mult),
                        reads=[pb[bcn], pxs.buf], writes=[u.buf], nowaw=True)
                    for r in range(3):
                        if r == 0:
                            P.op("dve", lambda e, u3=u3, uc3=uc3, j=j: e.tensor_scalar(
                                out=uc3, in0=u3[:, :, 0:8], scalar1=cpc("scw", j * 3), scalar2=None, op0=ALU.mult),
                                reads=[u.buf, cpk.buf], writes=[uc.buf])
                        else:
                            P.op("dve", lambda e, u3=u3, uc3=uc3, j=j, r=r: e.scalar_tensor_tensor(
                                out=uc3, in0=u3[:, :, r:r + 8], scalar=cpc("scw", j * 3 + r), in1=uc3,
                                op0=ALU.mult, op1=ALU.add),
                                reads=[u.buf, uc.buf, cpk.buf], writes=[uc.buf])
                    P.op("dve", lambda e, uc=uc, bbn=bbn, j=j: e.tensor_tensor(
                        out=y_scT[:, j, LP:LP + 128], in0=bank[bbn][:, 0:128], in1=uc[:, 0:128], op=ALU.mult),
                        reads=[pb[bbn], uc.buf], writes=[y_scT.buf])
                    P.op("pool", lambda e, u3=u3, j=j: e.tensor_copy(out=nscs[:, j, :, :], in_=u3[:, :, 8:10]),
                         reads=[u.buf], writes=[nscs.buf])
            if tok0 == 3 * 512:
                P.dma("sp", nscp_d[:, :], uhist[:].rearrange("p j r -> p (j r)"), ch_nscp, reads=[uhist.buf])
        P.dma("sp", nscs_d[:, :], nscs[:].rearrange("p j b r -> p (j b r)"), ch_nscs, reads=[nscs.buf])
        dtt = A.alloc("dtt", [128, 17, 16])
        P.op("dve", lambda e: e.tensor_tensor(out=dtt[:], in0=bank[7][:, 0:272].rearrange("p (n h) -> p n h", n=17),
                                              in1=bc(cpk[:, _CP["dtb"]:_CP["dtb"] + 16].unsqueeze(1), [128, 17, 16]), op=ALU.add),
             reads=[pb[7], cpk.buf], writes=[dtt.buf])
        P.op("act", lambda e: e.activation(out=dtt[:], in_=dtt[:], func=AF.Exp), reads=[dtt.buf], writes=[dtt.buf])
        P.op("act", lambda e: e.activation(out=dt_all[:], in_=dtt[:], func=AF.Ln, bias=1.0), reads=[dtt.buf], writes=[dt_all.buf])

        P.barrier()
        A.off = mark1
        if stop == 1:
            P.emit(nc, final_chans=out_chans)
            return nc, P

        Wout = A.alloc("Wout", [128, 16, 1024], BF16)
        ch_wout = [P.chan("wout%d" % i) for i in range(4)]
        wout_bufs = [Buf("wout%d" % i) for i in range(4)]
        for i in range(4):
            P.dma("pool", Wout[:, i * 4:(i + 1) * 4, :], wout_d[i * 512:(i + 1) * 512, :].rearrange("(k p) n -> p k n", p=128),
                  ch_wout[i], writes=[wout_bufs[i]])
        NXS = 3
        xinb = [A.alloc("xinb%d" % i, [128, 1024]) for i in range(NXS)]
        xnb = A.alloc("xnb", [128, 1024], BF16)
        hTb = A.alloc("hTb", [128, 8, 128], BF16)
        ext = [A.alloc("ext%d" % i, [128, 4, 176]) for i in range(2)]
        cv = [A.alloc("cv%d" % i, [128, 4, 128]) for i in range(2)]
        xsT0 = A.alloc("xsT", [128, 8, 128], BF16)
        BT0 = A.alloc("BT", [128, 4, 128], BF16)
        CT0 = A.alloc("CT", [128, 4, 128], BF16)
        szT0 = A.alloc("szT", [128, 8, 128], BF16)
        sm16_0 = A.alloc("sm16", [128, 12, 16])
        dhl0 = A.alloc("dhl", [128, 2, 16], BF16)
        xdt = A.alloc("xdt", [128, 1024], BF16)
        xdtd = A.alloc("xdtd", [128, 1024], BF16)
        Btok = A.alloc("Btok", [128, 512], BF16)
        smk = A.alloc("smk", [128, 4, 128])
        Lsb = [A.alloc("Lsb%d" % i, [128, 4, 128]) for i in range(2)]
        Mt = A.alloc("Mt", [128, 16, 128], BF16)
        ST = A.alloc("ST", [128, 1024])
        STb = A.alloc("STb", [128, 1024], BF16)
        ycomb = A.alloc("ycomb", [128, 1024])
        ygn = A.alloc("ygn", [128, 1024], BF16)
        yssdT = A.alloc("yssdT", [128, 8, 128], BF16)
        xhist = A.alloc("xhist", [128, 16, 3])
        xcst = A.alloc("xcst", [128, 16, 16, 3])
        nxcs = A.alloc("nxcs", [128, 16, 16, 3])
        ss4 = A.alloc("ss4", [128, 12])
        sttb = A.alloc("sttb", [128, 8])
        ptmp = None
        CTm = [A.alloc("CTm%d" % i, [128, 4, 128], BF16) for i in range(2)]
        Bm = A.alloc("Bm", [128, 512], BF16)
        decT = A.alloc("decT", [128, 8, 16])
        S0n = [A.alias(ext[0], [128, 8, 128], span=ext[0].nbytes + ext[1].nbytes),
               A.alias(cv[0], [128, 8, 128], span=cv[0].nbytes + cv[1].nbytes),
               A.alias(ST, [128, 8, 128])]
        assert ext[1].off == ext[0].off + ext[0].nbytes and cv[1].off == cv[0].off + cv[0].nbytes
        S0n_bufs = [[ext[0].buf, ext[1].buf], [cv[0].buf, cv[1].buf], [ST.buf]]
        dtAx = A.alias(Mt, [128, 2, 1024], BF16)
        S0b = A.alias(ygn, [128, 1024], BF16)
        S0T = A.alias(STb, [128, 1024], BF16)
        bsets = [dict(xsT=xsT0, BT=BT0, CT=CT0, szT=szT0, sm16=sm16_0, dhl=dhl0),
                 dict(xsT=A.alias(xcst, [128, 8, 128], BF16), szT=A.alias(nxcs, [128, 8, 128], BF16),
                      BT=A.alias(CTm[0], [128, 4, 128], BF16), CT=A.alias(CTm[1], [128, 4, 128], BF16),
                      sm16=A.alias(Bm, [128, 12, 16]), dhl=A.alias(decT, [128, 2, 16], BF16))]
        ch_xinb = [P.chan("xinb%d" % i) for i in range(NXS)]
        ch_xcst = P.chan("xcst")
        ch_s0 = [P.chan("s0n%d" % i) for i in range(3)]
        ch_nxcp = P.chan("nxcp")
        ch_nxcs = P.chan("nxcs")
        ch_nstp = P.chan("nstp")
        out_chans += [ch_nxcp, ch_nxcs, ch_nstp] + ch_s0
        P.dma("sp", xcst[:].rearrange("p j b r -> p (j b r)"), xcst_d[:, :], ch_xcst, writes=[xcst.buf])
        P.op("pool", lambda e: e.memset(xhist[:], 0.0), writes=[xhist.buf])
        for i in range(2):
            P.op("pool", lambda e, i=i: e.memset(CTm[i][:], 0.0), writes=[CTm[i].buf])

        pb1f = pb[1]
        pb1b = pb[1]
        b1tok = bankbf[5][:, 0:512]

        DTR, DT, DTA, TMP, NAC, EAC, DLT, DTE, DEC, EXPX = range(10)

        chunk_list = [(LP, True, -1)] + [(c * 128, False, c) for c in range(NCH)]

        def load_x1b(n):
            if n < len(chunk_list):
                t0 = chunk_list[n][0]
                P.dma("sp", xinb[n % NXS][:], xall_d[t0:t0 + 128, :], ch_xinb[n % NXS], writes=[xinb[n % NXS].buf])

        def conv_wide(eng, wt, c_, taps, shp, j0, ebuf):
            nd = len(shp)

            def wb(r):
                w = cvw[:, j0 * 4 + r:j0 * 4 + 16:4]
                w = w.unsqueeze(2) if nd == 3 else w.unsqueeze(2).unsqueeze(3)
                return bc(w, shp)

            bb = cvw[:, 64 + j0:64 + j0 + 4]
            bb = bc(bb.unsqueeze(2) if nd == 3 else bb.unsqueeze(2).unsqueeze(3), shp)
            cview = c_[:] if nd == 3 else c_[:].rearrange("p a (b l) -> p a b l", b=16)
            wview = wt[:].bitcast(F32).rearrange("p (a t) -> p a t", a=4) if wt.ap.dtype != F32 else wt[:]
            if nd == 4:
                wview = wview.rearrange("p a (b l) -> p a b l", b=16)
            for jj in range(4):
                P.op("act", lambda e, jj=jj: e.activation(out=cview[:, jj], in_=taps[0][:, jj], func=AF.Identity,
                                                          scale=cvw[:, (j0 + jj) * 4:(j0 + jj) * 4 + 1],
                                                          bias=cvw[:, 64 + j0 + jj:64 + j0 + jj + 1]),
                     reads=[ebuf, cvw.buf], writes=[c_.buf])
            for jj in range(4):
                for r in range(1, 4):
                    P.op(eng, lambda e, r=r, jj=jj: e.scalar_tensor_tensor(
                        out=cview[:, jj], in0=taps[r][:, jj], scalar=cvw[:, (j0 + jj) * 4 + r:(j0 + jj) * 4 + r + 1], in1=cview[:, jj],
                        op0=ALU.mult, op1=ALU.add),
                        reads=[ebuf, cvw.buf, c_.buf], writes=[c_.buf])

        def front(ci):
            tok0, sample, cidx = chunk_list[ci]
            S = bsets[0] if sample else bsets[cidx % 2]
            xsT, BT, CT, szT, sm16, dhl = S["xsT"], S["BT"], S["CT"], S["szT"], S["sm16"], S["dhl"]

            def s16(i):
                return sm16[:, i, :]

            xs_ = xinb[ci % NXS]
            tix = tok0 // 128
            P.op("act", lambda e, xs_=xs_, tix=tix: e.activation(out=xnb[:], in_=xs_[:], func=AF.Identity, scale=rstd_all[:, tix:tix + 1]),
                 reads=[xs_.buf, rstd_all.buf], writes=[xnb.buf])
            stage_A_post(xnb, A_m, Bv_m, sample, hTb[:], hTb.buf, ycomb if sample else None)
            yield
            P.op("dve", lambda e, tix=tix: e.tensor_copy(out=s16(DT), in_=dt_all[:, tix, :]), reads=[dt_all.buf], writes=[sm16.buf])
            P.op("dve", lambda e: e.tensor_tensor(out=s16(DTA), in0=s16(DT), in1=Abc[:], op=ALU.mult),
                 reads=[sm16.buf, Abc.buf], writes=[sm16.buf])
            P.op("dve", lambda e: e.tensor_copy(out=dhl[:, 0, :], in_=s16(DTA)), reads=[sm16.buf], writes=[dhl.buf])
            P.op("dve", lambda e: e.tensor_tensor(out=s16(TMP), in0=s16(DTA), in1=dhl[:, 0, :], op=ALU.subtract),
                 reads=[sm16.buf, dhl.buf], writes=[sm16.buf])
            P.op("dve", lambda e: e.tensor_copy(out=dhl[:, 1, :], in_=s16(TMP)), reads=[sm16.buf], writes=[dhl.buf])
            trm = triBD_b if sample else tri_b
            onm = same_b if sample else ones_b[:]
            for hl in range(2):
                P.op("pe", lambda e, hl=hl, trm=trm: e.matmul(bank[1][:, 16:32], lhsT=trm, rhs=dhl[:, hl, :], start=(hl == 0), stop=(hl == 1)),
                     reads=[dhl.buf, cb.buf], writes=[pb1f])
            for hl in range(2):
                P.op("pe", lambda e, hl=hl, onm=onm: e.matmul(bank[1][:, 32:48], lhsT=onm, rhs=dhl[:, hl, :], start=(hl == 0), stop=(hl == 1)),
                     reads=[dhl.buf, cb.buf, ones_b.buf], writes=[pb1f])
            P.op("dve", lambda e: e.tensor_scalar(out=s16(NAC), in0=bank[1][:, 16:32], scalar1=-1.0, scalar2=None, op0=ALU.mult),
                 reads=[pb1f], writes=[sm16.buf])
            P.op("act", lambda e: e.activation(out=s16(EAC), in_=bank[1][:, 16:32], func=AF.Exp), reads=[pb1f], writes=[sm16.buf])
            P.op("dve", lambda e: e.tensor_tensor(out=s16(DLT), in0=bank[1][:, 32:48], in1=s16(NAC), op=ALU.add),
                 reads=[pb1f, sm16.buf], writes=[sm16.buf])
            P.op("act", lambda e: e.activation(out=s16(DTE), in_=s16(DLT), func=AF.Exp), reads=[sm16.buf], writes=[sm16.buf])
            if not sample:
                P.op("act", lambda e: e.activation(out=s16(DEC), in_=bank[1][:, 32:48], func=AF.Exp), reads=[pb1f], writes=[sm16.buf])
            yield

            for qi, q in enumerate((0, 1, 2, 3, 4, 5)):
                bi = 2 + qi % 2
                for jj in range(4):
                    for k in range(8):
                        P.op("pe", lambda e, q=q, jj=jj, k=k, bi=bi: e.matmul(
                            bank[bi][:, jj * 128:(jj + 1) * 128], lhsT=Wssm[:, k, q * 512 + jj * 128:q * 512 + (jj + 1) * 128],
                            rhs=hTb[:, k, :], start=(k == 0), stop=(k == 7)),
                            reads=[wssm_bufs[q], hTb.buf], writes=[pb[bi]])
                bv = bank[bi].rearrange("p (a t) -> p a t", a=4)
                if q < 2:
                    thv = ext[q % 2][:, :, 0:128]
                    P.op("act", lambda e, bv=bv, thv=thv: e.activation(out=thv, in_=bv, func=AF.Tanh, scale=0.5),
                         reads=[pb[bi]], writes=[ext[q % 2].buf])
                    P.op("dve", lambda e, q=q, bv=bv, thv=thv, szT=szT: e.scalar_tensor_tensor(
                        out=szT[:, q * 4:(q + 1) * 4, :], in0=thv, scalar=1.0, in1=bv, op0=ALU.add, op1=ALU.mult),
                        reads=[ext[q % 2].buf, pb[bi]], writes=[szT.buf])
                    yield
                    continue
                jq = q - 2
                j0 = jq * 4
                e_ = ext[jq % 2]
                c_ = cv[jq % 2]
                on_pool = False
                ceng = "pool" if on_pool else "dve"
                wt_ = ptmp if on_pool else xnb
                if not sample:
                    e3 = e_[:, :, 0:131]
                    P.op("pool", lambda e, e3=e3, j0=j0: e.tensor_copy(out=e3[:, :, 0:3], in_=xhist[:, j0:j0 + 4, :]),
                         reads=[xhist.buf], writes=[e_.buf])
                    P.op("act", lambda e, e3=e3, bv=bv: e.activation(out=e3[:, :, 3:131], in_=bv, func=AF.Copy),
                         reads=[pb[bi]], writes=[e_.buf], nowaw=True)
                    conv_wide(ceng, wt_, c_, [e3[:, :, r:r + 128] for r in range(4)], [128, 4, 128], j0, e_.buf)
                    P.op("pool", lambda e, e3=e3, j0=j0: e.tensor_copy(out=xhist[:, j0:j0 + 4, :], in_=e3[:, :, 128:131]),
                         reads=[e_.buf], writes=[xhist.buf])
                else:
                    e4 = e_[:].rearrange("p a (b c) -> p a b c", b=16)
                    P.op("pool", lambda e, e4=e4, j0=j0: e.tensor_copy(out=e4[:, :, :, 0:3], in_=xcst[:, j0:j0 + 4, :, :]),
                         reads=[xcst.buf], writes=[e_.buf])
                    P.op("act", lambda e, e4=e4, bi=bi: e.activation(
                        out=e4[:, :, :, 3:11], in_=bank[bi].rearrange("p (a b l) -> p a b l", a=4, b=16), func=AF.Copy),
                        reads=[pb[bi]], writes=[e_.buf], nowaw=True)
                    conv_wide(ceng, wt_, c_, [e4[:, :, :, r:r + 8] for r in range(4)], [128, 4, 16, 8], j0, e_.buf)
                    P.op("pool", lambda e, e4=e4, j0=j0: e.tensor_copy(out=nxcs[:, j0:j0 + 4, :, :], in_=e4[:, :, :, 8:11]),
                         reads=[e_.buf], writes=[nxcs.buf])
                if jq < 2:
                    dst, dbuf = xsT[:, j0:j0 + 4, :], xsT.buf
                elif jq == 2:
                    dst, dbuf = BT[:], BT.buf
                else:
                    dst, dbuf = CT[:], CT.buf
                thv = e_[:, :, 0:128]
                P.op("act", lambda e, c_=c_, thv=thv: e.activation(out=thv, in_=c_[:], func=AF.Tanh),
                     reads=[c_.buf], writes=[e_.buf])
                P.op("dve", lambda e, c_=c_, dst=dst, thv=thv: e.scalar_tensor_tensor(
                    out=dst, in0=thv, scalar=1.0, in1=c_[:], op0=ALU.add, op1=ALU.mult),
                    reads=[e_.buf, c_.buf], writes=[dbuf])
                yield

        def back(ci):
            tok0, sample, cidx = chunk_list[ci]
            first = (not sample) and cidx == 0
            S = bsets[0] if sample else bsets[cidx % 2]
            xsT, BT, CT, szT, sm16, dhl = S["xsT"], S["BT"], S["CT"], S["szT"], S["sm16"], S["dhl"]

            def s16(i):
                return sm16[:, i, :]

            xs_ = xinb[ci % NXS]
            trm = triBD_b if sample else tri_b
            for j in range(8):
                P.op("pe", lambda e, j=j: e.transpose(bankbf[4][:, j * 128:(j + 1) * 128], xsT[:, j, :], ident_b),
                     reads=[xsT.buf, cb.buf], writes=[pb[4]])
            if ci == 1:
                cp(70)
            for g in range(4):
                P.op("pe", lambda e, g=g: e.transpose(b1tok[:, g * 128:(g + 1) * 128], BT[:, g, :], ident_b),
                     reads=[BT.buf, cb.buf], writes=[pb[5]])
            if ci == 1:
                cp(71)
            P.op("dve", lambda e: e.tensor_tensor(out=xdt[:].rearrange("p (h q) -> p h q", h=16),
                                                  in0=bankbf[4].rearrange("p (h q) -> p h q", h=16),
                                                  in1=bc(s16(DT).unsqueeze(2), [128, 16, 64]), op=ALU.mult),
                 reads=[pb[4], sm16.buf], writes=[xdt.buf])
            P.op("pool", lambda e: e.tensor_tensor(out=xdtd[:].rearrange("p (h q) -> p h q", h=16),
                                                   in0=xdt[:].rearrange("p (h q) -> p h q", h=16),
                                                   in1=bc(s16(DTE).unsqueeze(2), [128, 16, 64]), op=ALU.mult),
                 reads=[xdt.buf, sm16.buf], writes=[xdtd.buf])
            P.op("act", lambda e: e.activation(out=Btok[:], in_=b1tok, func=AF.Copy), reads=[pb[5]], writes=[Btok.buf])
            yield
            for g in range(4):
                P.op("pe", lambda e, g=g: e.matmul(bank[5][:, g * 128:(g + 1) * 128], lhsT=BT[:, g, :], rhs=CT[:, g, :], start=True, stop=True),
                     reads=[BT.buf, CT.buf], writes=[pb[5]])
            P.op("act", lambda e: e.activation(out=smk[:], in_=bank[5].rearrange("p (g l) -> p g l", g=4), func=AF.Copy),
                 reads=[pb[5]], writes=[smk.buf])
            if (not sample) and (not first):
                for g in range(4):
                    bi = 6 + g // 2
                    P.op("pe", lambda e, g=g, bi=bi: e.matmul(bank[bi][:, (g % 2) * 256:(g % 2) * 256 + 256], lhsT=CT[:, g, :],
                                                              rhs=STb[:, g * 256:(g + 1) * 256], start=True, stop=True),
                         reads=[CT.buf, STb.buf], writes=[pb[bi]])
            yield
            ngm = negmBD_b if sample else negm_b
            ntm = ntri[:, 1, :] if sample else ntri[:, 0, :]
            for g in range(4):
                bi = 4 + g % 2
                for r in range(4):
                    h = 4 * g + r
                    osl = bank[bi][:, r * 128:(r + 1) * 128]
                    for hl in range(2):
                        P.op("pe", lambda e, osl=osl, hl=hl, h=h, trm=trm: e.matmul(
                            osl, lhsT=bc(dhl[:, hl, h:h + 1], [128, 128]), rhs=trm, start=(hl == 0), stop=False),
                            reads=[dhl.buf, cb.buf], writes=[pb[bi]])
                    for hl in range(2):
                        P.op("pe", lambda e, osl=osl, hl=hl, h=h, ntm=ntm: e.matmul(
                            osl, lhsT=ntm, rhs=bc(dhl[:, hl, h:h + 1], [128, 128]), start=False, stop=False),
                            reads=[dhl.buf, ntri.buf], writes=[pb[bi]])
                    P.op("pe", lambda e, osl=osl, ngm=ngm: e.matmul(osl, lhsT=ident_b, rhs=ngm, start=False, stop=True),
                         reads=[cb.buf], writes=[pb[bi]])
                L_ = Lsb[g % 2]
                P.op("act", lambda e, L_=L_, bi=bi: e.activation(out=L_[:], in_=bank[bi].rearrange("p (r l) -> p r l", r=4), func=AF.Exp),
                     reads=[pb[bi]], writes=[L_.buf])
                P.op("dve", lambda e, L_=L_, g=g: e.tensor_tensor(out=Mt[:, 4 * g:4 * g + 4, :], in0=L_[:],
                                                                   in1=bc(smk[:, g, :].unsqueeze(1), [128, 4, 128]), op=ALU.mult),
                     reads=[L_.buf, smk.buf], writes=[Mt.buf])
                yield
            for j in range(8):
                bi = 4 + j // 4
                col = (j % 4) * 128
                P.op("pe", lambda e, j=j, bi=bi, col=col: e.matmul(bank[bi][:, col:col + 128], lhsT=xsT[:, j, :], rhs=diagD[:, j, :],
                                                                   start=True, stop=False),
                     reads=[xsT.buf, diagD.buf], writes=[pb[bi]])
                for hh in range(2):
                    h = 2 * j + hh
                    P.op("pe", lambda e, h=h, bi=bi, col=col, hh=hh: e.matmul(
                        bank[bi][:, col + hh * 64:col + (hh + 1) * 64], lhsT=Mt[:, h, :], rhs=xdt[:, h * 64:(h + 1) * 64],
                        start=False, stop=(hh == 1)),
                        reads=[Mt.buf, xdt.buf], writes=[pb[bi]])
            yd = pair(4)
            yo = pair(6)
            if first:
                P.op("dve", lambda e: e.tensor_copy(out=ycomb[:], in_=yd), reads=[pb[4], pb[5]], writes=[ycomb.buf])
            elif not sample:
                P.op("dve", lambda e: e.tensor_tensor(out=ycomb[:].rearrange("p (h q) -> p h q", h=16),
                                                      in0=yo.rearrange("p (h q) -> p h q", h=16),
                                                      in1=bc(s16(EAC).unsqueeze(2), [128, 16, 64]), op=ALU.mult),
                     reads=[pb[6], pb[7], sm16.buf], writes=[ycomb.buf])
                P.op("dve", lambda e: e.tensor_tensor(out=ycomb[:], in0=ycomb[:], in1=yd, op=ALU.add),
                     reads=[ycomb.buf, pb[4], pb[5]], writes=[ycomb.buf])
            else:
                for hl in range(2):
                    P.op("dve", lambda e, hl=hl: e.tensor_copy(out=dtAx[:, hl, :].rearrange("p (h q) -> p h q", h=16),
                                                               in_=bc(dhl[:, hl, :].unsqueeze(2), [128, 16, 64])),
                         reads=[dhl.buf, Mt.buf], writes=[dtAx.buf])
                selb = A.alias(Bm, [128, 16], BF16)
                P.op("dve", lambda e: e.tensor_copy(out=selb[:], in_=cpk[:, _CP["seqsel"]:_CP["seqsel"] + 16]),
                     reads=[cpk.buf], writes=[selb.buf])
                for jp in range(8):
                    for hl in range(2):
                        P.op("pe", lambda e, jp=jp, hl=hl: e.matmul(bank[1][:, 64 + jp * 16:64 + (jp + 1) * 16],
                                                                    lhsT=dtAx[:, hl, jp * 128:(jp + 1) * 128], rhs=selb[:],
                                                                    start=(hl == 0), stop=(hl == 1)),
                             reads=[dtAx.buf, selb.buf], writes=[pb1b])
                P.op("act", lambda e: e.activation(out=decT[:], in_=bank[1][:, 64:192].rearrange("p (j b) -> p j b", j=8), func=AF.Exp),
                     reads=[pb1b], writes=[decT.buf])
                for bi in (6, 7):
                    P.op("pe", lambda e, bi=bi: e.matmul(bank[bi], lhsT=zero_b[:], rhs=cb[:, 0:512], start=True, stop=False),
                         reads=[zero_b.buf, cb.buf], writes=[pb[bi]])

                def load_s0(b):
                    if b < NSEQ:
                        P.dma("sp", S0n[b % 3][:], sst_d[b].rearrange("(j p) n -> p j n", p=128), ch_s0[b % 3], writes=S0n_bufs[b % 3])

                load_s0(0)
                load_s0(1)
                for b in range(NSEQ):
                    sn_ = S0n[b % 3]
                    snb = S0n_bufs[b % 3]
                    cm = CTm[b % 2]
                    if b >= 1:
                        P.dma("sp", nsts_d[b - 1].rearrange("(j p) n -> p j n", p=128), S0n[(b - 1) % 3][:], ch_s0[(b - 1) % 3],
                              reads=S0n_bufs[(b - 1) % 3])
                    load_s0(b + 2)
                    if b >= 2:
                        P.op("pool", lambda e, cm=cm, b=b: e.memset(cm[:, :, (b - 2) * 8:(b - 1) * 8], 0.0), writes=[cm.buf])
                    P.op("pool", lambda e, cm=cm, b=b: e.tensor_copy(out=cm[:, :, b * 8:(b + 1) * 8], in_=CT[:, :, b * 8:(b + 1) * 8]),
                         reads=[CT.buf], writes=[cm.buf])
                    P.op("act", lambda e, sn_=sn_: e.activation(out=S0b[:], in_=sn_[:].rearrange("p j n -> p (j n)"), func=AF.Copy),
                         reads=snb, writes=[S0b.buf])
                    for jp in range(8):
                        P.op("pe", lambda e, jp=jp: e.transpose(bankbf[0][:, jp * 128:(jp + 1) * 128], S0b[:, jp * 128:(jp + 1) * 128], ident_b),
                             reads=[S0b.buf, cb.buf], writes=[pb[0]])
                    P.op("act", lambda e: e.activation(out=S0T[:], in_=bankbf[0], func=AF.Copy), reads=[pb[0]], writes=[S0T.buf])
                    for g in range(4):
                        bi = 6 + g // 2
                        P.op("pe", lambda e, g=g, bi=bi, cm=cm, b=b: e.matmul(
                            bank[bi][:, (g % 2) * 256:(g % 2) * 256 + 256], lhsT=cm[:, g, :], rhs=S0T[:, g * 256:(g + 1) * 256],
                            start=False, stop=(b == NSEQ - 1 and g % 2 == 1)),
                            reads=[cm.buf, S0T.buf], writes=[pb[bi]])
                    P.op("dve", lambda e, b=b: e.tensor_scalar(out=Bm[:], in0=Btok[:], scalar1=cpc("seqsel", b), scalar2=None, op0=ALU.mult),
                         reads=[Btok.buf, cpk.buf, selb.buf], writes=[Bm.buf])
                    for jp in range(8):
                        bi = 2 + jp // 4
                        gq = jp // 2
                        P.op("pe", lambda e, jp=jp, bi=bi, gq=gq: e.matmul(
                            bank[bi][:, (jp % 4) * 128:(jp % 4 + 1) * 128], lhsT=xdtd[:, jp * 128:(jp + 1) * 128],
                            rhs=Bm[:, gq * 128:(gq + 1) * 128], start=True, stop=True),
                            reads=[xdtd.buf, Bm.buf], writes=[pb[bi]])
                    for jp in range(8):
                        bi = 2 + jp // 4
                        P.op("dve", lambda e, jp=jp, bi=bi, sn_=sn_, b=b: e.scalar_tensor_tensor(
                            out=sn_[:, jp, :], in0=sn_[:, jp, :], scalar=decT[:, jp, b:b + 1],
                            in1=bank[bi][:, (jp % 4) * 128:(jp % 4 + 1) * 128], op0=ALU.mult, op1=ALU.add),
                            reads=snb + [decT.buf, pb[bi]], writes=snb)
                P.dma("sp", nsts_d[NSEQ - 1].rearrange("(j p) n -> p j n", p=128), S0n[(NSEQ - 1) % 3][:], ch_s0[(NSEQ - 1) % 3],
                      reads=S0n_bufs[(NSEQ - 1) % 3])
                P.op("dve", lambda e: e.tensor_tensor(out=ycomb[:].rearrange("p (h q) -> p h q", h=16),
                                                      in0=yo.rearrange("p (h q) -> p h q", h=16),
                                                      in1=bc(s16(EAC).unsqueeze(2), [128, 16, 64]), op=ALU.mult),
                     reads=[pb[6], pb[7], sm16.buf], writes=[ycomb.buf])
                P.op("dve", lambda e: e.tensor_tensor(out=ycomb[:], in0=ycomb[:], in1=yd, op=ALU.add),
                     reads=[ycomb.buf, pb[4], pb[5]], writes=[ycomb.buf])
            yield
            for j in range(8):
                P.op("pe", lambda e, j=j: e.transpose(bankbf[6][:, j * 128:(j + 1) * 128], szT[:, j, :], ident_b),
                     reads=[szT.buf, cb.buf], writes=[pb[6]])
            P.op("dve", lambda e: e.tensor_tensor(out=ycomb[:], in0=ycomb[:], in1=bankbf[6], op=ALU.mult),
                 reads=[ycomb.buf, pb[6]], writes=[ycomb.buf])
            for g in range(4):
                P.op("act", lambda e, g=g: e.activation(out=ygn[:, g * 256:(g + 1) * 256], in_=ycomb[:, g * 256:(g + 1) * 256],
                                                        func=AF.Square, accum_out=ss4[:, g:g + 1]),
                     reads=[ycomb.buf], writes=[ygn.buf, ss4.buf])
            rstd_pool(ss4, 0, 4, 256, eps=4.0 * EPS)
            P.op("pool", lambda e: e.tensor_tensor(out=ygn[:].rearrange("p (g q) -> p g q", g=4),
                                                   in0=ycomb[:].rearrange("p (g q) -> p g q", g=4),
                                                   in1=bc(ss4[:, 8:12].unsqueeze(2), [128, 4, 256]), op=ALU.mult),
                 reads=[ycomb.buf, ss4.buf], writes=[ygn.buf])
            yield
            if not sample:
                st_ = pair(4)
                for g in range(4):
                    bi = 4 + g // 2
                    P.op("pe", lambda e, g=g, bi=bi: e.matmul(bank[bi][:, (g % 2) * 256:(g % 2) * 256 + 256],
                                                              lhsT=Btok[:, g * 128:(g + 1) * 128], rhs=xdtd[:, g * 256:(g + 1) * 256],
                                                              start=True, stop=True),
                         reads=[Btok.buf, xdtd.buf], writes=[pb[bi]])
                if first:
                    P.op("dve", lambda e: e.tensor_copy(out=ST[:], in_=st_), reads=[pb[4], pb[5]], writes=[ST.buf])
                else:
                    P.op("pool", lambda e: e.tensor_tensor(out=ST[:].rearrange("p (h q) -> p h q", h=16),
                                                           in0=ST[:].rearrange("p (h q) -> p h q", h=16),
                                                           in1=bc(s16(DEC).unsqueeze(2), [128, 16, 64]), op=ALU.mult),
                         reads=[ST.buf, sm16.buf], writes=[ST.buf])
                    P.op("dve", lambda e: e.tensor_tensor(out=ST[:], in0=ST[:], in1=st_, op=ALU.add),
                         reads=[ST.buf, pb[4], pb[5]], writes=[ST.buf])
                if cidx < NCH - 1:
                    P.op("act", lambda e: e.activation(out=STb[:], in_=ST[:], func=AF.Copy), reads=[ST.buf], writes=[STb.buf])
            for j in range(8):
                P.op("pe", lambda e, j=j: e.transpose(bankbf[7][:, j * 128:(j + 1) * 128], ygn[:, j * 128:(j + 1) * 128], ident_b),
                     reads=[ygn.buf, cb.buf], writes=[pb[7]])
            for j in range(8):
                P.op("act", lambda e, j=j: e.activation(out=yssdT[:, j, :], in_=bankbf[7][:, j * 128:(j + 1) * 128],
                                                        func=AF.Identity, scale=cpc("ng", j)),
                     reads=[pb[7], cpk.buf], writes=[yssdT.buf])
            yield
            for half in range(2):
                bi = 6 + half
                for k in range(16):
                    if k < 8:
                        lh = y_scT[:, k, tok0:tok0 + 128]
                        rb = [y_scT.buf]
                    else:
                        lh = yssdT[:, k - 8, :]
                        rb = [yssdT.buf]
                    P.op("pe", lambda e, lh=lh, k=k, half=half, bi=bi: e.matmul(
                        bank[bi], lhsT=lh, rhs=Wout[:, k, half * 512:(half + 1) * 512], start=(k == 0), stop=(k == 15)),
                        reads=rb + [wout_bufs[k // 4]], writes=[pb[bi]])
                yield
            mix = pair(6)
            P.op("act", lambda e: e.activation(out=ygn[:], in_=mix, func=AF.Square, accum_out=sttb[:, 4:5]),
                 reads=[pb[6], pb[7]], writes=[ygn.buf, sttb.buf])
            rstd_pool(sttb, 4, 1, D)
            gm = gmS if sample else gmP
            P.op("dve", lambda e, gm=gm: e.tensor_tensor(out=ycomb[:], in0=mix, in1=gm[:], op=ALU.mult),
                 reads=[pb[6], pb[7], gm.buf], writes=[ycomb.buf])
            P.op("dve", lambda e, xs_=xs_: e.scalar_tensor_tensor(out=xs_[:], in0=ycomb[:], scalar=sttb[:, 6:7], in1=xs_[:],
                                                                  op0=ALU.mult, op1=ALU.add),
                 reads=[ycomb.buf, sttb.buf, xs_.buf], writes=[xs_.buf])
            P.dma("sp", x1s_d[tok0:tok0 + 128, :], xs_[:], ch_xinb[ci % NXS], reads=[xs_.buf])
            load_x1b(ci + NXS)
            if sample:
                P.dma("sp", nxcs_d[:, :], nxcs[:].rearrange("p j b r -> p (j b r)"), ch_nxcs, reads=[nxcs.buf])
            if (not sample) and cidx == NCH - 1:
                P.dma("sp", nxcp_d[:, :], xhist[:].rearrange("p j r -> p (j r)"), ch_nxcp, reads=[xhist.buf])
                for jp in range(8):
                    bi = 4 + jp // 4
                    P.op("pe", lambda e, jp=jp, bi=bi: e.transpose(bank[bi][:, (jp % 4) * 128:(jp % 4 + 1) * 128],
                                                                   ST[:, jp * 128:(jp + 1) * 128], ident_f),
                         reads=[ST.buf, cpk.buf], writes=[pb[bi]])
                P.op("dve", lambda e: e.tensor_copy(out=ycomb[:], in_=pair(4)), reads=[pb[4], pb[5]], writes=[ycomb.buf])
                P.dma("sp", nstp_d[:, :].rearrange("(j p) n -> p j n", p=128), ycomb[:].rearrange("p (j n) -> p j n", j=8),
                      ch_nstp, reads=[ycomb.buf])
            yield

        def drain(g, cpbase=None):
            for i, _ in enumerate(g):
                if cpbase is not None:
                    cp(cpbase + i)

        def interleave(ga, gb, pattern="BFBBFBBFBBFBFBFBBFBF"):
            gens = {"B": ga, "F": gb}
            for ch in pattern:
                g = gens.get(ch)
                if g is None:
                    continue
                try:
                    next(g)
                except StopIteration:
                    gens[ch] = None
            for g in gens.values():
                if g is not None:
                    for _ in g:
                        pass

        for n in range(NXS):
            load_x1b(n)
        drain(front(0))
        cp(50)
        drain(back(0))
        cp(51)
        drain(front(1))
        cp(52)
        import os
        noil = os.environ.get("KNOIL", "0")
        for ci in range(1, len(chunk_list)):
            if noil == "1":
                drain(back(ci), 60 if ci == 1 else None)
                if ci == 1:
                    cp(55)
                if ci + 1 < len(chunk_list):
                    drain(front(ci + 1))
            else:
                interleave(back(ci), front(ci + 1) if ci + 1 < len(chunk_list) else None)
            if ci == 1:
                cp(53)
            if ci == 2:
                cp(54)

        P.barrier()
        A.off = mark_p2
        if stop == 2:
            P.emit(nc, final_chans=out_chans)
            return nc, P

        Wg = A.alloc("Wg", [128, 8, HID], BF16)
        Wu = A.alloc("Wu", [128, 8, HID], BF16)
        Wd = A.alloc("Wd", [128, NHT, 1024], BF16)
        ch_wg = [P.chan("wg%d" % i) for i in range(4)]
        ch_wu = [P.chan("wu%d" % i) for i in range(4)]
        ch_wd = [P.chan("wd%d" % i) for i in range(4)]
        wg_b = [Buf("wg%d" % i) for i in range(4)]
        wu_b = [Buf("wu%d" % i) for i in range(4)]
        wd_b = [Buf("wd%d" % i) for i in range(4)]
        wd_split = [(0, 6), (6, 12), (12, 17), (17, 22)]
        for i in range(4):
            P.dma("pool", Wg[:, :, i * 704:(i + 1) * 704], wg_d[i].rearrange("p (k c) -> p k c", k=8), ch_wg[i], writes=[wg_b[i]])
            P.dma("pool", Wu[:, :, i * 704:(i + 1) * 704], wu_d[i].rearrange("p (k c) -> p k c", k=8), ch_wu[i], writes=[wu_b[i]])
        for i, (a0, a1) in enumerate(wd_split):
            P.dma("pool", Wd[:, a0:a1, :], wd_d[a0 * 128:a1 * 128, :].rearrange("(i p) n -> p i n", p=128), ch_wd[i], writes=[wd_b[i]])

        def wd_buf(i):
            for n, (a0, a1) in enumerate(wd_split):
                if a0 <= i < a1:
                    return wd_b[n]

        gfP = A.alloc("gfP", [128, 1024])
        gfS = A.alloc("gfS", [128, 1024])
        ch_gf = [P.chan("gfload%d" % i) for i in range(2)]
        P.dma("sp", gfP[:], gsave_d[0], ch_gf[0], writes=[gfP.buf])
        P.dma("sp", gfS[:], gsave_d[1], ch_gf[1], writes=[gfS.buf])
        xf = [[A.alloc("xf%d_%d" % (s, t), [128, 1024]) for t in range(2)] for s in range(2)]
        ch_xf = [[P.chan("xf%d_%d" % (s, t)) for t in range(2)] for s in range(2)]
        out_chans += [c for r in ch_xf for c in r]
        xn2 = A.alloc("xn2", [128, 1024], BF16)
        h2T = A.alloc("h2T", [128, 8, 256], BF16)
        aT = A.alloc("aT", [128, NHT, 256], BF16)
        sg = [A.alloc("sg%d" % i, [128, 256]) for i in range(2)]
        ftmp = A.alloc("ftmp", [128, 1024])

        tiles = [(t * 256, 256, False) for t in range(8)] + [(LP, 128, True)]

        def load_x2(n):
            if n < len(tiles):
                t0, ts, _ = tiles[n]
                for tt in range(ts // 128):
                    P.dma("sp", xf[n % 2][tt][:], x1s_d[t0 + tt * 128:t0 + (tt + 1) * 128, :], ch_xf[n % 2][tt], writes=[xf[n % 2][tt].buf])

        h2Ts = [h2T, A.alloc("h2T1", [128, 8, 256], BF16)]

        xn2s = [xn2, A.alloc("xn2b", [128, 1024], BF16)]
        stq2 = [A.alloc("stq2_%d" % i, [128, 4]) for i in range(2)]

        def stage_A2_pre(ti):
            t0_, ts_, smp_ = tiles[ti]
            for tt in range(ts_ // 128):
                stage_A_pre(xf[ti % 2][tt], xn2s[tt], stq2[tt])

        def stage_A2_post(ti):
            t0_, ts_, smp_ = tiles[ti]
            for tt in range(ts_ // 128):
                stage_A_post(xn2s[tt], A_f, Bv_f, smp_, h2Ts[ti % 2][:, :, tt * 128:(tt + 1) * 128], h2Ts[ti % 2].buf, ftmp)

        def stage_A2(ti):
            stage_A2_pre(ti)
            stage_A2_post(ti)

        load_x2(0)
        stage_A2(0)
        for ti, (tok0, TS, sample) in enumerate(tiles):
            nt = TS // 128
            sl = ti % 2
            h2c = h2Ts[ti % 2]
            load_x2(ti + 1)
            for i in range(NHT):
                bi = 1 + i % 3
                wq = i * 128 // 704
                wq2 = (i * 128 + 127) // 704
                for k in range(8):
                    P.op("pe", lambda e, i=i, k=k, bi=bi, TS=TS, h2c=h2c: e.matmul(bank[bi][:, 0:TS], lhsT=Wg[:, k, i * 128:(i + 1) * 128],
                                                                                   rhs=h2c[:, k, 0:TS], start=(k == 0), stop=(k == 7)),
                         reads=[wg_b[wq], wg_b[wq2], h2c.buf], writes=[pb[bi]])
                for k in range(8):
                    P.op("pe", lambda e, i=i, k=k, bi=bi, TS=TS, h2c=h2c: e.matmul(bank[bi][:, 256:256 + TS], lhsT=Wu[:, k, i * 128:(i + 1) * 128],
                                                                                   rhs=h2c[:, k, 0:TS], start=(k == 0), stop=(k == 7)),
                         reads=[wu_b[wq], wu_b[wq2], h2c.buf], writes=[pb[bi]])
                s_ = sg[i % 2]
                P.op("act", lambda e, s_=s_, bi=bi, TS=TS: e.activation(out=s_[:, 0:TS], in_=bank[bi][:, 0:TS], func=AF.Silu),
                     reads=[pb[bi]], writes=[s_.buf])
                P.op("dve", lambda e, s_=s_, bi=bi, i=i, TS=TS: e.tensor_tensor(out=aT[:, i, 0:TS], in0=s_[:, 0:TS],
                                                                                 in1=bank[bi][:, 256:256 + TS], op=ALU.mult),
                     reads=[s_.buf, pb[bi]], writes=[aT.buf])
                if i == 8 and ti + 1 < len(tiles):
                    stage_A2_pre(ti + 1)
            for tt in range(nt):
                for half in range(2):
                    bi = 4 + tt * 2 + half
                    for i in range(NHT):
                        P.op("pe", lambda e, i=i, tt=tt, half=half, bi=bi: e.matmul(
                            bank[bi], lhsT=aT[:, i, tt * 128:(tt + 1) * 128], rhs=Wd[:, i, half * 512:(half + 1) * 512],
                            start=(i == 0), stop=(i == NHT - 1)),
                            reads=[aT.buf, wd_buf(i)], writes=[pb[bi]])
            if ti + 1 < len(tiles):
                stage_A2_post(ti + 1)
            for tt in range(nt):
                fp_ = pair(4 + tt * 2)
                pbs = [pb[4 + tt * 2], pb[5 + tt * 2]]
                xt = xf[sl][tt]
                P.op("act", lambda e, fp_=fp_: e.activation(out=xn2[:], in_=fp_, func=AF.Square, accum_out=stt[:, 4:5]),
                     reads=pbs, writes=[xn2.buf, stt.buf])
                rstd_pool(stt, 4, 1, D)
                gf = gfS if sample else gfP
                P.op("dve", lambda e, fp_=fp_, gf=gf: e.tensor_tensor(out=ftmp[:], in0=fp_, in1=gf[:], op=ALU.mult),
                     reads=pbs + [gf.buf], writes=[ftmp.buf])
                P.op("dve", lambda e, xt=xt: e.scalar_tensor_tensor(out=xt[:], in0=ftmp[:], scalar=stt[:, 6:7], in1=xt[:],
                                                                    op0=ALU.mult, op1=ALU.add),
                     reads=[ftmp.buf, stt.buf, xt.buf], writes=[xt.buf])
                P.dma("sp", y_d[tok0 + tt * 128:tok0 + (tt + 1) * 128, :], xt[:], ch_xf[sl][tt], reads=[xt.buf])

        P.emit(nc, final_chans=out_chans)
    return nc, P


def _fm(v, ntile):
    return np.ascontiguousarray(np.asarray(v, np.float32).reshape(ntile, 128).T)


def _host_consts():
    idx = np.arange(128)
    ident = np.eye(128, dtype=np.float32)
    tri = (idx[:, None] <= idx[None, :]).astype(np.float32)
    negm = np.where(idx[None, :] < idx[:, None], NEG, 0.0).astype(np.float32)
    same = (idx[:, None] // LS == idx[None, :] // LS).astype(np.float32)
    triBD = tri * same
    negmBD = np.where(triBD > 0, 0.0, NEG).astype(np.float32)
    seqsel = (idx[:, None] // LS == np.arange(NSEQ)[None, :]).astype(np.float32)
    return ident, tri, negm, triBD, negmBD, same, seqsel


_CACHE = {}


def kernel(x_prompt, x_sample, c_prompt, c_sample, state_sc_conv, state_ssm_conv, state_ssm,
           w_ada, b_ada, g_mix_pre, g_mix_post, g_ffn_pre, g_ffn_post, w_in, sc_conv_w,
           ssm_conv_w, ssm_conv_b, dt_bias, a_log, d_skip, ssm_norm_g, w_out, w_gate, w_up, w_down):
    f32 = np.float32
    x_prompt = np.asarray(x_prompt, f32)
    x_sample = np.asarray(x_sample, f32)
    w_in_ = np.asarray(w_in, f32)[0]
    w_ada_ = np.asarray(w_ada, f32)[0]
    wada = np.ascontiguousarray(w_ada_.reshape(8, 128, 6, 1024).transpose(2, 1, 0, 3).reshape(6, 128, 8192))
    sc = w_in_[:, 0:3072].reshape(8, 128, 3, 8, 128)
    wsc = np.ascontiguousarray(sc[:, :, [1, 2, 0], :, :].transpose(3, 1, 0, 2, 4).reshape(8, 128, 8 * 384))
    ssm = w_in_[:, 3072:6144].reshape(8, 128, 6, 512)
    wssm = np.ascontiguousarray(ssm.transpose(2, 1, 0, 3).reshape(6, 128, 8 * 512))
    wdt = np.ascontiguousarray(w_in_[:, 6144:6160].reshape(8, 128, 16).transpose(1, 0, 2).reshape(128, 128))
    wout = np.ascontiguousarray(np.asarray(w_out, f32)[0])
    wg = np.ascontiguousarray(np.asarray(w_gate, f32)[0].reshape(8, 128, 4, 704).transpose(2, 1, 0, 3).reshape(4, 128, 8 * 704))
    wu = np.ascontiguousarray(np.asarray(w_up, f32)[0].reshape(8, 128, 4, 704).transpose(2, 1, 0, 3).reshape(4, 128, 8 * 704))
    wd = np.ascontiguousarray(np.asarray(w_down, f32)[0])
    badar = np.ascontiguousarray(np.asarray(b_ada, f32).reshape(1, 6144))
    gpostr = np.ascontiguousarray(np.stack([np.asarray(g_mix_post, f32)[0], np.asarray(g_ffn_post, f32)[0]]))

    ident, tri, negm, triBD, negmBD, same, seqsel = _host_consts()
    scw = np.asarray(sc_conv_w, f32)[0]
    xcw = np.asarray(ssm_conv_w, f32)[0]
    base = np.zeros((128, CPW), f32)

    def put(name, arr):
        arr = np.asarray(arr, f32)
        base[:, _CP[name]:_CP[name] + arr.shape[1]] = arr

    put("ident", ident); put("tri", tri); put("negm", negm); put("triBD", triBD); put("negmBD", negmBD)
    put("same", same); put("seqsel", seqsel)
    put("gpre", _fm(np.asarray(g_mix_pre, f32)[0], 8))
    put("gfpre", _fm(np.asarray(g_ffn_pre, f32)[0], 8))
    put("scw", scw.reshape(3, 8, 128).transpose(2, 1, 0).reshape(128, 24))
    put("xcw", xcw.reshape(4, 16, 128).transpose(2, 1, 0).reshape(128, 64))
    put("xcb", _fm(np.asarray(ssm_conv_b, f32)[0], 16))
    put("dtb", np.broadcast_to(np.asarray(dt_bias, f32).reshape(1, 16), (128, 16)))
    put("alog", np.broadcast_to(np.asarray(a_log, f32).reshape(1, 16), (128, 16)))
    put("dcol", _fm(np.repeat(np.asarray(d_skip, f32)[0], 64), 8))
    put("ng", _fm(np.asarray(ssm_norm_g, f32)[0], 8))
    put("bada", _fm(np.asarray(b_ada, f32)[0], 48))
    put("eps", np.full((128, 1), EPS, f32))

    in_maps = []
    for i in range(NCORES):
        cpk = base.copy()
        call = np.concatenate([np.asarray(c_prompt, f32)[i:i + 1], np.asarray(c_sample, f32)[16 * i:16 * i + 16]], axis=0)
        cpk[:, _CP["cT"]:_CP["cT"] + 136] = call.reshape(17, 8, 128).transpose(2, 1, 0).reshape(128, 136)
        xall = np.concatenate([x_prompt[i], x_sample[16 * i:16 * i + 16].reshape(128, D)], axis=0)
        scs = np.asarray(state_sc_conv, f32)[0, 16 * i:16 * i + 16]
        scst = scs.reshape(16, 2, 8, 128).transpose(3, 2, 0, 1).reshape(128, 256)
        xcs = np.asarray(state_ssm_conv, f32)[0, 16 * i:16 * i + 16]
        xcst = xcs.reshape(16, 3, 16, 128).transpose(3, 2, 0, 1).reshape(128, 768)
        sst = np.asarray(state_ssm, f32)[0, 16 * i:16 * i + 16].reshape(16, 1024, 128)
        in_maps.append({
            "xall": np.ascontiguousarray(xall), "cpk": cpk, "scst": np.ascontiguousarray(scst),
            "xcst": np.ascontiguousarray(xcst), "sst": np.ascontiguousarray(sst),
            "wada": wada, "badar": badar, "gpostr": gpostr, "wsc": wsc, "wssm": wssm, "wdt": wdt,
            "wout": wout, "wg": wg, "wu": wu, "wd": wd,
        })

    if "nc" not in _CACHE:
        import os
        _CACHE["nc"] = build_program(int(os.environ.get("KSTOP", "99")))
    nc, _ = _CACHE["nc"]
    import os
    ncr = int(os.environ.get("KCORES", str(NCORES)))
    res = run_bass_kernel_spmd(nc, in_maps[:ncr], core_ids=list(range(ncr)))
    R = list(res.results)
    while len(R) < NCORES:
        R.append(R[0])

    y_prompt = np.stack([R[i]["y"][0:LP] for i in range(NCORES)]).astype(f32)
    y_sample = np.concatenate([R[i]["y"][LP:].reshape(16, LS, D) for i in range(NCORES)], axis=0).astype(f32)
    nscp = np.stack([R[i]["nscp"].reshape(128, 8, 2).transpose(2, 1, 0).reshape(2, 1024) for i in range(NCORES)])[None]
    nxcp = np.stack([R[i]["nxcp"].reshape(128, 16, 3).transpose(2, 1, 0).reshape(3, 2048) for i in range(NCORES)])[None]
    nstp = np.stack([R[i]["nstp"].reshape(16, 64, 128) for i in range(NCORES)])[None]
    nscs = np.concatenate([R[i]["nscs"].reshape(128, 8, 16, 2).transpose(2, 3, 1, 0).reshape(16, 2, 1024) for i in range(NCORES)])[None]
    nxcs = np.concatenate([R[i]["nxcs"].reshape(128, 16, 16, 3).transpose(2, 3, 1, 0).reshape(16, 3, 2048) for i in range(NCORES)])[None]
    nsts = np.concatenate([R[i]["nsts"].reshape(16, 16, 64, 128) for i in range(NCORES)])[None]
    return (y_prompt, y_sample, np.ascontiguousarray(nscp, f32), np.ascontiguousarray(nxcp, f32),
            np.ascontiguousarray(nstp, f32), np.ascontiguousarray(nscs, f32), np.ascontiguousarray(nxcs, f32),
            np.ascontiguousarray(nsts, f32))
```
